# Optimizing a Trainium2 kernel written in Bass

```python
import math
import jax
import jax.numpy as jnp
from jax import lax
import numpy as np

D_MODEL = 1024
BATCH = 4
SEQ = 4096
DEPTH = 2

RMS_EPS = 1e-6
ROPE_THETA = 500000.0

SSD_HEAD_DIM = 64
SSD_D_INNER = D_MODEL
SSD_HEADS = SSD_D_INNER // SSD_HEAD_DIM
SSD_GROUPS = 4
SSD_D_STATE = 128
SSD_CONV = 4
SSD_CHUNK = 256
SSD_CONV_DIM = SSD_D_INNER + 2 * SSD_GROUPS * SSD_D_STATE

DIL_PAIRS = ((128, 1), (512, 4), (2048, 16))
DIL_HEAD_DIM = 64
DIL_HEADS_PER_GROUP = 8
DIL_HEADS = DIL_HEADS_PER_GROUP * len(DIL_PAIRS)
DIL_QKV = DIL_HEADS * DIL_HEAD_DIM
DIL_OUT = DIL_HEADS_PER_GROUP * DIL_HEAD_DIM

MOBA_HEAD_DIM = 64
MOBA_WIDTH = D_MODEL
MOBA_HEADS = MOBA_WIDTH // MOBA_HEAD_DIM
MOBA_BLOCK = 256
MOBA_TOPK = 3
MOBA_Q_CHUNK = 16

EVEN_IN_SIZES = (SSD_D_INNER, SSD_CONV_DIM, SSD_HEADS, DIL_QKV, DIL_QKV, DIL_QKV, DIL_OUT)
EVEN_IN = sum(EVEN_IN_SIZES)
EVEN_MIX = SSD_D_INNER + DIL_OUT
ODD_IN = 4 * MOBA_WIDTH
N_EVEN = (DEPTH + 1) // 2
N_ODD = DEPTH // 2

kernel_name = "hybrid_ssd_dilated_moba_trunk"


def rms_norm(x, w):
    xf = x.astype(jnp.float32)
    y = xf * lax.rsqrt(jnp.mean(xf * xf, axis=-1, keepdims=True) + RMS_EPS)
    return (y * w.astype(jnp.float32)).astype(x.dtype)


def partial_rope(x, pos):
    e = x.shape[-1]
    rd = e // 4
    half = rd // 2
    inv = ROPE_THETA ** (-jnp.arange(half, dtype=jnp.float32) * 2.0 / rd)
    ang = pos.astype(jnp.float32)[:, None] * inv[None, :]
    cos = jnp.cos(ang)[:, None, :]
    sin = jnp.sin(ang)[:, None, :]
    xr = x[..., :rd].astype(jnp.float32)
    x1, x2 = xr[..., :half], xr[..., half:]
    rot = jnp.concatenate([x1 * cos - x2 * sin, x2 * cos + x1 * sin], axis=-1)
    return jnp.concatenate([rot.astype(x.dtype), x[..., rd:]], axis=-1)


def split_cols(t, sizes):
    outs, o = [], 0
    for s in sizes:
        outs.append(t[..., o:o + s])
        o += s
    return outs


def causal_dwconv(x, w, b):
    K = w.shape[1]
    rhs = jnp.transpose(w)[:, None, :].astype(x.dtype)
    y = lax.conv_general_dilated(x, rhs, window_strides=(1,), padding=((K - 1, 0),),
                                 dimension_numbers=("NWC", "WIO", "NWC"),
                                 feature_group_count=x.shape[-1])
    return y + b.astype(x.dtype)


def segsum(a):
    T = a.shape[-1]
    cs = jnp.cumsum(a, axis=-1)
    diff = cs[..., :, None] - cs[..., None, :]
    mask = jnp.tril(jnp.ones((T, T), dtype=bool))
    return jnp.where(mask, diff, -jnp.inf)


def ssd_chunked_scan(x, dt, A, Bm, Cm):
    Bsz, S, H, P = x.shape
    G, N = Bm.shape[2], Bm.shape[3]
    J = H // G
    L = SSD_CHUNK
    pad = (-S) % L
    xdt = x.astype(jnp.float32) * dt[..., None]
    a = dt * A
    Bf = Bm.astype(jnp.float32)
    Cf = Cm.astype(jnp.float32)
    if pad:
        padseq = lambda t: jnp.pad(t, [(0, 0), (0, pad)] + [(0, 0)] * (t.ndim - 2))
        xdt, a, Bf, Cf = padseq(xdt), padseq(a), padseq(Bf), padseq(Cf)
    nc = (S + pad) // L
    xdt = xdt.reshape(Bsz, nc, L, G, J, P)
    Bc = Bf.reshape(Bsz, nc, L, G, N)
    Cc = Cf.reshape(Bsz, nc, L, G, N)
    a = a.reshape(Bsz, nc, L, G, J).transpose(0, 3, 4, 1, 2)
    a_cs = jnp.cumsum(a, axis=-1)
    cb = jnp.einsum("bclgn,bcsgn->bgcls", Cc, Bc)
    m = cb[:, :, None] * jnp.exp(segsum(a))
    y_diag = jnp.einsum("bgjcls,bcsgjp->bclgjp", m, xdt)
    w_end = jnp.exp(a_cs[..., -1:] - a_cs).transpose(0, 3, 4, 1, 2)
    states = jnp.einsum("bclgn,bclgjp->bcgjpn", Bc, xdt * w_end[..., None])
    a_last = jnp.pad(a_cs[..., -1], [(0, 0)] * 3 + [(1, 0)])
    decay_chunk = jnp.exp(segsum(a_last))
    states0 = jnp.concatenate([jnp.zeros_like(states[:, :1]), states], axis=1)
    entering = jnp.einsum("bgjzc,bcgjpn->bzgjpn", decay_chunk, states0)[:, :-1]
    w_start = jnp.exp(a_cs).transpose(0, 3, 4, 1, 2)
    y_off = jnp.einsum("bclgn,bcgjpn->bclgjp", Cc, entering) * w_start[..., None]
    return (y_diag + y_off).reshape(Bsz, nc * L, H, P)[:, :S]


def dilated_group(q, k, v, window, dil):
    Bsz, S, h, e = q.shape
    band = window // dil
    Lsub = S // dil
    nb = -(-Lsub // band)
    Lp = nb * band

    def to_sub(t):
        t = t.astype(jnp.float32).reshape(Bsz, Lsub, dil, h, e).transpose(0, 3, 2, 1, 4)
        t = jnp.pad(t, ((0, 0), (0, 0), (0, 0), (0, Lp - Lsub), (0, 0)))
        return t.reshape(Bsz, h, dil, nb, band, e)

    def with_prev(t):
        prev = jnp.pad(t, ((0, 0), (0, 0), (0, 0), (1, 0), (0, 0), (0, 0)))[:, :, :, :-1]
        return jnp.concatenate([prev, t], axis=4)

    qs = to_sub(q)
    kb = with_prev(to_sub(k))
    vb = with_prev(to_sub(v))
    s = jnp.einsum("bhrnqe,bhrnke->bhrnqk", qs, kb) * (e ** -0.5)
    i = jnp.arange(band)[:, None]
    j = jnp.arange(2 * band)[None, :]
    dist = i + band - j
    n = jnp.arange(nb)[:, None, None]
    valid = (dist >= 0) & (dist <= band) & ((n * band + j[None] - band) >= 0)
    s = jnp.where(valid, s, -jnp.inf)
    mx = jnp.max(s, axis=-1, keepdims=True)
    p = jnp.exp(s - mx)
    den = jnp.sum(p, axis=-1)
    o = jnp.einsum("bhrnqk,bhrnke->bhrnqe", p, vb) / den[..., None]
    lse = mx[..., 0] + jnp.log(den)
    o = o.reshape(Bsz, h, dil, Lp, e)[:, :, :, :Lsub].transpose(0, 3, 2, 1, 4).reshape(Bsz, S, h, e)
    lse = lse.reshape(Bsz, h, dil, Lp)[:, :, :, :Lsub].transpose(0, 3, 2, 1).reshape(Bsz, S, h)
    return o, lse


def moba_attention(q, k, v):
    Bsz, S, H, e = q.shape
    blk = MOBA_BLOCK
    nblk = -(-S // blk)
    Sp = nblk * blk
    scale = e ** -0.5

    def heads_first(t):
        t = t.astype(jnp.float32).transpose(0, 2, 1, 3)
        return jnp.pad(t, ((0, 0), (0, 0), (0, Sp - S), (0, 0)))

    qh = heads_first(q)
    kb = heads_first(k).reshape(Bsz, H, nblk, blk, e)
    vb = heads_first(v).reshape(Bsz, H, nblk, blk, e)
    kmean = jnp.mean(kb, axis=3)
    gate = jnp.einsum("bhte,bhne->bhtn", qh, kmean)
    tblk = jnp.arange(Sp) // blk
    past = jnp.arange(nblk)[None, :] < tblk[:, None]
    gate = jnp.where(past, gate, -jnp.inf)
    ksel = min(MOBA_TOPK, nblk)
    _, sel = lax.top_k(gate, ksel)
    sel_valid = sel < tblk[:, None]

    Qc = MOBA_Q_CHUNK
    nq = Sp // Qc
    qc = qh.reshape(Bsz, H, nq, Qc, e).transpose(2, 0, 1, 3, 4)
    selc = sel.reshape(Bsz, H, nq, Qc, ksel).transpose(2, 0, 1, 3, 4)
    valc = sel_valid.reshape(Bsz, H, nq, Qc, ksel).transpose(2, 0, 1, 3, 4)
    bi = jnp.arange(Bsz)[:, None, None, None]
    hi = jnp.arange(H)[None, :, None, None]

    def one_chunk(args):
        c, q_c, sel_c, val_c = args
        own = (c * Qc) // blk
        k_sel = kb[bi, hi, sel_c]
        v_sel = vb[bi, hi, sel_c]
        s_sel = jnp.einsum("bhqe,bhqjke->bhqjk", q_c, k_sel) * scale
        s_sel = jnp.where(val_c[..., None], s_sel, -jnp.inf).reshape(Bsz, H, Qc, ksel * blk)
        k_own = lax.dynamic_index_in_dim(kb, own, axis=2, keepdims=False)
        v_own = lax.dynamic_index_in_dim(vb, own, axis=2, keepdims=False)
        s_own = jnp.einsum("bhqe,bhke->bhqk", q_c, k_own) * scale
        qpos = c * Qc + jnp.arange(Qc)
        kpos = own * blk + jnp.arange(blk)
        s_own = jnp.where(kpos[None, :] <= qpos[:, None], s_own, -jnp.inf)
        p = jax.nn.softmax(jnp.concatenate([s_sel, s_own], axis=-1), axis=-1)
        p_sel = p[..., :ksel * blk].reshape(Bsz, H, Qc, ksel, blk)
        return (jnp.einsum("bhqjk,bhqjke->bhqe", p_sel, v_sel)
                + jnp.einsum("bhqk,bhke->bhqe", p[..., ksel * blk:], v_own))

    out = lax.map(one_chunk, (jnp.arange(nq), qc, selc, valc))
    return out.transpose(1, 0, 3, 2, 4).reshape(Bsz, Sp, H, e)[:, :S]


def ssd_dilated_layer(x, norm_w, w_in, conv_w, conv_b, dt_bias, a_log, d_skip, ssd_norm_w,
                      q_norm, k_norm, w_out, pos):
    Bsz, S, _ = x.shape
    h = rms_norm(x, norm_w)
    proj = jnp.einsum("bsd,df->bsf", h, w_in.astype(h.dtype))
    z, xbc, dt, q, k, v, g = split_cols(proj, EVEN_IN_SIZES)
    xbc = jax.nn.silu(causal_dwconv(xbc, conv_w, conv_b))
    xs, bm, cm = split_cols(xbc, (SSD_D_INNER, SSD_GROUPS * SSD_D_STATE, SSD_GROUPS * SSD_D_STATE))
    xs = xs.reshape(Bsz, S, SSD_HEADS, SSD_HEAD_DIM)
    dt = jax.nn.softplus(dt.astype(jnp.float32) + dt_bias.astype(jnp.float32))
    A = -jnp.exp(a_log.astype(jnp.float32))
    y = ssd_chunked_scan(xs, dt, A, bm.reshape(Bsz, S, SSD_GROUPS, SSD_D_STATE),
                         cm.reshape(Bsz, S, SSD_GROUPS, SSD_D_STATE))
    y = y + d_skip.astype(jnp.float32)[:, None] * xs.astype(jnp.float32)
    y = y.reshape(Bsz, S, SSD_D_INNER) * jax.nn.silu(z.astype(jnp.float32))
    y = rms_norm(y.reshape(Bsz, S, SSD_GROUPS, -1),
                 ssd_norm_w.reshape(SSD_GROUPS, -1)).reshape(Bsz, S, SSD_D_INNER)
    q = partial_rope(rms_norm(q.reshape(Bsz, S, DIL_HEADS, DIL_HEAD_DIM), q_norm), pos)
    k = partial_rope(rms_norm(k.reshape(Bsz, S, DIL_HEADS, DIL_HEAD_DIM), k_norm), pos)
    v = v.reshape(Bsz, S, DIL_HEADS, DIL_HEAD_DIM)
    outs, lses = [], []
    for gi, (window, dil) in enumerate(DIL_PAIRS):
        hs = slice(gi * DIL_HEADS_PER_GROUP, (gi + 1) * DIL_HEADS_PER_GROUP)
        o_g, lse_g = dilated_group(q[:, :, hs], k[:, :, hs], v[:, :, hs], window, dil)
        outs.append(o_g)
        lses.append(lse_g)
    wts = jax.nn.softmax(jnp.stack(lses, axis=0), axis=0)
    o = jnp.sum(wts[..., None] * jnp.stack(outs, axis=0), axis=0)
    o = o.reshape(Bsz, S, DIL_OUT) * jax.nn.silu(g.astype(jnp.float32))
    mixed = jnp.concatenate([y.astype(jnp.float32), o], axis=-1).astype(x.dtype)
    return x + jnp.einsum("bsf,fd->bsd", mixed, w_out.astype(x.dtype))


def moba_layer(x, norm_w, w_in, q_norm, k_norm, w_out, pos):
    Bsz, S, _ = x.shape
    h = rms_norm(x, norm_w)
    proj = jnp.einsum("bsd,df->bsf", h, w_in.astype(h.dtype))
    q, k, v, g = split_cols(proj, (MOBA_WIDTH,) * 4)
    q = partial_rope(rms_norm(q.reshape(Bsz, S, MOBA_HEADS, MOBA_HEAD_DIM), q_norm), pos)
    k = partial_rope(rms_norm(k.reshape(Bsz, S, MOBA_HEADS, MOBA_HEAD_DIM), k_norm), pos)
    v = v.reshape(Bsz, S, MOBA_HEADS, MOBA_HEAD_DIM)
    o = moba_attention(q, k, v).reshape(Bsz, S, MOBA_WIDTH) * jax.nn.silu(g.astype(jnp.float32))
    return x + jnp.einsum("bsf,fd->bsd", o.astype(x.dtype), w_out.astype(x.dtype))


def setup_inputs(seed: int = 0) -> dict:
    key = jax.random.key(seed)
    ks = jax.random.split(key, 20)
    f32 = jnp.float32
    nrm = lambda k, shape, s: jax.random.normal(k, shape, f32) * s
    x = jax.random.normal(ks[0], (BATCH, SEQ, D_MODEL), f32)
    even_norm_w = 1.0 + nrm(ks[1], (N_EVEN, D_MODEL), 0.05)
    even_w_in = nrm(ks[2], (N_EVEN, D_MODEL, EVEN_IN), D_MODEL ** -0.5)
    even_conv_w = nrm(ks[3], (N_EVEN, SSD_CONV_DIM, SSD_CONV), SSD_CONV ** -0.5)
    even_conv_b = nrm(ks[4], (N_EVEN, SSD_CONV_DIM), 0.01)
    dt0 = jnp.exp(jax.random.uniform(ks[5], (N_EVEN, SSD_HEADS), f32,
                                     math.log(1e-3), math.log(1e-1)))
    even_dt_bias = dt0 + jnp.log(-jnp.expm1(-dt0))
    even_a_log = jnp.log(jax.random.uniform(ks[6], (N_EVEN, SSD_HEADS), f32, 1.0, 16.0))
    even_d_skip = 1.0 + nrm(ks[7], (N_EVEN, SSD_HEADS), 0.1)
    even_ssd_norm_w = 1.0 + nrm(ks[8], (N_EVEN, SSD_D_INNER), 0.05)
    even_q_norm = 1.0 + nrm(ks[9], (N_EVEN, DIL_HEAD_DIM), 0.05)
    even_k_norm = 1.0 + nrm(ks[10], (N_EVEN, DIL_HEAD_DIM), 0.05)
    even_w_out = nrm(ks[11], (N_EVEN, EVEN_MIX, D_MODEL), EVEN_MIX ** -0.5)
    odd_norm_w = 1.0 + nrm(ks[12], (N_ODD, D_MODEL), 0.05)
    odd_w_in = nrm(ks[13], (N_ODD, D_MODEL, ODD_IN), D_MODEL ** -0.5)
    odd_q_norm = 1.0 + nrm(ks[14], (N_ODD, MOBA_HEAD_DIM), 0.05)
    odd_k_norm = 1.0 + nrm(ks[15], (N_ODD, MOBA_HEAD_DIM), 0.05)
    odd_w_out = nrm(ks[16], (N_ODD, MOBA_WIDTH, D_MODEL), MOBA_WIDTH ** -0.5)
    return {"x": x, "even_norm_w": even_norm_w, "even_w_in": even_w_in,
            "even_conv_w": even_conv_w, "even_conv_b": even_conv_b,
            "even_dt_bias": even_dt_bias, "even_a_log": even_a_log, "even_d_skip": even_d_skip,
            "even_ssd_norm_w": even_ssd_norm_w, "even_q_norm": even_q_norm,
            "even_k_norm": even_k_norm, "even_w_out": even_w_out,
            "odd_norm_w": odd_norm_w, "odd_w_in": odd_w_in, "odd_q_norm": odd_q_norm,
            "odd_k_norm": odd_k_norm, "odd_w_out": odd_w_out}


def reference(x, even_norm_w, even_w_in, even_conv_w, even_conv_b, even_dt_bias, even_a_log,
              even_d_skip, even_ssd_norm_w, even_q_norm, even_k_norm, even_w_out,
              odd_norm_w, odd_w_in, odd_q_norm, odd_k_norm, odd_w_out):
    pos = jnp.arange(x.shape[1], dtype=jnp.int32)
    for layer in range(DEPTH):
        li = layer // 2
        if layer % 2 == 0:
            x = ssd_dilated_layer(x, even_norm_w[li], even_w_in[li], even_conv_w[li],
                                  even_conv_b[li], even_dt_bias[li], even_a_log[li],
                                  even_d_skip[li], even_ssd_norm_w[li], even_q_norm[li],
                                  even_k_norm[li], even_w_out[li], pos)
        else:
            x = moba_layer(x, odd_norm_w[li], odd_w_in[li], odd_q_norm[li], odd_k_norm[li],
                           odd_w_out[li], pos)
    return x
```

```python
import contextlib
import numpy as np
import concourse.bass as bass
import concourse.mybir as mybir
from concourse.bass_utils import run_bass_kernel_spmd

F32 = mybir.dt.float32
BF16 = mybir.dt.bfloat16
AF = mybir.ActivationFunctionType
ALU = mybir.AluOpType
AX = mybir.AxisListType

SEQ = 4096
NT = 32
DM = 1024
EVEN_IN = 8208
NEG = -30000.0
EPS = 1e-6
ENGINES = ("pe", "act", "dve", "pool", "sp")
EPOCH = 30000


class Sched:
    def __init__(self, nc):
        self.nc = nc
        self.ops = []
        self.dummy = None

    @staticmethod
    def _norm(toks):
        return tuple((t,) if not isinstance(t, tuple) else t for t in toks)

    def add(self, eng, fn, reads=(), writes=(), dma=None):
        assert eng in ENGINES, eng
        self.ops.append(dict(eng=eng, fn=fn, reads=self._norm(reads), writes=self._norm(writes), dma=dma,
                             deps=set(), needs_inc=False))

    def claim(self, tok, eng="dve"):
        d = self.dummy
        self.add(eng, lambda e: e.memset(d[:, 0:1], 0.0), writes=[tok])

    def _analyze(self):
        state = {}
        kids = {}

        def related(tok):
            out = []
            for n in range(1, len(tok) + 1):
                p = tok[:n]
                if p in state:
                    out.append(p)
            for c in kids.get(tok, ()):
                if c in state:
                    out.append(c)
            return out

        def register(tok):
            if tok not in state:
                state[tok] = [None, []]
                for n in range(1, len(tok)):
                    kids.setdefault(tok[:n], set()).add(tok)

        for i, op in enumerate(self.ops):
            deps = set()
            for k in op["reads"]:
                for r in related(k):
                    if state[r][0] is not None:
                        deps.add(state[r][0])
            for k in op["writes"]:
                for r in related(k):
                    if state[r][0] is not None:
                        deps.add(state[r][0])
                    deps.update(state[r][1])
            deps.discard(i)
            pruned = set()
            for d in deps:
                p = self.ops[d]
                if p["dma"] is None and op["dma"] is None and p["eng"] == op["eng"] == "pe":
                    continue
                pruned.add(d)
            op["deps"] = pruned
            if op["dma"] is not None:
                op["needs_inc"] = True
            for d in pruned:
                self.ops[d]["needs_inc"] = True
            for k in op["reads"]:
                register(k)
                state[k][1].append(i)
            for k in op["writes"]:
                register(k)
                for c in list(kids.get(k, ())):
                    if c in state:
                        del state[c]
                state[k] = [i, []]

    def emit(self, stack):
        nc = self.nc
        self._analyze()
        counters = {}
        sig = {}
        semkeys = []
        for i, op in enumerate(self.ops):
            if not op["needs_inc"]:
                continue
            if op["dma"] is not None:
                key = ("dma", op["dma"])
                step = 16
            else:
                key = ("eng", op["eng"])
                step = 1
            c = counters.get(key, 0) + 1
            counters[key] = c
            sk = (key, c // EPOCH)
            ec = counters.get(("ec", sk), 0) + step
            counters[("ec", sk)] = ec
            sig[i] = (sk, ec)
            if sk not in semkeys:
                semkeys.append(sk)
        sems = {}
        for n, sk in enumerate(semkeys):
            sems[sk] = stack.enter_context(nc.semaphore("s%d" % n))
        self.n_sems = len(semkeys)
        per_eng = {e: [] for e in ENGINES}
        for i, op in enumerate(self.ops):
            per_eng[op["eng"]].append(i)
        ops = self.ops

        def run(e_name, eobj):
            waited = {}
            for i in per_eng[e_name]:
                op = ops[i]
                need = {}
                for d in op["deps"]:
                    sk, v = sig[d]
                    if waited.get(sk, 0) >= v:
                        continue
                    if need.get(sk, 0) < v:
                        need[sk] = v
                for sk, v in need.items():
                    eobj.wait_ge(sems[sk], v)
                    waited[sk] = v
                if op["fn"] is None:
                    continue
                ins = op["fn"](eobj)
                if op["needs_inc"]:
                    assert ins is not None
                    ins.then_inc(sems[sig[i][0]], 16 if op["dma"] is not None else 1)

        block = stack.enter_context(nc.Block())

        @block.tensor
        def _(e):
            run("pe", e)

        @block.scalar
        def _(e):
            run("act", e)

        @block.vector
        def _(e):
            run("dve", e)

        @block.gpsimd
        def _(e):
            run("pool", e)

        @block.sync
        def _(e):
            run("sp", e)


def ACT(out, in_, func, **kw):
    return lambda e: e.activation(out=out, in_=in_, func=func, **kw)


def TT(out, in0, in1, op):
    return lambda e: e.tensor_tensor(out=out, in0=in0, in1=in1, op=op)


def TS(out, in0, s1, s2, op0, op1=None):
    if op1 is None:
        return lambda e: e.tensor_scalar(out=out, in0=in0, scalar1=s1, scalar2=0.0, op0=op0, op1=ALU.add)
    return lambda e: e.tensor_scalar(out=out, in0=in0, scalar1=s1, scalar2=s2, op0=op0, op1=op1)


def STT(out, in0, scalar, in1, op0, op1):
    return lambda e: e.scalar_tensor_tensor(out=out, in0=in0, scalar=scalar, in1=in1, op0=op0, op1=op1)


def CP(out, in_):
    return lambda e: e.tensor_copy(out=out, in_=in_)


def ACP(out, in_):
    return lambda e: e.copy(out=out, in_=in_)


def DMA(out, in_):
    return lambda e: e.dma_start(out=out, in_=in_)


def MM(items):
    def fn(e):
        ins = None
        for (o, l, r, s0, s1) in items:
            ins = e.matmul(o, lhsT=l, rhs=r, start=s0, stop=s1)
        return ins
    return fn


def TRS(items):
    def fn(e):
        ins = None
        for (o, i_, idn) in items:
            ins = e.transpose(out=o, in_=i_, identity=idn)
        return ins
    return fn


CST_W = 2304


def make_consts():
    c = np.zeros((128, CST_W), np.float32)
    p = np.arange(128)[:, None]
    f = np.arange(128)[None, :]
    c[:, 0:128] = (p == f)
    c[:, 128:256] = (p <= f)
    c[:, 256:384] = 1.0
    c[127, 384:512] = 1.0
    c[:, 512:640] = np.where(p <= f, 0.0, NEG)
    c[:, 640:768] = np.where(p >= f, 0.0, NEG)
    inv = 500000.0 ** (-np.arange(8, dtype=np.float32) * 2.0 / 16.0)
    pos = (np.arange(NT)[None, :, None] * 128 + np.arange(128)[:, None, None]).astype(np.float32)
    ang = pos * inv[None, None, :].astype(np.float32)
    c[:, 768:1024] = np.cos(ang).reshape(128, 256)
    c[:, 1024:1280] = np.sin(ang).reshape(128, 256)
    t = np.arange(NT)[:, None]
    n = np.arange(16)[None, :]
    vb = np.where(n < (t // 2), 0.0, -1e30).astype(np.float32).reshape(1, 512)
    c[:, 1280:1792] = vb
    c[:, 1792:2304] = np.where(n == (t // 2), 0.0, -1.0).astype(np.float32).reshape(1, 512)
    return c


def make_koh():
    k = np.arange(SEQ)[None, :] // 256
    n = np.arange(16)[:, None]
    return np.where(k == n, 30000.0, 0.0).astype(np.float32)


class Prog:
    pass


def build_program(debug=False, stop_after=None, only=None):
    on = lambda p: (only is None) or (p in only)
    nc = bass.Bass("TRN2", target_bir_lowering=False)
    K = Prog()
    din = lambda name, shape: nc.dram_tensor(name, shape, F32, kind="ExternalInput").ap()
    x_d = din("x", [SEQ, DM])
    e_normw = din("even_norm_w", [1, DM])
    e_win = din("even_w_in", [DM, EVEN_IN])
    e_convw = din("even_conv_w", [2048, 4])
    e_convb = din("even_conv_b", [2048, 1])
    e_dtb = din("even_dt_bias", [1, 16])
    e_alog = din("even_a_log", [1, 16])
    e_dskip = din("even_d_skip", [1, 16])
    e_ssdnw = din("even_ssd_norm_w", [1, DM])
    e_qn = din("even_q_norm", [1, 64])
    e_kn = din("even_k_norm", [1, 64])
    e_wout = din("even_w_out", [1536, DM])
    o_normw = din("odd_norm_w", [1, DM])
    o_win = din("odd_w_in", [DM, 4096])
    o_qn = din("odd_q_norm", [1, 64])
    o_kn = din("odd_k_norm", [1, 64])
    o_wout = din("odd_w_out", [DM, DM])
    cst_d = din("cst", [128, CST_W])
    koh_d = din("koh", [16, SEQ])
    out_d = nc.dram_tensor("out", [SEQ, DM], F32, kind="ExternalOutput").ap()
    skind = "ExternalOutput" if debug else "Internal"
    mixT0 = nc.dram_tensor("mixT0", [12, 128, SEQ], BF16, kind=skind).ap()
    x1_d = nc.dram_tensor("x1s", [SEQ, DM], F32, kind=skind).ap()
    mixT1 = nc.dram_tensor("mixT1", [8, 128, SEQ], BF16, kind=skind).ap()
    acs_scr = nc.dram_tensor("acs_scr", [64, 16, 256], F32, kind="Internal").ap()

    st = contextlib.ExitStack()
    sb = lambda name, shape, dt: st.enter_context(nc.sbuf_tensor("k_" + name, shape, dt))
    cst = sb("cst_sb", [128, CST_W], F32)
    cb = sb("cstb", [128, 640], BF16)
    cv = sb("cv", [128, 8], F32)
    hT = sb("hT", [128, 8, SEQ], BF16)
    stats = sb("stats", [128, 640], F32)
    dummy = sb("dummy", [128, 4], F32)
    ARN = 65536
    AR = sb("arena", [128, ARN], BF16)
    ps = [st.enter_context(nc.psum_tensor("ps%d" % i, [128, 512], F32)) for i in range(8)]
    psb = [p[:].bitcast(BF16) for p in ps]

    S = Sched(nc)
    S.dummy = dummy
    identf = cst[:, 0:128]
    triU = cst[:, 128:256]
    onesf = cst[:, 256:384]
    e127 = cst[:, 384:512]
    cosT = cst[:, 768:1024].rearrange("p (t i) -> p t i", i=8)
    sinT = cst[:, 1024:1280].rearrange("p (t i) -> p t i", i=8)
    validb = cst[:, 1280:1792]
    ownm1 = cst[:, 1792:2304]
    identb = cb[:, 0:128]
    maskLE = cb[:, 128:256]
    maskGE = cb[:, 256:384]
    onesb = cb[:, 384:512]
    id30k = cb[:, 512:640]
    maskPO = cb[:, 128:384]
    PS = lambda b: ("ps", b)

    def arv(off_bytes, shape, dt):
        n = int(np.prod(shape[1:]))
        esz = 2 if dt == BF16 else 4
        a = AR[:, off_bytes // 2: off_bytes // 2 + n * esz // 2]
        if dt != BF16:
            a = a.bitcast(dt)
        if len(shape) == 3:
            a = a.rearrange("p (a b) -> p a b", b=shape[2])
        elif len(shape) == 4:
            a = a.rearrange("p (a b c) -> p a b c", b=shape[2], c=shape[3])
        return a

    S.add("sp", DMA(cst[:], cst_d[:, :]), writes=["cst"], dma="cst")
    S.add("dve", CP(cb[:, 0:128], cst[:, 0:128]), reads=["cst"], writes=["cb"])
    S.add("dve", CP(cb[:, 128:256], cst[:, 512:640]), reads=["cst"], writes=["cb"])
    S.add("dve", CP(cb[:, 256:384], cst[:, 640:768]), reads=["cst"], writes=["cb"])
    S.add("dve", CP(cb[:, 384:512], cst[:, 256:384]), reads=["cst"], writes=["cb"])
    S.add("dve", TS(cb[:, 512:640], cst[:, 0:128], 30000.0, None, ALU.mult), reads=["cst"], writes=["cb"])
    S.add("dve", lambda e: e.memset(cv[:, 0:1], EPS), writes=["cv"])
    S.add("dve", lambda e: e.memset(stats[:], 0.0), writes=["stats"])

    def qk_shift(qn_d, kn_d, col, nwqk, scr, scr_tok):
        S.add("sp", DMA(nwqk[:, 0, :], qn_d[0:1, :].partition_broadcast(128)), writes=[("AR", "nwqk")], dma="nwq0")
        S.add("sp", DMA(nwqk[:, 2, :], kn_d[0:1, :].partition_broadcast(128)), writes=[("AR", "nwqk")], dma="nwq1")
        tmp = stats[:, 480:484]
        S.add("act", ACT(scr[:, 0:64], nwqk[:, 0, :], AF.Abs), reads=[("AR", "nwqk")], writes=[scr_tok])
        S.add("act", ACT(scr[:, 64:128], nwqk[:, 2, :], AF.Abs), reads=[("AR", "nwqk")], writes=[scr_tok])
        S.add("dve", lambda e: e.tensor_reduce(out=tmp[:, 0:1], in_=scr[:, 0:64], axis=AX.X, op=ALU.max),
              reads=[scr_tok], writes=[("stats", "c")])
        S.add("dve", lambda e: e.tensor_reduce(out=tmp[:, 1:2], in_=scr[:, 64:128], axis=AX.X, op=ALU.max),
              reads=[scr_tok], writes=[("stats", "c")])
        S.add("dve", TT(tmp[:, 2:3], tmp[:, 0:1], tmp[:, 1:2], ALU.mult), reads=[("stats", "c")], writes=[("stats", "c")])
        S.add("dve", TS(cv[:, col:col + 1], tmp[:, 2:3], -8.0, None, ALU.mult), reads=[("stats", "c")], writes=["cv"])
        S.add("dve", TS(nwqk[:, 0, :], nwqk[:, 0, :], 0.125, None, ALU.mult), reads=[("AR", "nwqk")], writes=[("AR", "nwqk")])
        S.add("dve", CP(nwqk[:, 1, :], nwqk[:, 0, :]), reads=[("AR", "nwqk")], writes=[("AR", "nwqk")])
        S.add("dve", CP(nwqk[:, 3, :], nwqk[:, 2, :]), reads=[("AR", "nwqk")], writes=[("AR", "nwqk")])

    def norm_transpose_tile(xt, xt_tok, t, nw, nw_tok, hb, hb_tok, junk, scol, trbank):
        ss = stats[:, scol + t: scol + t + 1]
        ln = stats[:, scol + 32 + t: scol + 33 + t]
        rs = stats[:, scol + 64 + t: scol + 65 + t]
        stok = ("stats", scol + t)
        S.add("act", ACT(junk, xt, AF.Square, accum_out=ss), reads=[xt_tok], writes=[("AR", "junk"), stok])
        S.add("act", ACT(ln, ss, AF.Ln, scale=1.0 / DM, bias=cv[:, 0:1]), reads=[stok, "cv"], writes=[stok])
        S.add("act", ACT(rs, ln, AF.Exp, scale=-0.5), reads=[stok], writes=[stok])
        S.add("dve", STT(hb, xt, rs, nw, ALU.mult, ALU.mult), reads=[xt_tok, stok, nw_tok], writes=[hb_tok])
        pb = psb[trbank]
        S.add("pe", TRS([(pb[:, kc * 128:(kc + 1) * 128], hb[:, kc * 128:(kc + 1) * 128], identb) for kc in range(8)]),
              reads=[hb_tok, "cb"], writes=[PS(trbank)])
        S.add("act", ACP(hT[:, :, t * 128:(t + 1) * 128], pb[:, :].rearrange("p (k c) -> p k c", c=128)),
              writes=[PS(trbank), ("hT", t)])

    def phase_norm(src_d, normw_d, scol):
        S.claim(("AR",))
        xin = [arv(i * 4096, [128, DM], F32) for i in range(2)]
        hb = [arv(8192 + i * 2048, [128, DM], BF16) for i in range(2)]
        nw = arv(12288, [128, DM], F32)
        junk = arv(16384, [128, DM], BF16)
        S.add("sp", DMA(nw, normw_d[0:1, :].partition_broadcast(128)), writes=[("AR", "nw")], dma="nw")
        for t in range(NT):
            s = t % 2
            S.add("sp", DMA(xin[s], src_d[t * 128:(t + 1) * 128, :]), writes=[("AR", "xin", s)], dma=("xin", s))
            norm_transpose_tile(xin[s], ("AR", "xin", s), t, nw, ("AR", "nw"), hb[s], ("AR", "hb", s), junk, 0 + scol, 2 + s)

    def load_w(dst, wv, c0, n, tok, chan):
        S.add("pool", DMA(dst, wv[:, :, c0:c0 + n]), writes=[tok], dma=chan)

    def proj_tm(bank, cols, t, w, n, reads, toks=None):
        items = [(ps[bank][:, cols:cols + n], hT[:, kc, t * 128:(t + 1) * 128], w[:, kc, 0:n], kc == 0, kc == 7) for kc in range(8)]
        S.add("pe", MM(items), reads=[("hT",)] + list(reads), writes=[PS(bank)])

    e_winv = e_win.rearrange("(kc p) f -> p kc f", p=128)
    if on('norm'):
        phase_norm(x_d, e_normw, 0)

    def phase_ssd():
        S.claim(("AR",))
        A_ = lambda *k: ("AR",) + k
        o = 0

        def alloc(nbytes):
            nonlocal o
            r = o
            o += (nbytes + 63) // 64 * 64
            assert o <= ARN * 2, o
            return r
        wconv = arv(alloc(8 * 128 * 2), [128, 8, 128], BF16)
        wz_off = alloc(8 * 256 * 2)
        wz = arv(wz_off, [128, 8, 256], BF16)
        wdt = arv(alloc(8 * 16 * 2), [128, 8, 16], BF16)
        cin_off = alloc((SEQ + 4) * 4)
        cin = arv(cin_off, [128, SEQ + 4], F32)
        zs_all = arv(cin_off, [128, NT, 256], BF16)
        cacc_off = alloc(1024 * 4)
        cacc = arv(cacc_off, [128, 1024], F32)
        cacc2 = [arv(cacc_off + j * 2048, [128, 512], F32) for j in range(2)]
        cwb = arv(alloc(8 * 4), [128, 8], F32)
        ch_off = alloc(4 * SEQ * 2)
        chT = [arv(ch_off + i * SEQ * 2, [128, SEQ], BF16) for i in range(4)]
        mixst = arv(ch_off, [128, 2, SEQ], BF16)
        xs_tok = arv(alloc(NT * 256 * 2), [128, NT, 256], BF16)
        B_tok = arv(alloc(NT * 128 * 2), [128, NT, 128], BF16)
        xdt_c = [arv(alloc(512 * 2), [128, 2, 256], BF16) for _ in range(2)]
        xdtw_c = [arv(alloc(512 * 2), [128, 2, 256], BF16) for _ in range(2)]
        dtt = arv(alloc(512 * 4), [128, NT, 16], F32)
        acs = arv(alloc(512 * 4), [128, NT, 16], F32)
        wst = arv(alloc(512 * 4), [128, NT, 16], F32)
        tmp5 = arv(alloc(512 * 4), [128, NT, 16], F32)
        tmp6 = arv(alloc(512 * 4), [128, NT, 16], F32)
        wend = tmp6
        dec = arv(alloc(256 * 4), [128, 16, 16], F32)
        aend = arv(alloc(256 * 4), [128, 16, 16], F32)
        dtg = arv(alloc(128 * 4), [128, 128], F32)
        dtwg = arv(alloc(128 * 4), [128, 128], F32)
        wsg = arv(alloc(128 * 4), [128, 128], F32)
        bc16 = arv(alloc(64 * 4), [128, 4, 16], F32)
        acsT = [arv(alloc(256 * 4), [128, 256], F32) for _ in range(2)]
        Sst = arv(alloc(256 * 4), [128, 256], F32)
        Sbf = [arv(alloc(256 * 2), [128, 256], BF16) for _ in range(2)]
        Dt = [arv(alloc(384 * 2), [128, 384], BF16) for _ in range(2)]
        Ag = [arv(alloc(384 * 4), [128, 384], F32) for _ in range(2)]
        CBm = [arv(alloc(384 * 4), [128, 384], F32)] * 2
        m384 = arv(alloc(384 * 2), [128, 384], BF16)
        Ag1 = arv(alloc(384 * 4), [128, 384], F32)
        bcb = [[arv(off + h * 1024, [128, 256], F32) for h in range(4)] for off in (cacc_off, wz_off)]
        bcb_tok = [[A_("cacc", h) for h in range(4)], [A_("wz", h) for h in range(4)]]
        Mt = [arv(alloc(384 * 2), [128, 384], BF16) for _ in range(4)]
        E1 = arv(alloc(512 * 4), [128, 2, 256], F32)
        E2 = [arv(alloc(512 * 4), [128, 2, 256], F32) for _ in range(2)]
        E3 = [arv(alloc(512 * 4), [128, 2, 256], F32)] * 2
        yb = [arv(alloc(512 * 2), [128, 2, 256], BF16) for _ in range(2)]
        junk = arv(alloc(512 * 2), [128, 512], BF16)
        nwg = arv(alloc(256 * 4), [128, 256], F32)

        A = lambda *k: ("AR",) + k
        S.add("dve", CP(m384[:, 0:128], triU), reads=["cst"], writes=[A("m384")])
        S.add("dve", CP(m384[:, 128:256], onesf), reads=["cst"], writes=[A("m384")])
        S.add("dve", CP(m384[:, 256:384], triU), reads=["cst"], writes=[A("m384")])
        S.add("sp", DMA(bc16[:, 0, :], e_dtb[0:1, :].partition_broadcast(128)), writes=[A("bc16")], dma="bc0")
        S.add("sp", DMA(bc16[:, 1, :], e_alog[0:1, :].partition_broadcast(128)), writes=[A("bc16")], dma="bc1")
        S.add("sp", DMA(bc16[:, 2, :], e_dskip[0:1, :].partition_broadcast(128)), writes=[A("bc16")], dma="bc2")
        S.add("act", ACT(bc16[:, 1, :], bc16[:, 1, :], AF.Exp), reads=[A("bc16")], writes=[A("bc16")])
        S.add("dve", TS(bc16[:, 1, :], bc16[:, 1, :], -1.0, None, ALU.mult), reads=[A("bc16")], writes=[A("bc16")])
        load_w(wdt, e_winv, 3072, 16, A("wdt"), "wdt")
        for t in range(NT):
            proj_tm(0, t * 16, t, wdt, 16, [A("wdt")])
        dflat = lambda a: a.rearrange("p t h -> p (t h)")
        S.add("dve", TT(dtt, ps[0][:, :].rearrange("p (t h) -> p t h", h=16), bc16[:, 0:1, :].to_broadcast([128, NT, 16]), ALU.add),
              reads=[A("bc16")], writes=[PS(0), A("dtt")])
        S.add("act", ACT(dflat(tmp5), dflat(dtt), AF.Abs), reads=[A("dtt")], writes=[A("tmp5")])
        S.add("act", ACT(dflat(tmp5), dflat(tmp5), AF.Exp, scale=-1.0), reads=[A("tmp5")], writes=[A("tmp5")])
        S.add("act", ACT(dflat(tmp5), dflat(tmp5), AF.Ln, bias=1.0), reads=[A("tmp5")], writes=[A("tmp5")])
        S.add("dve", TS(dflat(tmp6), dflat(dtt), 0.0, None, ALU.max), reads=[A("dtt")], writes=[A("tmp6")])
        S.add("dve", TT(dflat(dtt), dflat(tmp6), dflat(tmp5), ALU.add), reads=[A("tmp5"), A("tmp6")], writes=[A("dtt")])
        S.add("dve", TT(tmp6, dtt, bc16[:, 1:2, :].to_broadcast([128, NT, 16]), ALU.mult), reads=[A("dtt"), A("bc16")], writes=[A("tmp6")])
        a4 = tmp6.rearrange("p (c j) h -> p c j h", j=2)
        acs4 = acs.rearrange("p (c j) h -> p c j h", j=2)
        S.add("pe", MM([(ps[1][:, 0:256], triU, a4[:, :, 0, :], True, True),
                        (ps[1][:, 256:512], onesf, a4[:, :, 0, :], True, False),
                        (ps[1][:, 256:512], triU, a4[:, :, 1, :], False, True)]), reads=[A("tmp6"), "cst"], writes=[PS(1)])
        S.add("dve", CP(acs4[:, :, 0, :], ps[1][:, 0:256].rearrange("p (c h) -> p c h", h=16)), writes=[PS(1), A("acs")])
        S.add("dve", CP(acs4[:, :, 1, :], ps[1][:, 256:512].rearrange("p (c h) -> p c h", h=16)), writes=[PS(1), A("acs")])
        S.add("pe", MM([(ps[0][:, 0:256], e127, acs4[:, :, 1, :], True, True)]), reads=[A("acs"), "cst"], writes=[PS(0)])
        S.add("act", ACT(dec.rearrange("p c h -> p (c h)"), ps[0][:, 0:256], AF.Exp), writes=[PS(0), A("dec")])
        S.add("dve", CP(aend.rearrange("p c h -> p (c h)"), ps[0][:, 0:256]), writes=[PS(0), A("aend")])
        for j in range(2):
            S.add("dve", TT(wend.rearrange("p (c j) h -> p c j h", j=2)[:, :, j, :], acs4[:, :, j, :], aend, ALU.subtract),
                  reads=[A("acs"), A("aend")], writes=[A("tmp6")])
        S.add("act", ACT(dflat(wend), dflat(wend), AF.Exp, scale=-1.0), reads=[A("tmp6")], writes=[A("tmp6")])
        S.add("act", ACT(dflat(wst), dflat(acs), AF.Exp), reads=[A("acs")], writes=[A("wst")])
        S.add("dve", TT(dflat(tmp5), dflat(dtt), dflat(wend), ALU.mult), reads=[A("dtt"), A("tmp6")], writes=[A("tmp5")])
        S.add("dve", TS(dflat(acs), dflat(acs), -1.0, None, ALU.mult), reads=[A("acs")], writes=[A("acs")])

        for g in range(4):
            chans = [1024 + g * 256, 1024 + g * 256 + 128, 2048 + g * 128, 2560 + g * 128]
            S.claim(A("cacc"))
            S.add("dve", lambda e: e.memset(cin[:, 0:4], 0.0), writes=[A("cin")])
            for ci, c0 in enumerate(chans):
                load_w(wconv, e_winv, c0, 128, A("wconv"), "wconv")
                S.add("sp", DMA(cwb[:, 0:4], e_convw[c0 - 1024:c0 - 1024 + 128, :]), writes=[A("cwb")], dma="cw")
                S.add("sp", DMA(cwb[:, 4:5], e_convb[c0 - 1024:c0 - 1024 + 128, :]), writes=[A("cwb")], dma="cb")
                def silu_step(i):
                    j = i % 2
                    S.add("act", ACT(chT[ci][:, i * 512:(i + 1) * 512], cacc2[j], AF.Silu), reads=[A("cacc", "h", j)], writes=[A("chT", ci, i // 2)])
                for i in range(8):
                    bk = i % 2
                    j = i % 2
                    items = [(ps[bk][:, 0:512], wconv[:, kc, :], hT[:, kc, i * 512:(i + 1) * 512], kc == 0, kc == 7) for kc in range(8)]
                    S.add("pe", MM(items), reads=[("hT",), A("wconv")], writes=[PS(bk)])
                    S.add("act", ACP(cin[:, 4 + i * 512: 4 + (i + 1) * 512], ps[bk][:, 0:512]), writes=[PS(bk), A("cin", i)])
                    q0 = i * 512
                    rd = [A("cin", i), A("cwb")] + ([A("cin", i - 1)] if i else [A("cin")])
                    S.add("act", ACT(cacc2[j], cin[:, 4 + q0: 4 + q0 + 512], AF.Identity, scale=cwb[:, 3:4], bias=cwb[:, 4:5]),
                          reads=rd, writes=[A("cacc", "h", j)])
                    for kk in (2, 1, 0):
                        sh = 3 - kk
                        S.add("dve", STT(cacc2[j], cin[:, 4 + q0 - sh: 4 + q0 - sh + 512], cwb[:, kk:kk + 1], cacc2[j], ALU.mult, ALU.add),
                              reads=rd + [A("cacc", "h", j)], writes=[A("cacc", "h", j)])
                    if i > 0:
                        silu_step(i - 1)
                silu_step(7)
            S.claim(A("cacc"))
            for t in range(NT):
                bk = 2 + t % 2
                S.add("pe", TRS([(psb[bk][:, j * 128:(j + 1) * 128], chT[j][:, t * 128:(t + 1) * 128], identb) for j in range(3)]),
                      reads=[A("chT", 0), A("chT", 1), A("chT", 2), "cb"], writes=[PS(bk)])
                S.add("act", ACP(xs_tok[:, t, :], psb[bk][:, 0:256]), writes=[PS(bk), A("xs_tok", t)])
                S.add("dve", CP(B_tok[:, t, :], psb[bk][:, 256:384]), writes=[PS(bk), A("B_tok", t)])
            S.add("dve", CP(dtg.rearrange("p (t h) -> p t h", h=4), dtt[:, :, 4 * g:4 * g + 4]), reads=[A("dtt")], writes=[A("dtg")])
            S.add("dve", CP(dtwg.rearrange("p (t h) -> p t h", h=4), tmp5[:, :, 4 * g:4 * g + 4]), reads=[A("tmp5")], writes=[A("dtwg")])
            S.add("dve", CP(wsg.rearrange("p (t h) -> p t h", h=4), wst[:, :, 4 * g:4 * g + 4]), reads=[A("wst")], writes=[A("wsg")])
            load_w(wz, e_winv, g * 256, 256, A("wz"), "wz")
            S.claim(A("cin"))
            for t in range(NT):
                proj_tm(t % 2, 0, t, wz, 256, [A("wz")])
                S.add("act", ACT(zs_all[:, t, :], ps[t % 2][:, 0:256], AF.Silu), writes=[PS(t % 2), A("cin", "zs", t)])
            S.add("sp", DMA(nwg, e_ssdnw[0:1, g * 256:(g + 1) * 256].partition_broadcast(128)), writes=[A("nwg")], dma="nwg")
            S.add("dve", lambda e: e.memset(Sst[:], 0.0), writes=[A("Sst")])
            BT, CT = chT[2], chT[3]
            DB = (5, 0)

            def prep(c):
                t0 = 2 * c
                ops = []
                xs3 = xs_tok[:, t0:t0 + 2, :].rearrange("p t (h e) -> p (t h) e", e=64)
                ops.append(("dve", TT(xdt_c[c % 2].rearrange("p t (h e) -> p (t h) e", e=64), xs3,
                                      dtg[:, t0 * 4:t0 * 4 + 8].unsqueeze(2).to_broadcast([128, 8, 64]), ALU.mult),
                            [A("xs_tok", t0), A("xs_tok", t0 + 1), A("dtg")], [A("xdt", c % 2)]))
                ops.append(("pool", TT(xdtw_c[c % 2].rearrange("p t (h e) -> p (t h) e", e=64), xs3,
                                       dtwg[:, t0 * 4:t0 * 4 + 8].unsqueeze(2).to_broadcast([128, 8, 64]), ALU.mult),
                            [A("xs_tok", t0), A("xs_tok", t0 + 1), A("dtwg")], [A("xdtw", c % 2)]))
                return ops

            def prep_acs(c):
                t0 = 2 * c
                aT = acsT[c % 2]
                ops = [("pe", TRS([(ps[3][0:16, 0:128], acs[:, t0, :], identf), (ps[3][0:16, 128:256], acs[:, t0 + 1, :], identf)]),
                        [A("acs"), "cst"], [PS(3)]),
                       ("dve", CP(aT[0:16, :], ps[3][0:16, 0:256]), [], [PS(3), A("acsT", c % 2)]),
                       ("sp", DMA(acs_scr[g * 16 + c, :, :], aT[0:16, :]), [A("acsT", c % 2)], [("acsscr", g * 16 + c)], ("acss", c % 2))]
                for h in range(4):
                    hh = 4 * g + h
                    ops.append(("sp", DMA(bcb[c % 2][h], acs_scr[g * 16 + c, hh:hh + 1, :].partition_broadcast(128)),
                                [("acsscr", g * 16 + c)], [bcb_tok[c % 2][h]], ("acsb", c % 2, h)))
                return ops

            def emit(ops):
                for op in ops:
                    S.add(*op)

            def deferred(c):
                t0 = 2 * c
                e2 = E2[c % 2]
                f2 = lambda a: a.rearrange("p t c -> p (t c)")
                sc0 = 128 + (g * NT + t0)
                st1 = [("pool", TT(E3[0].rearrange("p t (h e) -> p t h e", e=64), xs_tok[:, t0:t0 + 2, :].rearrange("p t (h e) -> p t h e", e=64),
                                  bc16[:, 2, 4 * g:4 * g + 4].unsqueeze(1).unsqueeze(3).to_broadcast([128, 2, 4, 64]), ALU.mult),
                        [A("xs_tok", t0), A("xs_tok", t0 + 1), A("bc16")], [A("E3")]),
                       ("dve", TT(f2(e2), f2(e2), f2(E3[0]), ALU.add), [A("E3"), A("E2", c % 2)], [A("E2", c % 2)]),
                       ("dve", TT(f2(e2), f2(e2), f2(zs_all[:, t0:t0 + 2, :]), ALU.mult), [A("cin", "zs", t0), A("cin", "zs", t0 + 1), A("E2", c % 2)], [A("E2", c % 2)])]
                for j in range(2):
                    st1.append(("act", ACT(junk[:, j * 256:(j + 1) * 256], e2[:, j, :], AF.Square, accum_out=stats[:, sc0 + j:sc0 + j + 1]),
                                [A("E2", c % 2)], [A("junk", j), ("stats", sc0 + j)]))
                ssv = stats[:, sc0:sc0 + 2]
                rsv = stats[:, sc0 + 128:sc0 + 130]
                st2 = [("act", ACT(rsv, ssv, AF.Ln, scale=1.0 / 256, bias=cv[:, 0:1]), [("stats", sc0), ("stats", sc0 + 1), "cv"], [("stats", "r", sc0)]),
                       ("act", ACT(rsv, rsv, AF.Exp, scale=-0.5), [("stats", "r", sc0)], [("stats", "r", sc0)])]
                st3 = [("dve", STT(yb[c % 2][:, j, :], e2[:, j, :], stats[:, sc0 + 128 + j:sc0 + 129 + j], nwg, ALU.mult, ALU.mult),
                        [A("E2", c % 2), ("stats", "r", sc0), A("nwg")], [A("yb", c % 2, j)]) for j in range(2)]
                st4 = [("pe", TRS([(psb[2][:, (j * 2 + i) * 128:(j * 2 + i + 1) * 128], yb[c % 2][:, j, i * 128:(i + 1) * 128], identb)
                                   for j in range(2) for i in range(2)]), [A("yb", c % 2), "cb"], [PS(2)]),
                       ("act", ACP(mixst[:, :, t0 * 128:(t0 + 2) * 128].rearrange("p i (j c) -> p i j c", c=128),
                                   psb[2][:, 0:512].rearrange("p (j i c) -> p i j c", i=2, c=128)),
                        [], [PS(2), A("chT", 0, t0 // 8), A("chT", 1, t0 // 8)])]
                return [st1, st2, st3, st4]

            def head(c):
                t0 = 2 * c
                s0 = c * 256
                S.add("pe", MM([(ps[4][:, 0:256], BT[:, s0:s0 + 128], CT[:, s0:s0 + 256], True, True),
                                (ps[4][:, 256:384], BT[:, s0 + 128:s0 + 256], CT[:, s0 + 128:s0 + 256], True, True)]),
                      reads=[A("chT", 2), A("chT", 3)], writes=[PS(4)])
                S.add("dve", TT(CBm[0][:, :], ps[4][:, 0:384], m384[:, :], ALU.mult), reads=[A("m384")], writes=[PS(4), A("CBm")])
                S.add("pe", MM([(ps[1][:, 0:256], B_tok[:, t0, :], xdtw_c[c % 2][:, 0, :], True, False),
                                (ps[1][:, 0:256], B_tok[:, t0 + 1, :], xdtw_c[c % 2][:, 1, :], False, True)]),
                      reads=[A("B_tok"), A("xdtw", c % 2)], writes=[PS(1)])
                fa(c, 0)
                fa(c, 1)

            def fa(c, h):
                t0 = 2 * c
                hh = 4 * g + h
                d_ = Dt[h % 2]
                bc = bcb[c % 2][h]
                S.add("act", ACT(Ag1[:, 0:256], bc[:, 0:256], AF.Abs, scale=-1.0, bias=acs[:, t0, hh:hh + 1]),
                      reads=[bcb_tok[c % 2][h], A("acs")], writes=[A("Ag1")])
                S.add("act", ACT(Ag1[:, 256:384], bc[:, 128:256], AF.Abs, scale=-1.0, bias=acs[:, t0 + 1, hh:hh + 1]),
                      reads=[bcb_tok[c % 2][h], A("acs")], writes=[A("Ag1")])
                S.add("act", ACT(d_[:, :], Ag1[:, :], AF.Exp, scale=-1.0), reads=[A("Ag1")], writes=[A("Dt", h % 2)])

            def fb(c, h):
                S.add("dve", STT(Mt[h][:, :], Dt[h % 2][:, :], 1.0, CBm[0][:, :], ALU.min, ALU.mult), reads=[A("Dt", h % 2), A("CBm")], writes=[A("Mt", h)])

            def back(c, h):
                m_ = Mt[h]
                xdt = xdt_c[c % 2]
                hc = slice(h * 64, (h + 1) * 64)
                S.add("pe", MM([(ps[6][:, h * 64:(h + 1) * 64], m_[:, 0:128], xdt[:, 0, hc], True, True),
                                (ps[6][:, 256 + h * 64:256 + (h + 1) * 64], m_[:, 128:256], xdt[:, 0, hc], True, False),
                                (ps[6][:, 256 + h * 64:256 + (h + 1) * 64], m_[:, 256:384], xdt[:, 1, hc], False, True)]),
                      reads=[A("Mt", h), A("xdt", c % 2)], writes=[PS(6)])

            def state_update(c):
                S.add("dve", TT(Sst.rearrange("p (h e) -> p h e", e=64), Sst.rearrange("p (h e) -> p h e", e=64),
                                dec[:, c, 4 * g:4 * g + 4].unsqueeze(2).to_broadcast([128, 4, 64]), ALU.mult),
                      reads=[A("Sst"), A("dec")], writes=[A("Sst")])
                S.add("dve", TT(Sst, Sst, ps[1][:, 0:256], ALU.add), reads=[A("Sst")], writes=[PS(1), A("Sst")])
                S.add("dve", CP(Sbf[(c + 1) % 2], Sst), reads=[A("Sst")], writes=[A("Sbf", (c + 1) % 2)])

            emit(prep(0))
            emit(prep_acs(0))
            head(0)
            pend = None
            for c in range(16):
                t0, t1 = 2 * c, 2 * c + 1
                s0 = c * 256
                fb(c, 0)
                if c + 1 < 16:
                    emit(prep_acs(c + 1))
                if pend:
                    emit(pend[0])
                fa(c, 2)
                fb(c, 1)
                state_update(c)
                back(c, 0)
                fa(c, 3)
                fb(c, 2)
                if pend:
                    emit(pend[1])
                if c + 1 < 16:
                    emit(prep(c + 1))
                back(c, 1)
                fb(c, 3)
                if pend:
                    emit(pend[2])
                back(c, 2)
                if pend:
                    emit(pend[3])
                back(c, 3)
                e2 = E2[c % 2]
                f2 = lambda a: a.rearrange("p t c -> p (t c)")
                if c > 0:
                    S.add("pe", MM([(ps[7][:, 0:256], CT[:, s0:s0 + 128], Sbf[c % 2], True, True),
                                    (ps[7][:, 256:512], CT[:, s0 + 128:s0 + 256], Sbf[c % 2], True, True)]),
                          reads=[A("chT", 3), A("Sbf", c % 2)], writes=[PS(7)])
                if c + 1 < 16:
                    head(c + 1)
                if c > 0:
                    S.add("dve", TT(E1.rearrange("p t (h e) -> p (t h) e", e=64), ps[7][:, :].rearrange("p (t h e) -> p (t h) e", h=4, e=64),
                                    wsg[:, t0 * 4:t0 * 4 + 8].unsqueeze(2).to_broadcast([128, 8, 64]), ALU.mult),
                          reads=[A("wsg")], writes=[PS(7), A("E1")])
                    S.add("dve", TT(f2(e2), ps[6][:, :], f2(E1), ALU.add), reads=[A("E1")], writes=[PS(6), A("E2", c % 2)])
                else:
                    S.add("dve", CP(f2(e2), ps[6][:, :]), writes=[PS(6), A("E2", c % 2)])
                pend = deferred(c)
            for stg in pend:
                emit(stg)
            for i in range(2):
                S.add("sp", DMA(mixT0[2 * g + i, :, :], mixst[:, i, :]), reads=[A("chT", i)], writes=[("mixT0", 2 * g + i)], dma=("mx", i))

    if on('ssd'):
        phase_ssd()

    def emit_pipeline(steps, depth=3, lag1=2, lag2=4):
        n = len(steps)
        for i in range(n + depth + lag2):
            if i < n:
                for op in steps[i]["front"]:
                    S.add(*op)
            for key, off in (("back", depth), ("late1", depth + lag1), ("late2", depth + lag2)):
                j = i - off
                if 0 <= j < n:
                    for op in steps[j].get(key, ()):
                        S.add(*op)

    SBANKS = (0, 1, 4, 5)
    DEPTH = 3

    def qk_group(gidx, t0, wqk, nwqk, bufs, qkT, perhead=None, qpad=None):
        qr, t1, t2, pr, qkb = bufs
        s = gidx % 2
        A = lambda *k: ("AR",) + k
        for half in range(2):
            for i in range(2):
                proj_tm(half, i * 256, t0 + 2 * half + i, wqk, 256, [A("wqk")])
            S.add("act", ACP(qr[s][:, 2 * half:2 * half + 2, :], ps[half][:, :].rearrange("p (i c) -> p i c", c=256)),
                  writes=[PS(half), A("qr", s, half)])
        fl = lambda a: a.rearrange("p i c -> p (i c)")
        v3 = lambda a: a.rearrange("p i (h e) -> p (i h) e", e=64)
        v4 = lambda a: a.rearrange("p i (h e) -> p i h e", e=64)
        S.add("pool", TT(fl(t1[s]), fl(qr[s]), fl(qr[s]), ALU.mult), reads=[A("qr", s)], writes=[A("t1", s)])
        ssq = stats[:, 512 + 16 * s: 528 + 16 * s]
        stok = ("stats", "qk", s)
        S.add("dve", lambda e: e.tensor_reduce(out=ssq, in_=v3(t1[s]), axis=AX.X, op=ALU.add), reads=[A("t1", s)], writes=[stok])
        S.add("act", ACT(ssq, ssq, AF.Ln, scale=1.0 / 64, bias=cv[:, 0:1]), reads=[stok, "cv"], writes=[stok])
        S.add("act", ACT(ssq, ssq, AF.Exp, scale=-0.5), reads=[stok], writes=[stok])
        S.add("dve", TT(v3(t1[s]), v3(qr[s]), ssq.unsqueeze(2).to_broadcast([128, 16, 64]), ALU.mult), reads=[stok, A("qr", s)], writes=[A("t1", s)])
        S.add("dve", TT(t2[s], t1[s], nwqk.rearrange("p h e -> p (h e)").unsqueeze(1).to_broadcast([128, 4, 256]), ALU.mult),
              reads=[A("t1", s), A("nwqk")], writes=[A("t1", s)])
        x1 = v4(t2[s])[:, :, :, 0:8]
        x2 = v4(t2[s])[:, :, :, 8:16]
        cs = cosT[:, t0:t0 + 4, :].unsqueeze(2).to_broadcast([128, 4, 4, 8])
        sn = sinT[:, t0:t0 + 4, :].unsqueeze(2).to_broadcast([128, 4, 4, 8])
        p4 = pr[s]
        pj = lambda j: p4[:, j].rearrange("p (i h) e -> p i h e", h=4)
        S.add("pool", TT(pj(0), x1, cs, ALU.mult), reads=[A("t1", s), "cst"], writes=[A("pr", s, 0)])
        S.add("pool", TT(pj(1), x2, sn, ALU.mult), reads=[A("t1", s), "cst"], writes=[A("pr", s, 1)])
        S.add("pool", TT(pj(2), x2, cs, ALU.mult), reads=[A("t1", s), "cst"], writes=[A("pr", s, 2)])
        S.add("pool", TT(pj(3), x1, sn, ALU.mult), reads=[A("t1", s), "cst"], writes=[A("pr", s, 3)])
        qb = v4(qkb[s])
        S.add("dve", TT(qb[:, :, :, 0:8], pj(0), pj(1), ALU.subtract), reads=[A("pr", s, 0), A("pr", s, 1)], writes=[A("qkb", s, 0)])
        S.add("dve", TT(qb[:, :, :, 8:16], pj(2), pj(3), ALU.add), reads=[A("pr", s, 2), A("pr", s, 3)], writes=[A("qkb", s, 1)])
        S.add("act", ACP(qb[:, :, :, 16:64], v4(t2[s])[:, :, :, 16:64]), reads=[A("t1", s)], writes=[A("qkb", s, 2)])
        if perhead is not None:
            for half in range(2):
                bk = 2 + half
                S.add("pe", TRS([(psb[bk][0:64, (jj * 4 + i) * 128:(jj * 4 + i + 1) * 128], qkb[s][:, i, (2 * half + jj) * 64:(2 * half + jj + 1) * 64], identb)
                                 for jj in range(2) for i in range(4)]), reads=[A("qkb", s), "cb"], writes=[PS(bk)])
                for jj in range(2):
                    dst, dtok = perhead[2 * half + jj]
                    eng, fn = ("act", ACP) if jj == 0 else ("dve", CP)
                    S.add(eng, fn(dst[0:64, t0 * 128:(t0 + 4) * 128], psb[bk][0:64, jj * 512:(jj + 1) * 512]),
                          writes=[PS(bk), dtok + ("qk", t0 // 4)])
            return
        bk = 2 + s
        S.add("pe", TRS([(psb[bk][:, (i * 2 + j) * 128:(i * 2 + j + 1) * 128], qkb[s][:, i, j * 128:(j + 1) * 128], identb)
                         for i in range(4) for j in range(2)]), reads=[A("qkb", s), "cb"], writes=[PS(bk)])
        if qpad is not None:
            qp0, qp1, kTt = qpad
            src = psb[bk][:, :].rearrange("p (i j c) -> p j i c", j=2, c=128)
            tv = lambda a: a[:, t0 * 128:(t0 + 4) * 128].rearrange("p (i c) -> p i c", c=128)
            S.add("act", ACP(tv(kTt), src[:, 1]), writes=[PS(bk), A("kTt", t0 // 4)])
            S.add("dve", CP(tv(qp0)[0:64], src[0:64, 0]), writes=[PS(bk), A("qp", 0, t0 // 4)])
            S.add("act", ACP(tv(qp1)[64:128], src[64:128, 0]), writes=[PS(bk), A("qp", 1, t0 // 4)])
            return
        S.add("act", ACP(qkT[:, :, t0 * 128:(t0 + 4) * 128].rearrange("p j (i c) -> p j i c", c=128),
                         psb[bk][:, :].rearrange("p (i j c) -> p j i c", j=2, c=128)),
              writes=[PS(bk)] + [A("qkT", t0 + i) for i in range(4)])

    def qk_groups_interleaved(calls):
        bounds = (0, 6, 8, 10, 12, 16, 19, None)
        recs = []
        for fn in calls:
            n0 = len(S.ops)
            fn()
            ops = S.ops[n0:]
            del S.ops[n0:]
            recs.append([ops[bounds[i]:bounds[i + 1]] for i in range(7)])
        n = len(recs)
        for k in range(n + 1):
            early = recs[k][0:4] if k < n else [[], [], [], []]
            late = recs[k - 1][4:7] if k >= 1 else [[], [], []]
            for stage in (early[0], late[0], early[1], late[1], early[2], late[2], early[3]):
                S.ops.extend(stage)

    def qk_bufs(alloc):
        qr = [arv(alloc(1024 * 4), [128, 4, 256], F32) for _ in range(2)]
        t1 = [arv(alloc(1024 * 4), [128, 4, 256], F32) for _ in range(2)]
        t2 = t1
        pr = [arv(alloc(512 * 4), [128, 4, 16, 8], F32) for _ in range(2)]
        qkb = [arv(alloc(1024 * 2), [128, 4, 256], BF16) for _ in range(2)]
        return (qr, t1, t2, pr, qkb)

    def silu_gate_T(wg, sgT, A):
        for i in range(8):
            bk = i % 2
            items = [(ps[bk][:, 0:512], wg[:, kc, :], hT[:, kc, i * 512:(i + 1) * 512], kc == 0, kc == 7) for kc in range(8)]
            S.add("pe", MM(items), reads=[("hT",), A("wg")], writes=[PS(bk)])
            S.add("act", ACT(sgT[:, i * 512:(i + 1) * 512], ps[bk][:, 0:512], AF.Silu), writes=[PS(bk), A("sgT", i)])

    def phase_dilated():
        S.claim(("AR",))
        o = 0

        def alloc(nbytes):
            nonlocal o
            r = o
            o += (nbytes + 63) // 64 * 64
            assert o <= ARN * 2, o
            return r
        A = lambda *k: ("AR",) + k
        wqk = arv(alloc(8 * 256 * 2), [128, 8, 256], BF16)
        wv = arv(alloc(8 * 128 * 2), [128, 8, 128], BF16)
        wg = arv(alloc(8 * 128 * 2), [128, 8, 128], BF16)
        qp = [arv(alloc(SEQ * 2), [128, SEQ], BF16) for _ in range(2)]
        kTt = arv(alloc(SEQ * 2), [128, SEQ], BF16)
        vblk = arv(alloc(NT * 256 * 2), [128, NT, 2, 128], BF16)
        sgT = arv(alloc(SEQ * 2), [128, SEQ], BF16)
        accN = arv(alloc(SEQ * 4), [128, SEQ], F32)
        accD = arv(alloc(SEQ * 4), [128, SEQ], F32)
        ost = arv(alloc(SEQ * 2), [128, SEQ], BF16)
        nwqk = arv(alloc(256 * 4), [128, 4, 64], F32)
        bufs = qk_bufs(alloc)
        Pt = [arv(alloc(512 * 2), [128, 512], BF16) for _ in range(4)]
        Rn2 = [bufs[1][j][:, 0:2, :].rearrange("p i c -> p (i c)") for j in range(2)]
        m01 = arv(alloc(512 * 2), [128, 512], BF16)
        for hh in range(2):
            S.add("dve", TS(m01[:, hh * 256:hh * 256 + 128], cst[:, 640:768], -1.0, 0.0, ALU.is_ge, ALU.add), reads=["cst"], writes=[A("m01")])
            S.add("dve", TS(m01[:, hh * 256 + 128:hh * 256 + 256], cst[:, 512:640], -1.0, 0.0, ALU.is_ge, ALU.add), reads=["cst"], writes=[A("m01")])
        S.add("pool", lambda e: e.memset(vblk[:, :, 0, 64:128], 1.0), writes=[A("vblk", "ones0")])
        S.add("pool", lambda e: e.memset(vblk[:, :, 1, 0:64], 1.0), writes=[A("vblk", "ones1")])
        S.add("pool", lambda e: e.memset(qp[0][64:128, :], 0.0), writes=[A("qp", 0)])
        S.add("pool", lambda e: e.memset(qp[1][0:64, :], 0.0), writes=[A("qp", 1)])
        qk_shift(e_qn, e_kn, 1, nwqk, bufs[1][0][:, 0, :], A("t1", 0))
        negC = cv[:, 1:2]
        sidx = 0
        gidx = 0
        def dil_loads(jp, gi):
            hcol = (8 * gi + 2 * jp) * 64
            load_w(wqk[:, :, 0:128], e_winv, 3088 + hcol, 128, A("wqk"), "wq")
            load_w(wqk[:, :, 128:256], e_winv, 4624 + hcol, 128, A("wqk"), "wk")
            load_w(wv, e_winv, 6160 + hcol, 128, A("wv"), "wv")

        load_w(wg, e_winv, 7696, 128, A("wg"), "wg")
        dil_loads(0, 0)
        for jp in range(4):
            silu_gate_T(wg, sgT, A)
            if jp + 1 < 4:
                load_w(wg, e_winv, 7696 + (jp + 1) * 128, 128, A("wg"), "wg")
            for gi, r in enumerate((1, 4, 16)):
                calls = []
                for t4 in range(NT // 4):
                    calls.append((lambda gidx=gidx, t4=t4: qk_group(gidx, t4 * 4, wqk, nwqk, bufs, None, qpad=(qp[0], qp[1], kTt))))
                    gidx += 1
                qk_groups_interleaved(calls)
                nb = NT // r
                for b4 in range(NT // 4):
                    bk = b4 % 2
                    for i in range(4):
                        blk = b4 * 4 + i
                        c, n = blk // nb, blk % nb
                        items = []
                        for kc in range(8):
                            hv = hT[:, kc, :].rearrange("p (u r) -> p u r", r=r)[:, 128 * n:128 * n + 128, c]
                            items.append((ps[bk][:, i * 128:(i + 1) * 128], hv, wv[:, kc, :], kc == 0, kc == 7))
                        S.add("pe", MM(items), reads=[("hT",), A("wv")], writes=[PS(bk)])
                    pv3 = ps[bk][:, :].rearrange("p (i c) -> p i c", c=128)
                    S.add("act", ACP(vblk[:, b4 * 4:b4 * 4 + 4, 0, 0:64], pv3[:, :, 0:64]), writes=[PS(bk), A("vblk", "v0", b4)])
                    S.add("dve", CP(vblk[:, b4 * 4:b4 * 4 + 4, 1, 64:128], pv3[:, :, 64:128]), writes=[PS(bk), A("vblk", "v1", b4)])
                qvh = [q_.rearrange("p (u r) -> p u r", r=r) for q_ in qp]
                kv = kTt.rearrange("p (u r) -> p u r", r=r)
                steps = []
                for b4 in range(NT // 4):
                    nbk, dbk = ((6, 7), (2, 3))[b4 % 2]
                    for i in range(4):
                        blk = b4 * 4 + i
                        c, n = blk // nb, blk % nb
                        sb_ = SBANKS[sidx % 4]
                        pt = Pt[sidx % 4]
                        ptk = A("Pt", sidx % 4)
                        sidx += 1
                        front, back = [], []
                        items = []
                        for hh in range(2):
                            hs = slice(64 * hh, 64 * hh + 64)
                            qa = qvh[hh][:, 128 * n:128 * n + 128, c]
                            ko = kv[:, 128 * n:128 * n + 128, c]
                            b0 = hh * 256
                            if n > 0:
                                kp = kv[:, 128 * (n - 1):128 * n, c]
                                items += [(ps[sb_][:, b0:b0 + 128], kp, qa, True, True),
                                          (ps[sb_][:, b0 + 128:b0 + 256], ko, qa, True, True)]
                            else:
                                items += [(ps[sb_][:, b0 + 128:b0 + 256], ko, qa, True, True)]
                        front.append(("pe", MM(items), [A("qp"), A("kTt")], [PS(sb_)]))
                        if n > 0:
                            front.append(("act", ACT(pt[:, :], ps[sb_][:, :], AF.Exp, bias=negC), ["cv"], [PS(sb_), ptk]))
                            front.append(("dve", TT(pt[:, :], pt[:, :], m01[:, :], ALU.mult), [A("m01")], [ptk]))
                        else:
                            for hh in range(2):
                                oc_ = slice(hh * 256 + 128, hh * 256 + 256)
                                front.append(("act", ACT(pt[:, oc_], ps[sb_][:, oc_], AF.Exp, bias=negC), ["cv"], [PS(sb_), ptk]))
                                front.append(("dve", TT(pt[:, oc_], pt[:, oc_], m01[:, oc_], ALU.mult), [A("m01")], [ptk]))
                        items = []
                        for hh in range(2):
                            hs = slice(64 * hh, 64 * hh + 64)
                            oc = slice(i * 128, (i + 1) * 128)
                            b0 = hh * 256
                            dst = ps[(nbk, dbk)[hh]]
                            if n > 0:
                                items += [(dst[:, oc], vblk[:, blk - 1, hh, :], pt[:, b0:b0 + 128], True, False),
                                          (dst[:, oc], vblk[:, blk, hh, :], pt[:, b0 + 128:b0 + 256], False, True)]
                            else:
                                items += [(dst[:, oc], vblk[:, blk, hh, :], pt[:, b0 + 128:b0 + 256], True, True)]
                        back.append(("pe", MM(items), [ptk, A("vblk")], [PS(nbk), PS(dbk)]))
                        if i == 3:
                            blk0 = b4 * 4
                            if r == 16:
                                c0 = blk0 // nb
                                dn = accN.rearrange("p (u r) -> p r u", r=16)[:, c0:c0 + 2, :]
                                dd = accD.rearrange("p (u r) -> p r u", r=16)[:, c0:c0 + 2, :]
                                sn_ = ps[nbk][:, :].rearrange("p (a b) -> p a b", b=256)
                                sd_ = ps[dbk][:, :].rearrange("p (a b) -> p a b", b=256)
                            else:
                                c0, n0 = blk0 // nb, blk0 % nb
                                dn = accN.rearrange("p (u r) -> p u r", r=r)[:, 128 * n0:128 * n0 + 512, c0]
                                dd = accD.rearrange("p (u r) -> p u r", r=r)[:, 128 * n0:128 * n0 + 512, c0]
                                sn_ = ps[nbk][:, :]
                                sd_ = ps[dbk][:, :]
                            if gi == 0:
                                back.append(("act", ACP(dn, sn_), [], [PS(nbk), A("accN")]))
                                back.append(("dve", CP(dd, sd_), [], [PS(dbk), A("accD")]))
                            else:
                                back.append(("dve", TT(dn, sn_, dn, ALU.add), [], [PS(nbk), A("accN")]))
                                back.append(("dve", TT(dd, sd_, dd, ALU.add), [], [PS(dbk), A("accD")]))
                        steps.append(dict(front=front, back=back))
                nxt = jp * 3 + gi + 1
                if nxt < 12:
                    dil_loads(nxt // 3, nxt % 3)
                emit_pipeline(steps, DEPTH)
            for hh, acc, atok in ((0, accN, A("accN")), (1, accD, A("accD"))):
                nr = slice(64 * hh, 64 * hh + 64)
                dr = slice(64 * (1 - hh), 64 * (1 - hh) + 64)
                for hf in range(2):
                    hc_ = slice(hf * 2048, (hf + 1) * 2048)
                    S.add("act", ACT(acc[dr, hc_], acc[dr, hc_], AF.Ln), writes=[atok + (hf,)])
                    S.add("act", ACT(acc[dr, hc_], acc[dr, hc_], AF.Exp, scale=-1.0), writes=[atok + (hf,)])
                for i8 in range(8):
                    cs_ = slice(i8 * 512, (i8 + 1) * 512)
                    bk = i8 % 2
                    rn = Rn2[i8 % 2]
                    rtok = A("t1", i8 % 2)
                    S.add("pe", MM([(ps[bk][nr, 0:512], identf[dr, dr], acc[dr, cs_], True, True)]), reads=[atok + (i8 // 4,), "cst"], writes=[PS(bk)])
                    S.add("dve", TT(rn[nr, :], ps[bk][nr, 0:512], acc[nr, cs_], ALU.mult), reads=[atok], writes=[PS(bk), rtok])
                    S.add("pool", TT(ost[nr, cs_], rn[nr, :], sgT[nr, cs_], ALU.mult), reads=[rtok, A("sgT")], writes=[A("ost", hh, i8)])
            S.add("sp", DMA(mixT0[8 + jp, :, :], ost[:, :]), reads=[A("ost")], writes=[("mixT0", 8 + jp)], dma="ost")

    if on('dil'):
        phase_dilated()

    def phase_outproj(mixT_d, nch, wout_d, res_d, dst_d, next_normw_d, scol, dst_tok, mix_tok, res_tok):
        S.claim(("AR",))
        A = lambda *k: ("AR",) + k
        o = 0

        def alloc(nbytes):
            nonlocal o
            r = o
            o += (nbytes + 63) // 64 * 64
            assert o <= ARN * 2, o
            return r
        wo = arv(alloc(nch * DM * 2), [128, nch, DM], BF16)
        NBUF = 3
        mt = [arv(alloc(nch * 128 * 2), [128, nch, 128], BF16) for _ in range(NBUF)]
        xr = [arv(alloc(DM * 4), [128, DM], F32) for _ in range(NBUF)]
        x1t = [arv(alloc(DM * 4), [128, DM], F32) for _ in range(NBUF)]
        hb = [arv(alloc(DM * 2), [128, DM], BF16) for _ in range(NBUF)]
        nw = arv(alloc(DM * 4), [128, DM], F32)
        junk = arv(alloc(DM * 2), [128, DM], BF16)
        wov = wout_d.rearrange("(c p) f -> p c f", p=128)
        for c in range(0, nch, 4):
            S.add("pool", DMA(wo[:, c:c + 4, :], wov[:, c:c + 4, :]), writes=[A("wo", c)], dma=("wo", c))
        if next_normw_d is not None:
            S.add("sp", DMA(nw, next_normw_d[0:1, :].partition_broadcast(128)), writes=[A("nw")], dma="nw")
        mv = mixT_d.rearrange("c p t -> p c t")
        for t in range(NT):
            s = t % NBUF
            S.add("sp", DMA(mt[s], mv[:, :, t * 128:(t + 1) * 128]), reads=mix_tok, writes=[A("mt", s)], dma=("mt", s))
            S.add("sp", DMA(xr[s], res_d[t * 128:(t + 1) * 128, :]), reads=res_tok, writes=[A("xr", s)], dma=("xr", s))
            for hf in range(2):
                items = [(ps[hf][:, 0:512], mt[s][:, c, :], wo[:, c, hf * 512:(hf + 1) * 512], c == 0, c == nch - 1) for c in range(nch)]
                S.add("pe", MM(items), reads=[A("mt", s), A("wo")], writes=[PS(hf)])
                S.add("dve", TT(x1t[s][:, hf * 512:(hf + 1) * 512], ps[hf][:, 0:512], xr[s][:, hf * 512:(hf + 1) * 512], ALU.add),
                      reads=[A("xr", s)], writes=[PS(hf), A("x1t", s, hf)])
            S.add("pool", DMA(dst_d[t * 128:(t + 1) * 128, :], x1t[s]), reads=[A("x1t", s)], writes=[(dst_tok, t)], dma=("x1o", s))
            if next_normw_d is not None:
                norm_transpose_tile(x1t[s], A("x1t", s), t, nw, A("nw"), hb[s], A("hb", s), junk, scol, 2 + t % 2)

    if on('out0'):
        phase_outproj(mixT0, 12, e_wout, x_d, x1_d, o_normw, 384, "x1s", [("mixT0",)], [])

    o_winv = o_win.rearrange("(kc p) f -> p kc f", p=128)

    def phase_moba():
        S.claim(("AR",))
        o = 0

        def alloc(nbytes):
            nonlocal o
            r = o
            o += (nbytes + 63) // 64 * 64
            assert o <= ARN * 2, o
            return r
        A = lambda *k: ("AR",) + k
        wqk = arv(alloc(8 * 256 * 2), [128, 8, 256], BF16)
        wv = arv(alloc(8 * 128 * 2), [128, 8, 128], BF16)
        wg = arv(alloc(8 * 128 * 2), [128, 8, 128], BF16)
        qaug = [arv(alloc(SEQ * 2), [128, SEQ], BF16) for _ in range(2)]
        kaug = [arv(alloc(SEQ * 2), [128, SEQ], BF16) for _ in range(2)]
        vaug = arv(alloc(NT * 256 * 2), [128, NT, 2, 128], BF16)
        sgT = arv(alloc(SEQ * 2), [128, SEQ], BF16)
        ost = arv(alloc(SEQ * 2), [128, SEQ], BF16)
        nwqk = arv(alloc(256 * 4), [128, 4, 64], F32)
        bufs = qk_bufs(alloc)
        Pt = [arv(alloc(512 * 2), [128, 512], BF16) for _ in range(3)]
        kbf2 = [arv(alloc(16 * 4), [128, 16], F32) for _ in range(2)]
        kbb2 = [arv(alloc(16 * 2), [128, 16], BF16) for _ in range(2)]
        gm2 = [arv(alloc(512 * 4), [128, NT, 16], F32) for _ in range(2)]
        mx2 = [arv(alloc(NT * 8 * 4), [128, NT, 8], F32) for _ in range(2)]
        thr2 = [arv(alloc(NT * 4), [128, NT], F32) for _ in range(2)]
        selb2 = [arv(alloc(512 * 2), [128, NT, 16], BF16) for _ in range(2)]
        Rt = [arv(alloc(256 * 4), [128, 256], F32) for _ in range(4)]
        rs = [arv(alloc(256 * 4), [128, 256], F32) for _ in range(4)]
        oo = [arv(alloc(256 * 4), [128, 256], F32) for _ in range(4)]
        qk_shift(o_qn, o_kn, 2, nwqk, bufs[1][0][:, 0, :], A("t1", 0))
        negC = cv[:, 2:3]
        QA = [A("qaug", h) for h in range(2)]
        KA = [A("kaug", h) for h in range(2)]
        for h in range(2):
            S.add("pool", (lambda h=h: (lambda e: e.memset(qaug[h][64:128, :], 0.0)))(), writes=[QA[h]])
            S.add("pool", (lambda h=h: (lambda e: e.memset(kaug[h][64:128, :], 0.0)))(), writes=[KA[h]])
            S.add("pool", DMA(kaug[h][64:80, :], koh_d[:, :]), writes=[KA[h] + ("oh",)], dma=("koh", h))
        S.add("pool", lambda e: e.memset(vaug[:, :, 0, 64:128], 1.0), writes=[A("vaug", "ones0")])
        S.add("pool", lambda e: e.memset(vaug[:, :, 1, 0:64], 1.0), writes=[A("vaug", "ones1")])
        MBANKS = (0, 1, 4)
        sidx = 0
        gidx = 0
        def moba_loads(hp):
            load_w(wg, o_winv, 3072 + hp * 128, 128, A("wg"), "wg")
            load_w(wqk[:, :, 0:128], o_winv, hp * 128, 128, A("wqk"), "wq")
            load_w(wqk[:, :, 128:256], o_winv, 1024 + hp * 128, 128, A("wqk"), "wk")
            load_w(wv, o_winv, 2048 + hp * 128, 128, A("wv"), "wv")

        moba_loads(0)
        for hp in range(8):
            silu_gate_T(wg, sgT, A)
            calls = []
            for t4 in range(NT // 4):
                calls.append((lambda gidx=gidx, t4=t4: qk_group(gidx, t4 * 4, wqk, nwqk, bufs, None,
                              perhead=[(qaug[0], QA[0]), (qaug[1], QA[1]), (kaug[0], KA[0]), (kaug[1], KA[1])])))
                gidx += 1
            qk_groups_interleaved(calls)
            for hh in range(2):
                S.add("dve", (lambda hh=hh: (lambda e: e.tensor_reduce(out=kbf2[hh][0:64, :], in_=kaug[hh][0:64, :].rearrange("p (n k) -> p n k", k=256),
                                                                    axis=AX.X, op=ALU.add)))(), reads=[KA[hh]], writes=[A("kbf", hh)])
                S.add("dve", TS(kbb2[hh][0:64, :], kbf2[hh][0:64, :], 1.0 / 256, None, ALU.mult), reads=[A("kbf", hh)], writes=[A("kbb", hh)])
            for hh in range(2):
                gb = 4 + hh
                items = [(ps[gb][:, t * 16:(t + 1) * 16], qaug[hh][0:64, t * 128:(t + 1) * 128], kbb2[hh][0:64, :], True, True) for t in range(NT)]
                S.add("pe", MM(items), reads=[QA[hh], A("kbb", hh)], writes=[PS(gb)])
            for b4 in range(NT // 4):
                bk = b4 % 2
                for i in range(4):
                    t = b4 * 4 + i
                    proj_tm(bk, i * 128, t, wv, 128, [A("wv")])
                pv3 = ps[bk][:, :].rearrange("p (i c) -> p i c", c=128)
                S.add("act", ACP(vaug[:, b4 * 4:b4 * 4 + 4, 0, 0:64], pv3[:, :, 0:64]), writes=[PS(bk), A("vaug", "v0", b4)])
                S.add("act", ACP(vaug[:, b4 * 4:b4 * 4 + 4, 1, 64:128], pv3[:, :, 64:128]), writes=[PS(bk), A("vaug", "v1", b4)])
            for hh in range(2):
                gb = 4 + hh
                gm_, mx_, thr_, selb_ = gm2[hh], mx2[hh], thr2[hh], selb2[hh]
                G = lambda *k: A("g", hh) + k
                S.add("dve", TT(gm_.rearrange("p t n -> p (t n)"), ps[gb][:, :], validb, ALU.add), reads=["cst"], writes=[PS(gb), G("gm")])
                for t in range(NT):
                    S.add("dve", (lambda t=t, mx_=mx_, gm_=gm_: (lambda e: e.max(out=mx_[:, t, :], in_=gm_[:, t, :])))(), reads=[G("gm")], writes=[G("mx", t)])
                S.add("dve", TS(thr_[:, :], mx_[:, :, 2], -1e29, None, ALU.max), reads=[G("mx")], writes=[G("thr")])
                S.add("dve", TT(gm_, gm_, thr_.unsqueeze(2).to_broadcast([128, NT, 16]), ALU.is_ge), reads=[G("thr"), G("gm")], writes=[G("gm")])
                S.add("dve", TT(selb_.rearrange("p t n -> p (t n)"), gm_.rearrange("p t n -> p (t n)"), ownm1, ALU.add),
                      reads=[G("gm"), "cst"], writes=[G("selb")])
            for hh in range(2):
                selb_ = selb2[hh]
                for t8 in range(4):
                    S.add("pe", TRS([(psb[3][64:80, i * 128:(i + 1) * 128], selb_[:, t8 * 8 + i, :], identb) for i in range(8)]),
                          reads=[A("g", hh, "selb"), "cb"], writes=[PS(3)])
                    S.add("act", ACP(qaug[hh][64:80, t8 * 1024:(t8 + 1) * 1024], psb[3][64:80, :]), writes=[PS(3), QA[hh] + ("m", t8)])
            steps = []
            for tb in range(16):
                q0 = tb * 256
                for hh in range(2):
                    xb = ((6, 7), (2, 3))[tb % 2][hh]
                    qa = qaug[hh][:, q0:q0 + 256]
                    for n in range(tb + 1):
                        sb_ = MBANKS[sidx % 3]
                        pt = Pt[sidx % 3]
                        ptk = A("Pt", sidx % 3)
                        sidx += 1
                        front, back, late1, late2 = [], [], [], []
                        k0 = kaug[hh][:, (2 * n) * 128:(2 * n + 1) * 128]
                        k1 = kaug[hh][:, (2 * n + 1) * 128:(2 * n + 2) * 128]
                        if n < tb:
                            items = [(ps[sb_][:, 0:256], k0, qa, True, True),
                                     (ps[sb_][:, 256:512], k1, qa, True, True)]
                            front.append(("pe", MM(items), [QA[hh], KA[hh]], [PS(sb_)]))
                            front.append(("act", ACT(pt[:, :], ps[sb_][:, :], AF.Exp, bias=negC), ["cv"], [PS(sb_), ptk]))
                            pv = [(0, 256, pt[:, 0:256], 2 * n), (0, 256, pt[:, 256:512], 2 * n + 1)]
                        else:
                            items = [(ps[sb_][:, 0:256], k0, qa, True, False),
                                     (ps[sb_][:, 0:128], identb, maskLE, False, True),
                                     (ps[sb_][:, 384:512], k1, qaug[hh][:, q0 + 128:q0 + 256], True, False),
                                     (ps[sb_][:, 384:512], identb, maskLE, False, True)]
                            front.append(("pe", MM(items), [QA[hh], KA[hh], "cb"], [PS(sb_)]))
                            front.append(("act", ACT(pt[:, 0:256], ps[sb_][:, 0:256], AF.Exp, bias=negC), ["cv"], [PS(sb_), ptk]))
                            front.append(("act", ACT(pt[:, 384:512], ps[sb_][:, 384:512], AF.Exp, bias=negC), ["cv"], [PS(sb_), ptk]))
                            pv = [(0, 256, pt[:, 0:256], 2 * n), (128, 256, pt[:, 384:512], 2 * n + 1)]
                        items = []
                        for pi, (c0_, c1_, rhs, kt) in enumerate(pv):
                            first = (n == 0 and pi == 0)
                            last = (n == tb and pi == 1)
                            items.append((ps[xb][:, c0_:c1_], vaug[:, kt, hh, :], rhs, first, last))
                        back.append(("pe", MM(items), [ptk, A("vaug")], [PS(xb)]))
                        if n == tb:
                            s_ = (tb * 2 + hh) % 4
                            nr = slice(64 * hh, 64 * hh + 64)
                            dr = slice(64 * (1 - hh), 64 * (1 - hh) + 64)
                            back.append(("dve", (lambda s_=s_, xb=xb, dr=dr: (lambda e: e.reciprocal(out=Rt[s_][dr, :], in_=ps[xb][dr, 0:256])))(),
                                         [], [PS(xb), A("Rt", s_)]))
                            back.append(("act", ACP(rs[s_][nr, :], ps[xb][nr, 0:256]), [], [PS(xb), A("rs", s_)]))
                            late1.append(("pe", MM([(ps[5][nr, 0:256], identf[dr, dr], Rt[s_][dr, :], True, True)]), [A("Rt", s_), "cst"], [("ps", 5, hh)]))
                            late2.append(("dve", TT(oo[s_][nr, :], ps[5][nr, 0:256], rs[s_][nr, :], ALU.mult), [A("rs", s_)], [("ps", 5, hh), A("oo", s_)]))
                            late2.append(("pool", TT(ost[nr, q0:q0 + 256], oo[s_][nr, :], sgT[nr, q0:q0 + 256], ALU.mult),
                                          [A("oo", s_), A("sgT")], [A("ost", tb, hh)]))
                        steps.append(dict(front=front, back=back, late1=late1, late2=late2))
            if hp + 1 < 8:
                moba_loads(hp + 1)
            emit_pipeline(steps, 2)
            S.add("sp", DMA(mixT1[hp, :, :], ost[:, :]), reads=[A("ost")], writes=[("mixT1", hp)], dma="ost")

    if stop_after != "l0":
        if on('moba'):
            phase_moba()
        if on('out1'):
            phase_outproj(mixT1, 8, o_wout, x1_d, out_d, None, 0, "out", [("mixT1",)], [("x1s",)])
        S.add("sp", None, reads=[("out",)])
    else:
        S.add("sp", None, reads=[("x1s",)])
    S.emit(st)
    K.n_ops = len(S.ops)
    K.n_sems = S.n_sems
    K.stack = st
    return nc, K


_CACHE = {}


def make_in_map(inputs, b, cst):
    g = lambda k: np.ascontiguousarray(np.asarray(inputs[k], dtype=np.float32)[0])
    m = {
        "x": np.ascontiguousarray(np.asarray(inputs["x"], dtype=np.float32)[b]),
        "even_norm_w": g("even_norm_w").reshape(1, DM),
        "even_w_in": g("even_w_in"),
        "even_conv_w": g("even_conv_w"),
        "even_conv_b": g("even_conv_b").reshape(2048, 1),
        "even_dt_bias": g("even_dt_bias").reshape(1, 16),
        "even_a_log": g("even_a_log").reshape(1, 16),
        "even_d_skip": g("even_d_skip").reshape(1, 16),
        "even_ssd_norm_w": g("even_ssd_norm_w").reshape(1, DM),
        "even_q_norm": g("even_q_norm").reshape(1, 64),
        "even_k_norm": g("even_k_norm").reshape(1, 64),
        "even_w_out": g("even_w_out"),
        "odd_norm_w": g("odd_norm_w").reshape(1, DM),
        "odd_w_in": g("odd_w_in"),
        "odd_q_norm": g("odd_q_norm").reshape(1, 64),
        "odd_k_norm": g("odd_k_norm").reshape(1, 64),
        "odd_w_out": g("odd_w_out"),
        "cst": cst,
        "koh": make_koh(),
    }
    return m


def kernel(**inputs):
    if "nc" not in _CACHE:
        _CACHE["nc"] = build_program()[0]
    nc = _CACHE["nc"]
    cst = make_consts()
    in_maps = [make_in_map(inputs, c % 4, cst) for c in range(8)]
    res = run_bass_kernel_spmd(nc, in_maps, core_ids=list(range(8)))
    out = np.stack([np.asarray(res.results[c]["out"], dtype=np.float32) for c in range(4)], axis=0)
    return out
```

```python
import contextlib
import numpy as np
import concourse.bass as bass
import concourse.mybir as mybir
from concourse.bass_utils import run_bass_kernel_spmd

F32 = mybir.dt.float32
BF16 = mybir.dt.bfloat16
AF = mybir.ActivationFunctionType
ALU = mybir.AluOpType
AX = mybir.AxisListType

SEQ = 4096
NT = 32
DM = 1024
EVEN_IN = 8208
NEG = -30000.0
EPS = 1e-6
ENGINES = ("pe", "act", "dve", "pool", "sp")
EPOCH = 30000


class Sched:
    def __init__(self, nc):
        self.nc = nc
        self.ops = []
        self.dummy = None

    @staticmethod
    def _norm(toks):
        return tuple((t,) if not isinstance(t, tuple) else t for t in toks)

    def add(self, eng, fn, reads=(), writes=(), dma=None):
        assert eng in ENGINES, eng
        self.ops.append(dict(eng=eng, fn=fn, reads=self._norm(reads), writes=self._norm(writes), dma=dma,
                             deps=set(), needs_inc=False))

    def claim(self, tok, eng="dve"):
        d = self.dummy
        self.add(eng, lambda e: e.memset(d[:, 0:1], 0.0), writes=[tok])

    def _analyze(self):
        state = {}
        kids = {}

        def related(tok):
            out = []
            for n in range(1, len(tok) + 1):
                p = tok[:n]
                if p in state:
                    out.append(p)
            for c in kids.get(tok, ()):
                if c in state:
                    out.append(c)
            return out

        def register(tok):
            if tok not in state:
                state[tok] = [None, []]
                for n in range(1, len(tok)):
                    kids.setdefault(tok[:n], set()).add(tok)

        for i, op in enumerate(self.ops):
            deps = set()
            for k in op["reads"]:
                for r in related(k):
                    if state[r][0] is not None:
                        deps.add(state[r][0])
            for k in op["writes"]:
                for r in related(k):
                    if state[r][0] is not None:
                        deps.add(state[r][0])
                    deps.update(state[r][1])
            deps.discard(i)
            pruned = set()
            for d in deps:
                p = self.ops[d]
                if p["dma"] is None and op["dma"] is None and p["eng"] == op["eng"] == "pe":
                    continue
                pruned.add(d)
            op["deps"] = pruned
            if op["dma"] is not None:
                op["needs_inc"] = True
            for d in pruned:
                self.ops[d]["needs_inc"] = True
            for k in op["reads"]:
                register(k)
                state[k][1].append(i)
            for k in op["writes"]:
                register(k)
                for c in list(kids.get(k, ())):
                    if c in state:
                        del state[c]
                state[k] = [i, []]

    def emit(self, stack):
        nc = self.nc
        self._analyze()
        counters = {}
        sig = {}
        semkeys = []
        for i, op in enumerate(self.ops):
            if not op["needs_inc"]:
                continue
            if op["dma"] is not None:
                key = ("dma", op["dma"])
                step = 16
            else:
                key = ("eng", op["eng"])
                step = 1
            c = counters.get(key, 0) + 1
            counters[key] = c
            sk = (key, c // EPOCH)
            ec = counters.get(("ec", sk), 0) + step
            counters[("ec", sk)] = ec
            sig[i] = (sk, ec)
            if sk not in semkeys:
                semkeys.append(sk)
        sems = {}
        for n, sk in enumerate(semkeys):
            sems[sk] = stack.enter_context(nc.semaphore("s%d" % n))
        self.n_sems = len(semkeys)
        per_eng = {e: [] for e in ENGINES}
        for i, op in enumerate(self.ops):
            per_eng[op["eng"]].append(i)
        ops = self.ops

        def run(e_name, eobj):
            waited = {}
            for i in per_eng[e_name]:
                op = ops[i]
                need = {}
                for d in op["deps"]:
                    sk, v = sig[d]
                    if waited.get(sk, 0) >= v:
                        continue
                    if need.get(sk, 0) < v:
                        need[sk] = v
                for sk, v in need.items():
                    eobj.wait_ge(sems[sk], v)
                    waited[sk] = v
                if op["fn"] is None:
                    continue
                ins = op["fn"](eobj)
                if op["needs_inc"]:
                    assert ins is not None
                    ins.then_inc(sems[sig[i][0]], 16 if op["dma"] is not None else 1)

        block = stack.enter_context(nc.Block())

        @block.tensor
        def _(e):
            run("pe", e)

        @block.scalar
        def _(e):
            run("act", e)

        @block.vector
        def _(e):
            run("dve", e)

        @block.gpsimd
        def _(e):
            run("pool", e)

        @block.sync
        def _(e):
            run("sp", e)


def ACT(out, in_, func, **kw):
    return lambda e: e.activation(out=out, in_=in_, func=func, **kw)


def TT(out, in0, in1, op):
    return lambda e: e.tensor_tensor(out=out, in0=in0, in1=in1, op=op)


def TS(out, in0, s1, s2, op0, op1=None):
    if op1 is None:
        return lambda e: e.tensor_scalar(out=out, in0=in0, scalar1=s1, scalar2=0.0, op0=op0, op1=ALU.add)
    return lambda e: e.tensor_scalar(out=out, in0=in0, scalar1=s1, scalar2=s2, op0=op0, op1=op1)


def STT(out, in0, scalar, in1, op0, op1):
    return lambda e: e.scalar_tensor_tensor(out=out, in0=in0, scalar=scalar, in1=in1, op0=op0, op1=op1)


def CP(out, in_):
    return lambda e: e.tensor_copy(out=out, in_=in_)


def ACP(out, in_):
    return lambda e: e.copy(out=out, in_=in_)


def DMA(out, in_):
    return lambda e: e.dma_start(out=out, in_=in_)


def MM(items):
    def fn(e):
        ins = None
        for (o, l, r, s0, s1) in items:
            ins = e.matmul(o, lhsT=l, rhs=r, start=s0, stop=s1)
        return ins
    return fn


def TRS(items):
    def fn(e):
        ins = None
        for (o, i_, idn) in items:
            ins = e.transpose(out=o, in_=i_, identity=idn)
        return ins
    return fn


CST_W = 2304


def make_consts():
    c = np.zeros((128, CST_W), np.float32)
    p = np.arange(128)[:, None]
    f = np.arange(128)[None, :]
    c[:, 0:128] = (p == f)
    c[:, 128:256] = (p <= f)
    c[:, 256:384] = 1.0
    c[127, 384:512] = 1.0
    c[:, 512:640] = np.where(p <= f, 0.0, NEG)
    c[:, 640:768] = np.where(p >= f, 0.0, NEG)
    inv = 500000.0 ** (-np.arange(8, dtype=np.float32) * 2.0 / 16.0)
    pos = (np.arange(NT)[None, :, None] * 128 + np.arange(128)[:, None, None]).astype(np.float32)
    ang = pos * inv[None, None, :].astype(np.float32)
    c[:, 768:1024] = np.cos(ang).reshape(128, 256)
    c[:, 1024:1280] = np.sin(ang).reshape(128, 256)
    t = np.arange(NT)[:, None]
    n = np.arange(16)[None, :]
    vb = np.where(n < (t // 2), 0.0, -1e30).astype(np.float32).reshape(1, 512)
    c[:, 1280:1792] = vb
    c[:, 1792:2304] = np.where(n == (t // 2), 0.0, -1.0).astype(np.float32).reshape(1, 512)
    return c


def make_koh():
    k = np.arange(SEQ)[None, :] // 256
    n = np.arange(16)[:, None]
    return np.where(k == n, 30000.0, 0.0).astype(np.float32)


class Prog:
    pass


def build_program(debug=False, stop_after=None, only=None):
    on = lambda p: (only is None) or (p in only)
    nc = bass.Bass("TRN2", target_bir_lowering=False)
    K = Prog()
    din = lambda name, shape: nc.dram_tensor(name, shape, F32, kind="ExternalInput").ap()
    x_d = din("x", [SEQ, DM])
    e_normw = din("even_norm_w", [1, DM])
    e_win = din("even_w_in", [DM, EVEN_IN])
    e_convw = din("even_conv_w", [2048, 4])
    e_convb = din("even_conv_b", [2048, 1])
    e_dtb = din("even_dt_bias", [1, 16])
    e_alog = din("even_a_log", [1, 16])
    e_dskip = din("even_d_skip", [1, 16])
    e_ssdnw = din("even_ssd_norm_w", [1, DM])
    e_qn = din("even_q_norm", [1, 64])
    e_kn = din("even_k_norm", [1, 64])
    e_wout = din("even_w_out", [1536, DM])
    o_normw = din("odd_norm_w", [1, DM])
    o_win = din("odd_w_in", [DM, 4096])
    o_qn = din("odd_q_norm", [1, 64])
    o_kn = din("odd_k_norm", [1, 64])
    o_wout = din("odd_w_out", [DM, DM])
    cst_d = din("cst", [128, CST_W])
    koh_d = din("koh", [16, SEQ])
    out_d = nc.dram_tensor("out", [SEQ, DM], F32, kind="ExternalOutput").ap()
    skind = "ExternalOutput" if debug else "Internal"
    mixT0 = nc.dram_tensor("mixT0", [12, 128, SEQ], BF16, kind=skind).ap()
    x1_d = nc.dram_tensor("x1s", [SEQ, DM], F32, kind=skind).ap()
    mixT1 = nc.dram_tensor("mixT1", [8, 128, SEQ], BF16, kind=skind).ap()
    acs_scr = nc.dram_tensor("acs_scr", [64, 16, 256], F32, kind="Internal").ap()

    st = contextlib.ExitStack()
    sb = lambda name, shape, dt: st.enter_context(nc.sbuf_tensor("k_" + name, shape, dt))
    cst = sb("cst_sb", [128, CST_W], F32)
    cb = sb("cstb", [128, 640], BF16)
    cv = sb("cv", [128, 8], F32)
    hT = sb("hT", [128, 8, SEQ], BF16)
    stats = sb("stats", [128, 640], F32)
    dummy = sb("dummy", [128, 4], F32)
    ARN = 65536
    AR = sb("arena", [128, ARN], BF16)
    ps = [st.enter_context(nc.psum_tensor("ps%d" % i, [128, 512], F32)) for i in range(8)]
    psb = [p[:].bitcast(BF16) for p in ps]

    S = Sched(nc)
    S.dummy = dummy
    identf = cst[:, 0:128]
    triU = cst[:, 128:256]
    onesf = cst[:, 256:384]
    e127 = cst[:, 384:512]
    cosT = cst[:, 768:1024].rearrange("p (t i) -> p t i", i=8)
    sinT = cst[:, 1024:1280].rearrange("p (t i) -> p t i", i=8)
    validb = cst[:, 1280:1792]
    ownm1 = cst[:, 1792:2304]
    identb = cb[:, 0:128]
    maskLE = cb[:, 128:256]
    maskGE = cb[:, 256:384]
    onesb = cb[:, 384:512]
    id30k = cb[:, 512:640]
    maskPO = cb[:, 128:384]
    PS = lambda b: ("ps", b)

    def arv(off_bytes, shape, dt):
        n = int(np.prod(shape[1:]))
        esz = 2 if dt == BF16 else 4
        a = AR[:, off_bytes // 2: off_bytes // 2 + n * esz // 2]
        if dt != BF16:
            a = a.bitcast(dt)
        if len(shape) == 3:
            a = a.rearrange("p (a b) -> p a b", b=shape[2])
        elif len(shape) == 4:
            a = a.rearrange("p (a b c) -> p a b c", b=shape[2], c=shape[3])
        return a

    S.add("sp", DMA(cst[:], cst_d[:, :]), writes=["cst"], dma="cst")
    S.add("dve", CP(cb[:, 0:128], cst[:, 0:128]), reads=["cst"], writes=["cb"])
    S.add("dve", CP(cb[:, 128:256], cst[:, 512:640]), reads=["cst"], writes=["cb"])
    S.add("dve", CP(cb[:, 256:384], cst[:, 640:768]), reads=["cst"], writes=["cb"])
    S.add("dve", CP(cb[:, 384:512], cst[:, 256:384]), reads=["cst"], writes=["cb"])
    S.add("dve", TS(cb[:, 512:640], cst[:, 0:128], 30000.0, None, ALU.mult), reads=["cst"], writes=["cb"])
    S.add("dve", lambda e: e.memset(cv[:, 0:1], EPS), writes=["cv"])
    S.add("dve", lambda e: e.memset(stats[:], 0.0), writes=["stats"])

    def qk_shift(qn_d, kn_d, col, nwqk, scr, scr_tok):
        S.add("sp", DMA(nwqk[:, 0, :], qn_d[0:1, :].partition_broadcast(128)), writes=[("AR", "nwqk")], dma="nwq0")
        S.add("sp", DMA(nwqk[:, 2, :], kn_d[0:1, :].partition_broadcast(128)), writes=[("AR", "nwqk")], dma="nwq1")
        tmp = stats[:, 480:484]
        S.add("act", ACT(scr[:, 0:64], nwqk[:, 0, :], AF.Abs), reads=[("AR", "nwqk")], writes=[scr_tok])
        S.add("act", ACT(scr[:, 64:128], nwqk[:, 2, :], AF.Abs), reads=[("AR", "nwqk")], writes=[scr_tok])
        S.add("dve", lambda e: e.tensor_reduce(out=tmp[:, 0:1], in_=scr[:, 0:64], axis=AX.X, op=ALU.max),
              reads=[scr_tok], writes=[("stats", "c")])
        S.add("dve", lambda e: e.tensor_reduce(out=tmp[:, 1:2], in_=scr[:, 64:128], axis=AX.X, op=ALU.max),
              reads=[scr_tok], writes=[("stats", "c")])
        S.add("dve", TT(tmp[:, 2:3], tmp[:, 0:1], tmp[:, 1:2], ALU.mult), reads=[("stats", "c")], writes=[("stats", "c")])
        S.add("dve", TS(cv[:, col:col + 1], tmp[:, 2:3], -8.0, None, ALU.mult), reads=[("stats", "c")], writes=["cv"])
        S.add("dve", TS(nwqk[:, 0, :], nwqk[:, 0, :], 0.125, None, ALU.mult), reads=[("AR", "nwqk")], writes=[("AR", "nwqk")])
        S.add("dve", CP(nwqk[:, 1, :], nwqk[:, 0, :]), reads=[("AR", "nwqk")], writes=[("AR", "nwqk")])
        S.add("dve", CP(nwqk[:, 3, :], nwqk[:, 2, :]), reads=[("AR", "nwqk")], writes=[("AR", "nwqk")])

    def norm_transpose_tile(xt, xt_tok, t, nw, nw_tok, hb, hb_tok, junk, scol, trbank, part="AB"):
        if "B" in part and "A" not in part:
            pb = psb[trbank]
            S.add("pe", TRS([(pb[:, kc * 128:(kc + 1) * 128], hb[:, kc * 128:(kc + 1) * 128], identb) for kc in range(8)]),
                  reads=[hb_tok, "cb"], writes=[PS(trbank)])
            S.add("act", ACP(hT[:, :, t * 128:(t + 1) * 128], pb[:, :].rearrange("p (k c) -> p k c", c=128)),
                  writes=[PS(trbank), ("hT", t)])
            return
        ss = stats[:, scol + t: scol + t + 1]
        ln = stats[:, scol + 32 + t: scol + 33 + t]
        rs = stats[:, scol + 64 + t: scol + 65 + t]
        stok = ("stats", scol + t)
        S.add("act", ACT(junk, xt, AF.Square, accum_out=ss), reads=[xt_tok], writes=[("AR", "junk"), stok])
        S.add("act", ACT(ln, ss, AF.Ln, scale=1.0 / DM, bias=cv[:, 0:1]), reads=[stok, "cv"], writes=[stok])
        S.add("act", ACT(rs, ln, AF.Exp, scale=-0.5), reads=[stok], writes=[stok])
        S.add("dve", STT(hb, xt, rs, nw, ALU.mult, ALU.mult), reads=[xt_tok, stok, nw_tok], writes=[hb_tok])
        if part == "A":
            return
        pb = psb[trbank]
        S.add("pe", TRS([(pb[:, kc * 128:(kc + 1) * 128], hb[:, kc * 128:(kc + 1) * 128], identb) for kc in range(8)]),
              reads=[hb_tok, "cb"], writes=[PS(trbank)])
        S.add("act", ACP(hT[:, :, t * 128:(t + 1) * 128], pb[:, :].rearrange("p (k c) -> p k c", c=128)),
              writes=[PS(trbank), ("hT", t)])

    def phase_norm(src_d, normw_d, scol):
        S.claim(("AR",))
        xin = [arv(i * 4096, [128, DM], F32) for i in range(3)]
        hb = [arv(12288 + i * 2048, [128, DM], BF16) for i in range(3)]
        nw = arv(18432, [128, DM], F32)
        junk = arv(22528, [128, DM], BF16)
        S.add("sp", DMA(nw, normw_d[0:1, :].partition_broadcast(128)), writes=[("AR", "nw")], dma="nw")
        for t in range(NT + 1):
            if t < NT:
                s = t % 3
                S.add("sp", DMA(xin[s], src_d[t * 128:(t + 1) * 128, :]), writes=[("AR", "xin", s)], dma=("xin", s))
                norm_transpose_tile(xin[s], ("AR", "xin", s), t, nw, ("AR", "nw"), hb[s], ("AR", "hb", s), junk, 0 + scol, 2 + t % 2, part="A")
            if t >= 1:
                p = t - 1
                norm_transpose_tile(None, None, p, None, None, hb[p % 3], ("AR", "hb", p % 3), None, 0 + scol, 2 + p % 2, part="B")

    def load_w(dst, wv, c0, n, tok, chan):
        S.add("pool", DMA(dst, wv[:, :, c0:c0 + n]), writes=[tok], dma=chan)

    def proj_tm(bank, cols, t, w, n, reads, toks=None):
        items = [(ps[bank][:, cols:cols + n], hT[:, kc, t * 128:(t + 1) * 128], w[:, kc, 0:n], kc == 0, kc == 7) for kc in range(8)]
        S.add("pe", MM(items), reads=[("hT",)] + list(reads), writes=[PS(bank)])

    e_winv = e_win.rearrange("(kc p) f -> p kc f", p=128)
    if on('norm'):
        phase_norm(x_d, e_normw, 0)

    def phase_ssd():
        S.claim(("AR",))
        A_ = lambda *k: ("AR",) + k
        o = 0

        def alloc(nbytes):
            nonlocal o
            r = o
            o += (nbytes + 63) // 64 * 64
            assert o <= ARN * 2, o
            return r
        wconv = arv(alloc(8 * 128 * 2), [128, 8, 128], BF16)
        wz_off = alloc(8 * 256 * 2)
        wz = arv(wz_off, [128, 8, 256], BF16)
        wdt = arv(alloc(8 * 16 * 2), [128, 8, 16], BF16)
        cin_off = alloc((SEQ + 4) * 4)
        cin = arv(cin_off, [128, SEQ + 4], F32)
        zs_all = arv(cin_off, [128, NT, 256], BF16)
        cacc_off = alloc(1024 * 4)
        cacc = arv(cacc_off, [128, 1024], F32)
        cacc2 = [arv(cacc_off + j * 2048, [128, 512], F32) for j in range(2)]
        cwb = arv(alloc(8 * 4), [128, 8], F32)
        ch_off = alloc(4 * SEQ * 2)
        chT = [arv(ch_off + i * SEQ * 2, [128, SEQ], BF16) for i in range(4)]
        mixst = arv(ch_off, [128, 2, SEQ], BF16)
        xs_tok = arv(alloc(NT * 256 * 2), [128, NT, 256], BF16)
        B_tok = arv(alloc(NT * 128 * 2), [128, NT, 128], BF16)
        xdt_c = [arv(alloc(512 * 2), [128, 2, 256], BF16) for _ in range(2)]
        xdtw_c = [arv(alloc(512 * 2), [128, 2, 256], BF16) for _ in range(2)]
        dtt = arv(alloc(512 * 4), [128, NT, 16], F32)
        acs = arv(alloc(512 * 4), [128, NT, 16], F32)
        wst = arv(alloc(512 * 4), [128, NT, 16], F32)
        tmp5 = arv(alloc(512 * 4), [128, NT, 16], F32)
        tmp6 = arv(alloc(512 * 4), [128, NT, 16], F32)
        wend = tmp6
        dec = arv(alloc(256 * 4), [128, 16, 16], F32)
        aend = arv(alloc(256 * 4), [128, 16, 16], F32)
        dtg = arv(alloc(128 * 4), [128, 128], F32)
        dtwg = arv(alloc(128 * 4), [128, 128], F32)
        wsg = arv(alloc(128 * 4), [128, 128], F32)
        bc16 = arv(alloc(64 * 4), [128, 4, 16], F32)
        acsT = [arv(alloc(256 * 4), [128, 256], F32) for _ in range(2)]
        Sst = arv(alloc(256 * 4), [128, 256], F32)
        Sbf = [arv(alloc(256 * 2), [128, 256], BF16) for _ in range(2)]
        Dt = [arv(alloc(384 * 2), [128, 384], BF16) for _ in range(2)]
        Ag = [arv(alloc(384 * 4), [128, 384], F32) for _ in range(2)]
        CBm = [arv(alloc(384 * 4), [128, 384], F32)] * 2
        m384 = arv(alloc(384 * 2), [128, 384], BF16)
        Ag1 = arv(alloc(384 * 4), [128, 384], F32)
        bcb = [[arv(off + h * 1024, [128, 256], F32) for h in range(4)] for off in (cacc_off, wz_off)]
        bcb_tok = [[A_("cacc", h) for h in range(4)], [A_("wz", h) for h in range(4)]]
        Mt = [arv(alloc(384 * 2), [128, 384], BF16) for _ in range(4)]
        E1 = arv(alloc(512 * 4), [128, 2, 256], F32)
        E2 = [arv(alloc(512 * 4), [128, 2, 256], F32) for _ in range(2)]
        E3 = [arv(alloc(512 * 4), [128, 2, 256], F32)] * 2
        yb = [arv(alloc(512 * 2), [128, 2, 256], BF16) for _ in range(2)]
        junk = arv(alloc(512 * 2), [128, 512], BF16)
        nwg = arv(alloc(256 * 4), [128, 256], F32)

        A = lambda *k: ("AR",) + k
        S.add("dve", CP(m384[:, 0:128], triU), reads=["cst"], writes=[A("m384")])
        S.add("dve", CP(m384[:, 128:256], onesf), reads=["cst"], writes=[A("m384")])
        S.add("dve", CP(m384[:, 256:384], triU), reads=["cst"], writes=[A("m384")])
        S.add("sp", DMA(bc16[:, 0, :], e_dtb[0:1, :].partition_broadcast(128)), writes=[A("bc16")], dma="bc0")
        S.add("sp", DMA(bc16[:, 1, :], e_alog[0:1, :].partition_broadcast(128)), writes=[A("bc16")], dma="bc1")
        S.add("sp", DMA(bc16[:, 2, :], e_dskip[0:1, :].partition_broadcast(128)), writes=[A("bc16")], dma="bc2")
        S.add("act", ACT(bc16[:, 1, :], bc16[:, 1, :], AF.Exp), reads=[A("bc16")], writes=[A("bc16")])
        S.add("dve", TS(bc16[:, 1, :], bc16[:, 1, :], -1.0, None, ALU.mult), reads=[A("bc16")], writes=[A("bc16")])
        load_w(wdt, e_winv, 3072, 16, A("wdt"), "wdt")
        for t in range(NT):
            proj_tm(0, t * 16, t, wdt, 16, [A("wdt")])
        dflat = lambda a: a.rearrange("p t h -> p (t h)")
        S.add("dve", TT(dtt, ps[0][:, :].rearrange("p (t h) -> p t h", h=16), bc16[:, 0:1, :].to_broadcast([128, NT, 16]), ALU.add),
              reads=[A("bc16")], writes=[PS(0), A("dtt")])
        S.add("act", ACT(dflat(tmp5), dflat(dtt), AF.Abs), reads=[A("dtt")], writes=[A("tmp5")])
        S.add("act", ACT(dflat(tmp5), dflat(tmp5), AF.Exp, scale=-1.0), reads=[A("tmp5")], writes=[A("tmp5")])
        S.add("act", ACT(dflat(tmp5), dflat(tmp5), AF.Ln, bias=1.0), reads=[A("tmp5")], writes=[A("tmp5")])
        S.add("dve", TS(dflat(tmp6), dflat(dtt), 0.0, None, ALU.max), reads=[A("dtt")], writes=[A("tmp6")])
        S.add("dve", TT(dflat(dtt), dflat(tmp6), dflat(tmp5), ALU.add), reads=[A("tmp5"), A("tmp6")], writes=[A("dtt")])
        S.add("dve", TT(tmp6, dtt, bc16[:, 1:2, :].to_broadcast([128, NT, 16]), ALU.mult), reads=[A("dtt"), A("bc16")], writes=[A("tmp6")])
        a4 = tmp6.rearrange("p (c j) h -> p c j h", j=2)
        acs4 = acs.rearrange("p (c j) h -> p c j h", j=2)
        S.add("pe", MM([(ps[1][:, 0:256], triU, a4[:, :, 0, :], True, True),
                        (ps[1][:, 256:512], onesf, a4[:, :, 0, :], True, False),
                        (ps[1][:, 256:512], triU, a4[:, :, 1, :], False, True)]), reads=[A("tmp6"), "cst"], writes=[PS(1)])
        S.add("dve", CP(acs4[:, :, 0, :], ps[1][:, 0:256].rearrange("p (c h) -> p c h", h=16)), writes=[PS(1), A("acs")])
        S.add("dve", CP(acs4[:, :, 1, :], ps[1][:, 256:512].rearrange("p (c h) -> p c h", h=16)), writes=[PS(1), A("acs")])
        S.add("pe", MM([(ps[0][:, 0:256], e127, acs4[:, :, 1, :], True, True)]), reads=[A("acs"), "cst"], writes=[PS(0)])
        S.add("act", ACT(dec.rearrange("p c h -> p (c h)"), ps[0][:, 0:256], AF.Exp), writes=[PS(0), A("dec")])
        S.add("dve", CP(aend.rearrange("p c h -> p (c h)"), ps[0][:, 0:256]), writes=[PS(0), A("aend")])
        for j in range(2):
            S.add("dve", TT(wend.rearrange("p (c j) h -> p c j h", j=2)[:, :, j, :], acs4[:, :, j, :], aend, ALU.subtract),
                  reads=[A("acs"), A("aend")], writes=[A("tmp6")])
        S.add("act", ACT(dflat(wend), dflat(wend), AF.Exp, scale=-1.0), reads=[A("tmp6")], writes=[A("tmp6")])
        S.add("act", ACT(dflat(wst), dflat(acs), AF.Exp), reads=[A("acs")], writes=[A("wst")])
        S.add("dve", TT(dflat(tmp5), dflat(dtt), dflat(wend), ALU.mult), reads=[A("dtt"), A("tmp6")], writes=[A("tmp5")])
        S.add("dve", TS(dflat(acs), dflat(acs), -1.0, None, ALU.mult), reads=[A("acs")], writes=[A("acs")])

        for g in range(4):
            chans = [1024 + g * 256, 1024 + g * 256 + 128, 2048 + g * 128, 2560 + g * 128]
            S.claim(A("cacc"))
            S.add("dve", lambda e: e.memset(cin[:, 0:4], 0.0), writes=[A("cin")])
            for ci, c0 in enumerate(chans):
                load_w(wconv, e_winv, c0, 128, A("wconv"), "wconv")
                S.add("sp", DMA(cwb[:, 0:4], e_convw[c0 - 1024:c0 - 1024 + 128, :]), writes=[A("cwb")], dma="cw")
                S.add("sp", DMA(cwb[:, 4:5], e_convb[c0 - 1024:c0 - 1024 + 128, :]), writes=[A("cwb")], dma="cb")
                def silu_step(i):
                    j = i % 2
                    S.add("act", ACT(chT[ci][:, i * 512:(i + 1) * 512], cacc2[j], AF.Silu), reads=[A("cacc", "h", j)], writes=[A("chT", ci, i // 2)])
                for i in range(8):
                    bk = i % 2
                    j = i % 2
                    items = [(ps[bk][:, 0:512], wconv[:, kc, :], hT[:, kc, i * 512:(i + 1) * 512], kc == 0, kc == 7) for kc in range(8)]
                    S.add("pe", MM(items), reads=[("hT",), A("wconv")], writes=[PS(bk)])
                    S.add("act", ACP(cin[:, 4 + i * 512: 4 + (i + 1) * 512], ps[bk][:, 0:512]), writes=[PS(bk), A("cin", i)])
                    q0 = i * 512
                    rd = [A("cin", i), A("cwb")] + ([A("cin", i - 1)] if i else [A("cin")])
                    S.add("act", ACT(cacc2[j], cin[:, 4 + q0: 4 + q0 + 512], AF.Identity, scale=cwb[:, 3:4], bias=cwb[:, 4:5]),
                          reads=rd, writes=[A("cacc", "h", j)])
                    for kk in (2, 1, 0):
                        sh = 3 - kk
                        S.add("dve", STT(cacc2[j], cin[:, 4 + q0 - sh: 4 + q0 - sh + 512], cwb[:, kk:kk + 1], cacc2[j], ALU.mult, ALU.add),
                              reads=rd + [A("cacc", "h", j)], writes=[A("cacc", "h", j)])
                    if i > 0:
                        silu_step(i - 1)
                silu_step(7)
            S.claim(A("cacc"))
            for t in range(NT):
                bk = 2 + t % 2
                S.add("pe", TRS([(psb[bk][:, j * 128:(j + 1) * 128], chT[j][:, t * 128:(t + 1) * 128], identb) for j in range(3)]),
                      reads=[A("chT", 0), A("chT", 1), A("chT", 2), "cb"], writes=[PS(bk)])
                S.add("act", ACP(xs_tok[:, t, :], psb[bk][:, 0:256]), writes=[PS(bk), A("xs_tok", t)])
                S.add("dve", CP(B_tok[:, t, :], psb[bk][:, 256:384]), writes=[PS(bk), A("B_tok", t)])
            S.add("dve", CP(dtg.rearrange("p (t h) -> p t h", h=4), dtt[:, :, 4 * g:4 * g + 4]), reads=[A("dtt")], writes=[A("dtg")])
            S.add("dve", CP(dtwg.rearrange("p (t h) -> p t h", h=4), tmp5[:, :, 4 * g:4 * g + 4]), reads=[A("tmp5")], writes=[A("dtwg")])
            S.add("dve", CP(wsg.rearrange("p (t h) -> p t h", h=4), wst[:, :, 4 * g:4 * g + 4]), reads=[A("wst")], writes=[A("wsg")])
            load_w(wz, e_winv, g * 256, 256, A("wz"), "wz")
            S.claim(A("cin"))
            for t in range(NT):
                proj_tm(t % 2, 0, t, wz, 256, [A("wz")])
                S.add("act", ACT(zs_all[:, t, :], ps[t % 2][:, 0:256], AF.Silu), writes=[PS(t % 2), A("cin", "zs", t)])
            S.add("sp", DMA(nwg, e_ssdnw[0:1, g * 256:(g + 1) * 256].partition_broadcast(128)), writes=[A("nwg")], dma="nwg")
            S.add("dve", lambda e: e.memset(Sst[:], 0.0), writes=[A("Sst")])
            BT, CT = chT[2], chT[3]
            DB = (5, 0)

            def prep(c):
                t0 = 2 * c
                ops = []
                xs3 = xs_tok[:, t0:t0 + 2, :].rearrange("p t (h e) -> p (t h) e", e=64)
                ops.append(("dve", TT(xdt_c[c % 2].rearrange("p t (h e) -> p (t h) e", e=64), xs3,
                                      dtg[:, t0 * 4:t0 * 4 + 8].unsqueeze(2).to_broadcast([128, 8, 64]), ALU.mult),
                            [A("xs_tok", t0), A("xs_tok", t0 + 1), A("dtg")], [A("xdt", c % 2)]))
                ops.append(("pool", TT(xdtw_c[c % 2].rearrange("p t (h e) -> p (t h) e", e=64), xs3,
                                       dtwg[:, t0 * 4:t0 * 4 + 8].unsqueeze(2).to_broadcast([128, 8, 64]), ALU.mult),
                            [A("xs_tok", t0), A("xs_tok", t0 + 1), A("dtwg")], [A("xdtw", c % 2)]))
                return ops

            def prep_acs(c):
                t0 = 2 * c
                aT = acsT[c % 2]
                ops = [("pe", TRS([(ps[3][0:16, 0:128], acs[:, t0, :], identf), (ps[3][0:16, 128:256], acs[:, t0 + 1, :], identf)]),
                        [A("acs"), "cst"], [PS(3)]),
                       ("dve", CP(aT[0:16, :], ps[3][0:16, 0:256]), [], [PS(3), A("acsT", c % 2)]),
                       ("sp", DMA(acs_scr[g * 16 + c, :, :], aT[0:16, :]), [A("acsT", c % 2)], [("acsscr", g * 16 + c)], ("acss", c % 2))]
                for h in range(4):
                    hh = 4 * g + h
                    ops.append(("sp", DMA(bcb[c % 2][h], acs_scr[g * 16 + c, hh:hh + 1, :].partition_broadcast(128)),
                                [("acsscr", g * 16 + c)], [bcb_tok[c % 2][h]], ("acsb", c % 2, h)))
                return ops

            def emit(ops):
                for op in ops:
                    S.add(*op)

            def deferred(c):
                t0 = 2 * c
                e2 = E2[c % 2]
                f2 = lambda a: a.rearrange("p t c -> p (t c)")
                sc0 = 128 + (g * NT + t0)
                st1 = [("pool", TT(E3[0].rearrange("p t (h e) -> p t h e", e=64), xs_tok[:, t0:t0 + 2, :].rearrange("p t (h e) -> p t h e", e=64),
                                  bc16[:, 2, 4 * g:4 * g + 4].unsqueeze(1).unsqueeze(3).to_broadcast([128, 2, 4, 64]), ALU.mult),
                        [A("xs_tok", t0), A("xs_tok", t0 + 1), A("bc16")], [A("E3")]),
                       ("dve", TT(f2(e2), f2(e2), f2(E3[0]), ALU.add), [A("E3"), A("E2", c % 2)], [A("E2", c % 2)]),
                       ("dve", TT(f2(e2), f2(e2), f2(zs_all[:, t0:t0 + 2, :]), ALU.mult), [A("cin", "zs", t0), A("cin", "zs", t0 + 1), A("E2", c % 2)], [A("E2", c % 2)])]
                for j in range(2):
                    st1.append(("act", ACT(junk[:, j * 256:(j + 1) * 256], e2[:, j, :], AF.Square, accum_out=stats[:, sc0 + j:sc0 + j + 1]),
                                [A("E2", c % 2)], [A("junk", j), ("stats", sc0 + j)]))
                ssv = stats[:, sc0:sc0 + 2]
                rsv = stats[:, sc0 + 128:sc0 + 130]
                st2 = [("act", ACT(rsv, ssv, AF.Ln, scale=1.0 / 256, bias=cv[:, 0:1]), [("stats", sc0), ("stats", sc0 + 1), "cv"], [("stats", "r", sc0)]),
                       ("act", ACT(rsv, rsv, AF.Exp, scale=-0.5), [("stats", "r", sc0)], [("stats", "r", sc0)])]
                st3 = [("dve", STT(yb[c % 2][:, j, :], e2[:, j, :], stats[:, sc0 + 128 + j:sc0 + 129 + j], nwg, ALU.mult, ALU.mult),
                        [A("E2", c % 2), ("stats", "r", sc0), A("nwg")], [A("yb", c % 2, j)]) for j in range(2)]
                st4 = [("pe", TRS([(psb[2][:, (j * 2 + i) * 128:(j * 2 + i + 1) * 128], yb[c % 2][:, j, i * 128:(i + 1) * 128], identb)
                                   for j in range(2) for i in range(2)]), [A("yb", c % 2), "cb"], [PS(2)]),
                       ("act", ACP(mixst[:, :, t0 * 128:(t0 + 2) * 128].rearrange("p i (j c) -> p i j c", c=128),
                                   psb[2][:, 0:512].rearrange("p (j i c) -> p i j c", i=2, c=128)),
                        [], [PS(2), A("chT", 0, t0 // 8), A("chT", 1, t0 // 8)])]
                return [st1, st2, st3, st4]

            def head(c):
                t0 = 2 * c
                s0 = c * 256
                S.add("pe", MM([(ps[4][:, 0:256], BT[:, s0:s0 + 128], CT[:, s0:s0 + 256], True, True),
                                (ps[4][:, 256:384], BT[:, s0 + 128:s0 + 256], CT[:, s0 + 128:s0 + 256], True, True)]),
                      reads=[A("chT", 2), A("chT", 3)], writes=[PS(4)])
                S.add("dve", TT(CBm[0][:, :], ps[4][:, 0:384], m384[:, :], ALU.mult), reads=[A("m384")], writes=[PS(4), A("CBm")])
                S.add("pe", MM([(ps[1][:, 0:256], B_tok[:, t0, :], xdtw_c[c % 2][:, 0, :], True, False),
                                (ps[1][:, 0:256], B_tok[:, t0 + 1, :], xdtw_c[c % 2][:, 1, :], False, True)]),
                      reads=[A("B_tok"), A("xdtw", c % 2)], writes=[PS(1)])
                fa(c, 0)
                fa(c, 1)

            def fa(c, h):
                t0 = 2 * c
                hh = 4 * g + h
                d_ = Dt[h % 2]
                bc = bcb[c % 2][h]
                S.add("act", ACT(Ag1[:, 0:256], bc[:, 0:256], AF.Abs, scale=-1.0, bias=acs[:, t0, hh:hh + 1]),
                      reads=[bcb_tok[c % 2][h], A("acs")], writes=[A("Ag1")])
                S.add("act", ACT(Ag1[:, 256:384], bc[:, 128:256], AF.Abs, scale=-1.0, bias=acs[:, t0 + 1, hh:hh + 1]),
                      reads=[bcb_tok[c % 2][h], A("acs")], writes=[A("Ag1")])
                S.add("act", ACT(d_[:, :], Ag1[:, :], AF.Exp, scale=-1.0), reads=[A("Ag1")], writes=[A("Dt", h % 2)])

            def fb(c, h):
                S.add("dve", STT(Mt[h][:, :], Dt[h % 2][:, :], 1.0, CBm[0][:, :], ALU.min, ALU.mult), reads=[A("Dt", h % 2), A("CBm")], writes=[A("Mt", h)])

            def back(c, h):
                m_ = Mt[h]
                xdt = xdt_c[c % 2]
                hc = slice(h * 64, (h + 1) * 64)
                S.add("pe", MM([(ps[6][:, h * 64:(h + 1) * 64], m_[:, 0:128], xdt[:, 0, hc], True, True),
                                (ps[6][:, 256 + h * 64:256 + (h + 1) * 64], m_[:, 128:256], xdt[:, 0, hc], True, False),
                                (ps[6][:, 256 + h * 64:256 + (h + 1) * 64], m_[:, 256:384], xdt[:, 1, hc], False, True)]),
                      reads=[A("Mt", h), A("xdt", c % 2)], writes=[PS(6)])

            def state_update(c):
                S.add("dve", TT(Sst.rearrange("p (h e) -> p h e", e=64), Sst.rearrange("p (h e) -> p h e", e=64),
                                dec[:, c, 4 * g:4 * g + 4].unsqueeze(2).to_broadcast([128, 4, 64]), ALU.mult),
                      reads=[A("Sst"), A("dec")], writes=[A("Sst")])
                S.add("dve", TT(Sst, Sst, ps[1][:, 0:256], ALU.add), reads=[A("Sst")], writes=[PS(1), A("Sst")])
                S.add("dve", CP(Sbf[(c + 1) % 2], Sst), reads=[A("Sst")], writes=[A("Sbf", (c + 1) % 2)])

            emit(prep(0))
            emit(prep_acs(0))
            head(0)
            pend = None
            for c in range(16):
                t0, t1 = 2 * c, 2 * c + 1
                s0 = c * 256
                fb(c, 0)
                if c + 1 < 16:
                    emit(prep_acs(c + 1))
                if pend:
                    emit(pend[0])
                fa(c, 2)
                fb(c, 1)
                state_update(c)
                back(c, 0)
                fa(c, 3)
                fb(c, 2)
                if pend:
                    emit(pend[1])
                if c + 1 < 16:
                    emit(prep(c + 1))
                back(c, 1)
                fb(c, 3)
                if pend:
                    emit(pend[2])
                back(c, 2)
                if pend:
                    emit(pend[3])
                back(c, 3)
                e2 = E2[c % 2]
                f2 = lambda a: a.rearrange("p t c -> p (t c)")
                if c > 0:
                    S.add("pe", MM([(ps[7][:, 0:256], CT[:, s0:s0 + 128], Sbf[c % 2], True, True),
                                    (ps[7][:, 256:512], CT[:, s0 + 128:s0 + 256], Sbf[c % 2], True, True)]),
                          reads=[A("chT", 3), A("Sbf", c % 2)], writes=[PS(7)])
                if c + 1 < 16:
                    head(c + 1)
                if c > 0:
                    S.add("dve", TT(E1.rearrange("p t (h e) -> p (t h) e", e=64), ps[7][:, :].rearrange("p (t h e) -> p (t h) e", h=4, e=64),
                                    wsg[:, t0 * 4:t0 * 4 + 8].unsqueeze(2).to_broadcast([128, 8, 64]), ALU.mult),
                          reads=[A("wsg")], writes=[PS(7), A("E1")])
                    S.add("dve", TT(f2(e2), ps[6][:, :], f2(E1), ALU.add), reads=[A("E1")], writes=[PS(6), A("E2", c % 2)])
                else:
                    S.add("dve", CP(f2(e2), ps[6][:, :]), writes=[PS(6), A("E2", c % 2)])
                pend = deferred(c)
            for stg in pend:
                emit(stg)
            for i in range(2):
                S.add("sp", DMA(mixT0[2 * g + i, :, :], mixst[:, i, :]), reads=[A("chT", i)], writes=[("mixT0", 2 * g + i)], dma=("mx", i))

    if on('ssd'):
        phase_ssd()

    def emit_pipeline(steps, depth=3, lag1=2, lag2=4):
        n = len(steps)
        for i in range(n + depth + lag2):
            if i < n:
                for op in steps[i]["front"]:
                    S.add(*op)
            for key, off in (("back", depth), ("late1", depth + lag1), ("late2", depth + lag2)):
                j = i - off
                if 0 <= j < n:
                    for op in steps[j].get(key, ()):
                        S.add(*op)

    SBANKS = (0, 1, 4, 5)
    DEPTH = 3

    def qk_group(gidx, t0, wqk, nwqk, bufs, qkT, perhead=None, qpad=None):
        qr, t1, t2, pr, qkb = bufs
        s = gidx % 2
        A = lambda *k: ("AR",) + k
        for half in range(2):
            for i in range(2):
                proj_tm(half, i * 256, t0 + 2 * half + i, wqk, 256, [A("wqk")])
            S.add("act", ACP(qr[s][:, 2 * half:2 * half + 2, :], ps[half][:, :].rearrange("p (i c) -> p i c", c=256)),
                  writes=[PS(half), A("qr", s, half)])
        fl = lambda a: a.rearrange("p i c -> p (i c)")
        v3 = lambda a: a.rearrange("p i (h e) -> p (i h) e", e=64)
        v4 = lambda a: a.rearrange("p i (h e) -> p i h e", e=64)
        S.add("pool", TT(fl(t1[s]), fl(qr[s]), fl(qr[s]), ALU.mult), reads=[A("qr", s)], writes=[A("t1", s)])
        ssq = stats[:, 512 + 16 * s: 528 + 16 * s]
        stok = ("stats", "qk", s)
        S.add("dve", lambda e: e.tensor_reduce(out=ssq, in_=v3(t1[s]), axis=AX.X, op=ALU.add), reads=[A("t1", s)], writes=[stok])
        S.add("act", ACT(ssq, ssq, AF.Ln, scale=1.0 / 64, bias=cv[:, 0:1]), reads=[stok, "cv"], writes=[stok])
        S.add("act", ACT(ssq, ssq, AF.Exp, scale=-0.5), reads=[stok], writes=[stok])
        S.add("dve", TT(v3(t1[s]), v3(qr[s]), ssq.unsqueeze(2).to_broadcast([128, 16, 64]), ALU.mult), reads=[stok, A("qr", s)], writes=[A("t1", s)])
        S.add("dve", TT(t2[s], t1[s], nwqk.rearrange("p h e -> p (h e)").unsqueeze(1).to_broadcast([128, 4, 256]), ALU.mult),
              reads=[A("t1", s), A("nwqk")], writes=[A("t1", s)])
        x1 = v4(t2[s])[:, :, :, 0:8]
        x2 = v4(t2[s])[:, :, :, 8:16]
        cs = cosT[:, t0:t0 + 4, :].unsqueeze(2).to_broadcast([128, 4, 4, 8])
        sn = sinT[:, t0:t0 + 4, :].unsqueeze(2).to_broadcast([128, 4, 4, 8])
        p4 = pr[s]
        pj = lambda j: p4[:, j].rearrange("p (i h) e -> p i h e", h=4)
        S.add("pool", TT(pj(0), x1, cs, ALU.mult), reads=[A("t1", s), "cst"], writes=[A("pr", s, 0)])
        S.add("pool", TT(pj(1), x2, sn, ALU.mult), reads=[A("t1", s), "cst"], writes=[A("pr", s, 1)])
        S.add("pool", TT(pj(2), x2, cs, ALU.mult), reads=[A("t1", s), "cst"], writes=[A("pr", s, 2)])
        S.add("pool", TT(pj(3), x1, sn, ALU.mult), reads=[A("t1", s), "cst"], writes=[A("pr", s, 3)])
        qb = v4(qkb[s])
        S.add("dve", TT(qb[:, :, :, 0:8], pj(0), pj(1), ALU.subtract), reads=[A("pr", s, 0), A("pr", s, 1)], writes=[A("qkb", s, 0)])
        S.add("dve", TT(qb[:, :, :, 8:16], pj(2), pj(3), ALU.add), reads=[A("pr", s, 2), A("pr", s, 3)], writes=[A("qkb", s, 1)])
        S.add("act", ACP(qb[:, :, :, 16:64], v4(t2[s])[:, :, :, 16:64]), reads=[A("t1", s)], writes=[A("qkb", s, 2)])
        if perhead is not None:
            for half in range(2):
                bk = 2 + half
                S.add("pe", TRS([(psb[bk][0:64, (jj * 4 + i) * 128:(jj * 4 + i + 1) * 128], qkb[s][:, i, (2 * half + jj) * 64:(2 * half + jj + 1) * 64], identb)
                                 for jj in range(2) for i in range(4)]), reads=[A("qkb", s), "cb"], writes=[PS(bk)])
                for jj in range(2):
                    dst, dtok = perhead[2 * half + jj]
                    eng, fn = ("act", ACP) if jj == 0 else ("dve", CP)
                    S.add(eng, fn(dst[0:64, t0 * 128:(t0 + 4) * 128], psb[bk][0:64, jj * 512:(jj + 1) * 512]),
                          writes=[PS(bk), dtok + ("qk", t0 // 4)])
            return
        bk = 2 + s
        S.add("pe", TRS([(psb[bk][:, (i * 2 + j) * 128:(i * 2 + j + 1) * 128], qkb[s][:, i, j * 128:(j + 1) * 128], identb)
                         for i in range(4) for j in range(2)]), reads=[A("qkb", s), "cb"], writes=[PS(bk)])
        if qpad is not None:
            qp0, qp1, kTt = qpad
            src = psb[bk][:, :].rearrange("p (i j c) -> p j i c", j=2, c=128)
            tv = lambda a: a[:, t0 * 128:(t0 + 4) * 128].rearrange("p (i c) -> p i c", c=128)
            S.add("act", ACP(tv(kTt), src[:, 1]), writes=[PS(bk), A("kTt", t0 // 4)])
            S.add("dve", CP(tv(qp0)[0:64], src[0:64, 0]), writes=[PS(bk), A("qp", 0, t0 // 4)])
            S.add("act", ACP(tv(qp1)[64:128], src[64:128, 0]), writes=[PS(bk), A("qp", 1, t0 // 4)])
            return
        S.add("act", ACP(qkT[:, :, t0 * 128:(t0 + 4) * 128].rearrange("p j (i c) -> p j i c", c=128),
                         psb[bk][:, :].rearrange("p (i j c) -> p j i c", j=2, c=128)),
              writes=[PS(bk)] + [A("qkT", t0 + i) for i in range(4)])

    def qk_groups_interleaved(calls):
        bounds = (0, 6, 8, 10, 12, 16, 19, None)
        recs = []
        for fn in calls:
            n0 = len(S.ops)
            fn()
            ops = S.ops[n0:]
            del S.ops[n0:]
            recs.append([ops[bounds[i]:bounds[i + 1]] for i in range(7)])
        n = len(recs)
        for k in range(n + 1):
            early = recs[k][0:4] if k < n else [[], [], [], []]
            late = recs[k - 1][4:7] if k >= 1 else [[], [], []]
            for stage in (early[0], late[0], early[1], late[1], early[2], late[2], early[3]):
                S.ops.extend(stage)

    def qk_bufs(alloc):
        qr = [arv(alloc(1024 * 4), [128, 4, 256], F32) for _ in range(2)]
        t1 = [arv(alloc(1024 * 4), [128, 4, 256], F32) for _ in range(2)]
        t2 = t1
        pr = [arv(alloc(512 * 4), [128, 4, 16, 8], F32) for _ in range(2)]
        qkb = [arv(alloc(1024 * 2), [128, 4, 256], BF16) for _ in range(2)]
        return (qr, t1, t2, pr, qkb)

    def silu_gate_T(wg, sgT, A):
        for i in range(8):
            bk = i % 2
            items = [(ps[bk][:, 0:512], wg[:, kc, :], hT[:, kc, i * 512:(i + 1) * 512], kc == 0, kc == 7) for kc in range(8)]
            S.add("pe", MM(items), reads=[("hT",), A("wg")], writes=[PS(bk)])
            S.add("act", ACT(sgT[:, i * 512:(i + 1) * 512], ps[bk][:, 0:512], AF.Silu), writes=[PS(bk), A("sgT", i)])

    def phase_dilated():
        S.claim(("AR",))
        o = 0

        def alloc(nbytes):
            nonlocal o
            r = o
            o += (nbytes + 63) // 64 * 64
            assert o <= ARN * 2, o
            return r
        A = lambda *k: ("AR",) + k
        wqk = arv(alloc(8 * 256 * 2), [128, 8, 256], BF16)
        wv = arv(alloc(8 * 128 * 2), [128, 8, 128], BF16)
        wg = arv(alloc(8 * 128 * 2), [128, 8, 128], BF16)
        qp = [arv(alloc(SEQ * 2), [128, SEQ], BF16) for _ in range(2)]
        kTt = arv(alloc(SEQ * 2), [128, SEQ], BF16)
        vblk = arv(alloc(NT * 256 * 2), [128, NT, 2, 128], BF16)
        sgT = arv(alloc(SEQ * 2), [128, SEQ], BF16)
        accN = arv(alloc(SEQ * 4), [128, SEQ], F32)
        accD = arv(alloc(SEQ * 4), [128, SEQ], F32)
        ost = arv(alloc(SEQ * 2), [128, SEQ], BF16)
        nwqk = arv(alloc(256 * 4), [128, 4, 64], F32)
        bufs = qk_bufs(alloc)
        Pt = [arv(alloc(512 * 2), [128, 512], BF16) for _ in range(4)]
        Rn2 = [bufs[1][j][:, 0:2, :].rearrange("p i c -> p (i c)") for j in range(2)]
        m01 = arv(alloc(512 * 2), [128, 512], BF16)
        for hh in range(2):
            S.add("dve", TS(m01[:, hh * 256:hh * 256 + 128], cst[:, 640:768], -1.0, 0.0, ALU.is_ge, ALU.add), reads=["cst"], writes=[A("m01")])
            S.add("dve", TS(m01[:, hh * 256 + 128:hh * 256 + 256], cst[:, 512:640], -1.0, 0.0, ALU.is_ge, ALU.add), reads=["cst"], writes=[A("m01")])
        S.add("pool", lambda e: e.memset(vblk[:, :, 0, 64:128], 1.0), writes=[A("vblk", "ones0")])
        S.add("pool", lambda e: e.memset(vblk[:, :, 1, 0:64], 1.0), writes=[A("vblk", "ones1")])
        S.add("pool", lambda e: e.memset(qp[0][64:128, :], 0.0), writes=[A("qp", 0)])
        S.add("pool", lambda e: e.memset(qp[1][0:64, :], 0.0), writes=[A("qp", 1)])
        qk_shift(e_qn, e_kn, 1, nwqk, bufs[1][0][:, 0, :], A("t1", 0))
        negC = cv[:, 1:2]
        sidx = 0
        gidx = 0
        def dil_loads(jp, gi):
            hcol = (8 * gi + 2 * jp) * 64
            load_w(wqk[:, :, 0:128], e_winv, 3088 + hcol, 128, A("wqk"), "wq")
            load_w(wqk[:, :, 128:256], e_winv, 4624 + hcol, 128, A("wqk"), "wk")
            load_w(wv, e_winv, 6160 + hcol, 128, A("wv"), "wv")

        load_w(wg, e_winv, 7696, 128, A("wg"), "wg")
        dil_loads(0, 0)
        for jp in range(4):
            silu_gate_T(wg, sgT, A)
            if jp + 1 < 4:
                load_w(wg, e_winv, 7696 + (jp + 1) * 128, 128, A("wg"), "wg")
            for gi, r in enumerate((1, 4, 16)):
                calls = []
                for t4 in range(NT // 4):
                    calls.append((lambda gidx=gidx, t4=t4: qk_group(gidx, t4 * 4, wqk, nwqk, bufs, None, qpad=(qp[0], qp[1], kTt))))
                    gidx += 1
                qk_groups_interleaved(calls)
                nb = NT // r
                for b4 in range(NT // 4):
                    bk = b4 % 2
                    for i in range(4):
                        blk = b4 * 4 + i
                        c, n = blk // nb, blk % nb
                        items = []
                        for kc in range(8):
                            hv = hT[:, kc, :].rearrange("p (u r) -> p u r", r=r)[:, 128 * n:128 * n + 128, c]
                            items.append((ps[bk][:, i * 128:(i + 1) * 128], hv, wv[:, kc, :], kc == 0, kc == 7))
                        S.add("pe", MM(items), reads=[("hT",), A("wv")], writes=[PS(bk)])
                    pv3 = ps[bk][:, :].rearrange("p (i c) -> p i c", c=128)
                    S.add("act", ACP(vblk[:, b4 * 4:b4 * 4 + 4, 0, 0:64], pv3[:, :, 0:64]), writes=[PS(bk), A("vblk", "v0", b4)])
                    S.add("dve", CP(vblk[:, b4 * 4:b4 * 4 + 4, 1, 64:128], pv3[:, :, 64:128]), writes=[PS(bk), A("vblk", "v1", b4)])
                qvh = [q_.rearrange("p (u r) -> p u r", r=r) for q_ in qp]
                kv = kTt.rearrange("p (u r) -> p u r", r=r)
                steps = []
                for b4 in range(NT // 4):
                    nbk, dbk = ((6, 7), (2, 3))[b4 % 2]
                    for i in range(4):
                        blk = b4 * 4 + i
                        c, n = blk // nb, blk % nb
                        sb_ = SBANKS[sidx % 4]
                        pt = Pt[sidx % 4]
                        ptk = A("Pt", sidx % 4)
                        sidx += 1
                        front, back = [], []
                        items = []
                        for hh in range(2):
                            hs = slice(64 * hh, 64 * hh + 64)
                            qa = qvh[hh][:, 128 * n:128 * n + 128, c]
                            ko = kv[:, 128 * n:128 * n + 128, c]
                            b0 = hh * 256
                            if n > 0:
                                kp = kv[:, 128 * (n - 1):128 * n, c]
                                items += [(ps[sb_][:, b0:b0 + 128], kp, qa, True, True),
                                          (ps[sb_][:, b0 + 128:b0 + 256], ko, qa, True, True)]
                            else:
                                items += [(ps[sb_][:, b0 + 128:b0 + 256], ko, qa, True, True)]
                        front.append(("pe", MM(items), [A("qp"), A("kTt")], [PS(sb_)]))
                        if n > 0:
                            front.append(("act", ACT(pt[:, :], ps[sb_][:, :], AF.Exp, bias=negC), ["cv"], [PS(sb_), ptk]))
                            front.append(("dve", TT(pt[:, :], pt[:, :], m01[:, :], ALU.mult), [A("m01")], [ptk]))
                        else:
                            for hh in range(2):
                                oc_ = slice(hh * 256 + 128, hh * 256 + 256)
                                front.append(("act", ACT(pt[:, oc_], ps[sb_][:, oc_], AF.Exp, bias=negC), ["cv"], [PS(sb_), ptk]))
                                front.append(("dve", TT(pt[:, oc_], pt[:, oc_], m01[:, oc_], ALU.mult), [A("m01")], [ptk]))
                        items = []
                        for hh in range(2):
                            hs = slice(64 * hh, 64 * hh + 64)
                            oc = slice(i * 128, (i + 1) * 128)
                            b0 = hh * 256
                            dst = ps[(nbk, dbk)[hh]]
                            if n > 0:
                                items += [(dst[:, oc], vblk[:, blk - 1, hh, :], pt[:, b0:b0 + 128], True, False),
                                          (dst[:, oc], vblk[:, blk, hh, :], pt[:, b0 + 128:b0 + 256], False, True)]
                            else:
                                items += [(dst[:, oc], vblk[:, blk, hh, :], pt[:, b0 + 128:b0 + 256], True, True)]
                        back.append(("pe", MM(items), [ptk, A("vblk")], [PS(nbk), PS(dbk)]))
                        if i == 3:
                            blk0 = b4 * 4
                            if r == 16:
                                c0 = blk0 // nb
                                dn = accN.rearrange("p (u r) -> p r u", r=16)[:, c0:c0 + 2, :]
                                dd = accD.rearrange("p (u r) -> p r u", r=16)[:, c0:c0 + 2, :]
                                sn_ = ps[nbk][:, :].rearrange("p (a b) -> p a b", b=256)
                                sd_ = ps[dbk][:, :].rearrange("p (a b) -> p a b", b=256)
                            else:
                                c0, n0 = blk0 // nb, blk0 % nb
                                dn = accN.rearrange("p (u r) -> p u r", r=r)[:, 128 * n0:128 * n0 + 512, c0]
                                dd = accD.rearrange("p (u r) -> p u r", r=r)[:, 128 * n0:128 * n0 + 512, c0]
                                sn_ = ps[nbk][:, :]
                                sd_ = ps[dbk][:, :]
                            if gi == 0:
                                back.append(("act", ACP(dn, sn_), [], [PS(nbk), A("accN")]))
                                back.append(("dve", CP(dd, sd_), [], [PS(dbk), A("accD")]))
                            else:
                                back.append(("dve", TT(dn, sn_, dn, ALU.add), [], [PS(nbk), A("accN")]))
                                back.append(("dve", TT(dd, sd_, dd, ALU.add), [], [PS(dbk), A("accD")]))
                        steps.append(dict(front=front, back=back))
                nxt = jp * 3 + gi + 1
                if nxt < 12:
                    dil_loads(nxt // 3, nxt % 3)
                emit_pipeline(steps, DEPTH)
            for hh, acc, atok in ((0, accN, A("accN")), (1, accD, A("accD"))):
                nr = slice(64 * hh, 64 * hh + 64)
                dr = slice(64 * (1 - hh), 64 * (1 - hh) + 64)
                for hf in range(2):
                    hc_ = slice(hf * 2048, (hf + 1) * 2048)
                    S.add("act", ACT(acc[dr, hc_], acc[dr, hc_], AF.Ln), writes=[atok + (hf,)])
                    S.add("act", ACT(acc[dr, hc_], acc[dr, hc_], AF.Exp, scale=-1.0), writes=[atok + (hf,)])
                for i8 in range(8):
                    cs_ = slice(i8 * 512, (i8 + 1) * 512)
                    bk = i8 % 2
                    rn = Rn2[i8 % 2]
                    rtok = A("t1", i8 % 2)
                    S.add("pe", MM([(ps[bk][nr, 0:512], identf[dr, dr], acc[dr, cs_], True, True)]), reads=[atok + (i8 // 4,), "cst"], writes=[PS(bk)])
                    S.add("dve", TT(rn[nr, :], ps[bk][nr, 0:512], acc[nr, cs_], ALU.mult), reads=[atok], writes=[PS(bk), rtok])
                    S.add("pool", TT(ost[nr, cs_], rn[nr, :], sgT[nr, cs_], ALU.mult), reads=[rtok, A("sgT")], writes=[A("ost", hh, i8)])
            S.add("sp", DMA(mixT0[8 + jp, :, :], ost[:, :]), reads=[A("ost")], writes=[("mixT0", 8 + jp)], dma="ost")

    if on('dil'):
        phase_dilated()

    def phase_outproj(mixT_d, nch, wout_d, res_d, dst_d, next_normw_d, scol, dst_tok, mix_tok, res_tok):
        S.claim(("AR",))
        A = lambda *k: ("AR",) + k
        o = 0

        def alloc(nbytes):
            nonlocal o
            r = o
            o += (nbytes + 63) // 64 * 64
            assert o <= ARN * 2, o
            return r
        wo = arv(alloc(nch * DM * 2), [128, nch, DM], BF16)
        NBUF = 3
        mt = [arv(alloc(nch * 128 * 2), [128, nch, 128], BF16) for _ in range(NBUF)]
        xr = [arv(alloc(DM * 4), [128, DM], F32) for _ in range(NBUF)]
        x1t = [arv(alloc(DM * 4), [128, DM], F32) for _ in range(NBUF)]
        hb = [arv(alloc(DM * 2), [128, DM], BF16) for _ in range(NBUF)]
        nw = arv(alloc(DM * 4), [128, DM], F32)
        junk = arv(alloc(DM * 2), [128, DM], BF16)
        wov = wout_d.rearrange("(c p) f -> p c f", p=128)
        for c in range(0, nch, 4):
            S.add("pool", DMA(wo[:, c:c + 4, :], wov[:, c:c + 4, :]), writes=[A("wo", c)], dma=("wo", c))
        if next_normw_d is not None:
            S.add("sp", DMA(nw, next_normw_d[0:1, :].partition_broadcast(128)), writes=[A("nw")], dma="nw")
        mv = mixT_d.rearrange("c p t -> p c t")
        for t in range(NT):
            s = t % NBUF
            S.add("sp", DMA(mt[s], mv[:, :, t * 128:(t + 1) * 128]), reads=mix_tok, writes=[A("mt", s)], dma=("mt", s))
            S.add("sp", DMA(xr[s], res_d[t * 128:(t + 1) * 128, :]), reads=res_tok, writes=[A("xr", s)], dma=("xr", s))
            for hf in range(2):
                items = [(ps[hf][:, 0:512], mt[s][:, c, :], wo[:, c, hf * 512:(hf + 1) * 512], c == 0, c == nch - 1) for c in range(nch)]
                S.add("pe", MM(items), reads=[A("mt", s), A("wo")], writes=[PS(hf)])
                S.add("dve", TT(x1t[s][:, hf * 512:(hf + 1) * 512], ps[hf][:, 0:512], xr[s][:, hf * 512:(hf + 1) * 512], ALU.add),
                      reads=[A("xr", s)], writes=[PS(hf), A("x1t", s, hf)])
            S.add("pool", DMA(dst_d[t * 128:(t + 1) * 128, :], x1t[s]), reads=[A("x1t", s)], writes=[(dst_tok, t)], dma=("x1o", s))
            if next_normw_d is not None:
                if t >= 1:
                    p = t - 1
                    norm_transpose_tile(None, None, p, None, None, hb[p % NBUF], A("hb", p % NBUF), None, scol, 2 + p % 2, part="B")
                norm_transpose_tile(x1t[s], A("x1t", s), t, nw, A("nw"), hb[s], A("hb", s), junk, scol, 2 + t % 2, part="A")
        if next_normw_d is not None:
            p = NT - 1
            norm_transpose_tile(None, None, p, None, None, hb[p % NBUF], A("hb", p % NBUF), None, scol, 2 + p % 2, part="B")

    if on('out0'):
        phase_outproj(mixT0, 12, e_wout, x_d, x1_d, o_normw, 384, "x1s", [("mixT0",)], [])

    o_winv = o_win.rearrange("(kc p) f -> p kc f", p=128)

    def phase_moba():
        S.claim(("AR",))
        o = 0

        def alloc(nbytes):
            nonlocal o
            r = o
            o += (nbytes + 63) // 64 * 64
            assert o <= ARN * 2, o
            return r
        A = lambda *k: ("AR",) + k
        wqk = arv(alloc(8 * 256 * 2), [128, 8, 256], BF16)
        wv = arv(alloc(8 * 128 * 2), [128, 8, 128], BF16)
        wg = arv(alloc(8 * 128 * 2), [128, 8, 128], BF16)
        qaug = [arv(alloc(SEQ * 2), [128, SEQ], BF16) for _ in range(2)]
        kaug = [arv(alloc(SEQ * 2), [128, SEQ], BF16) for _ in range(2)]
        vaug = arv(alloc(NT * 256 * 2), [128, NT, 2, 128], BF16)
        sgT = arv(alloc(SEQ * 2), [128, SEQ], BF16)
        ost = arv(alloc(SEQ * 2), [128, SEQ], BF16)
        nwqk = arv(alloc(256 * 4), [128, 4, 64], F32)
        bufs = qk_bufs(alloc)
        Pt = [arv(alloc(512 * 2), [128, 512], BF16) for _ in range(3)]
        kbf2 = [arv(alloc(16 * 4), [128, 16], F32) for _ in range(2)]
        kbb2 = [arv(alloc(16 * 2), [128, 16], BF16) for _ in range(2)]
        gm2 = [arv(alloc(512 * 4), [128, NT, 16], F32) for _ in range(2)]
        mx2 = [arv(alloc(NT * 8 * 4), [128, NT, 8], F32) for _ in range(2)]
        thr2 = [arv(alloc(NT * 4), [128, NT], F32) for _ in range(2)]
        selb2 = [arv(alloc(512 * 2), [128, NT, 16], BF16) for _ in range(2)]
        Rt = [arv(alloc(256 * 4), [128, 256], F32) for _ in range(4)]
        rs = [arv(alloc(256 * 4), [128, 256], F32) for _ in range(4)]
        oo = [arv(alloc(256 * 4), [128, 256], F32) for _ in range(4)]
        qk_shift(o_qn, o_kn, 2, nwqk, bufs[1][0][:, 0, :], A("t1", 0))
        negC = cv[:, 2:3]
        QA = [A("qaug", h) for h in range(2)]
        KA = [A("kaug", h) for h in range(2)]
        for h in range(2):
            S.add("pool", (lambda h=h: (lambda e: e.memset(qaug[h][64:128, :], 0.0)))(), writes=[QA[h]])
            S.add("pool", (lambda h=h: (lambda e: e.memset(kaug[h][64:128, :], 0.0)))(), writes=[KA[h]])
            S.add("pool", DMA(kaug[h][64:80, :], koh_d[:, :]), writes=[KA[h] + ("oh",)], dma=("koh", h))
        S.add("pool", lambda e: e.memset(vaug[:, :, 0, 64:128], 1.0), writes=[A("vaug", "ones0")])
        S.add("pool", lambda e: e.memset(vaug[:, :, 1, 0:64], 1.0), writes=[A("vaug", "ones1")])
        MBANKS = (0, 1, 4)
        sidx = 0
        gidx = 0
        def moba_loads(hp):
            load_w(wg, o_winv, 3072 + hp * 128, 128, A("wg"), "wg")
            load_w(wqk[:, :, 0:128], o_winv, hp * 128, 128, A("wqk"), "wq")
            load_w(wqk[:, :, 128:256], o_winv, 1024 + hp * 128, 128, A("wqk"), "wk")
            load_w(wv, o_winv, 2048 + hp * 128, 128, A("wv"), "wv")

        moba_loads(0)
        for hp in range(8):
            silu_gate_T(wg, sgT, A)
            calls = []
            for t4 in range(NT // 4):
                calls.append((lambda gidx=gidx, t4=t4: qk_group(gidx, t4 * 4, wqk, nwqk, bufs, None,
                              perhead=[(qaug[0], QA[0]), (qaug[1], QA[1]), (kaug[0], KA[0]), (kaug[1], KA[1])])))
                gidx += 1
            qk_groups_interleaved(calls)
            for hh in range(2):
                S.add("dve", (lambda hh=hh: (lambda e: e.tensor_reduce(out=kbf2[hh][0:64, :], in_=kaug[hh][0:64, :].rearrange("p (n k) -> p n k", k=256),
                                                                    axis=AX.X, op=ALU.add)))(), reads=[KA[hh]], writes=[A("kbf", hh)])
                S.add("dve", TS(kbb2[hh][0:64, :], kbf2[hh][0:64, :], 1.0 / 256, None, ALU.mult), reads=[A("kbf", hh)], writes=[A("kbb", hh)])
            for hh in range(2):
                gb = 4 + hh
                items = [(ps[gb][:, t * 16:(t + 1) * 16], qaug[hh][0:64, t * 128:(t + 1) * 128], kbb2[hh][0:64, :], True, True) for t in range(NT)]
                S.add("pe", MM(items), reads=[QA[hh], A("kbb", hh)], writes=[PS(gb)])
            for b4 in range(NT // 4):
                bk = b4 % 2
                for i in range(4):
                    t = b4 * 4 + i
                    proj_tm(bk, i * 128, t, wv, 128, [A("wv")])
                pv3 = ps[bk][:, :].rearrange("p (i c) -> p i c", c=128)
                S.add("act", ACP(vaug[:, b4 * 4:b4 * 4 + 4, 0, 0:64], pv3[:, :, 0:64]), writes=[PS(bk), A("vaug", "v0", b4)])
                S.add("act", ACP(vaug[:, b4 * 4:b4 * 4 + 4, 1, 64:128], pv3[:, :, 64:128]), writes=[PS(bk), A("vaug", "v1", b4)])
            for hh in range(2):
                gb = 4 + hh
                gm_, mx_, thr_, selb_ = gm2[hh], mx2[hh], thr2[hh], selb2[hh]
                G = lambda *k: A("g", hh) + k
                S.add("dve", TT(gm_.rearrange("p t n -> p (t n)"), ps[gb][:, :], validb, ALU.add), reads=["cst"], writes=[PS(gb), G("gm")])
                for t in range(NT):
                    S.add("dve", (lambda t=t, mx_=mx_, gm_=gm_: (lambda e: e.max(out=mx_[:, t, :], in_=gm_[:, t, :])))(), reads=[G("gm")], writes=[G("mx", t)])
                S.add("dve", TS(thr_[:, :], mx_[:, :, 2], -1e29, None, ALU.max), reads=[G("mx")], writes=[G("thr")])
                S.add("dve", TT(gm_, gm_, thr_.unsqueeze(2).to_broadcast([128, NT, 16]), ALU.is_ge), reads=[G("thr"), G("gm")], writes=[G("gm")])
                S.add("dve", TT(selb_.rearrange("p t n -> p (t n)"), gm_.rearrange("p t n -> p (t n)"), ownm1, ALU.add),
                      reads=[G("gm"), "cst"], writes=[G("selb")])
            for hh in range(2):
                selb_ = selb2[hh]
                for t8 in range(4):
                    S.add("pe", TRS([(psb[3][64:80, i * 128:(i + 1) * 128], selb_[:, t8 * 8 + i, :], identb) for i in range(8)]),
                          reads=[A("g", hh, "selb"), "cb"], writes=[PS(3)])
                    S.add("act", ACP(qaug[hh][64:80, t8 * 1024:(t8 + 1) * 1024], psb[3][64:80, :]), writes=[PS(3), QA[hh] + ("m", t8)])
            steps = []
            for tb in range(16):
                q0 = tb * 256
                for hh in range(2):
                    xb = ((6, 7), (2, 3))[tb % 2][hh]
                    qa = qaug[hh][:, q0:q0 + 256]
                    for n in range(tb + 1):
                        sb_ = MBANKS[sidx % 3]
                        pt = Pt[sidx % 3]
                        ptk = A("Pt", sidx % 3)
                        sidx += 1
                        front, back, late1, late2 = [], [], [], []
                        k0 = kaug[hh][:, (2 * n) * 128:(2 * n + 1) * 128]
                        k1 = kaug[hh][:, (2 * n + 1) * 128:(2 * n + 2) * 128]
                        if n < tb:
                            items = [(ps[sb_][:, 0:256], k0, qa, True, True),
                                     (ps[sb_][:, 256:512], k1, qa, True, True)]
                            front.append(("pe", MM(items), [QA[hh], KA[hh]], [PS(sb_)]))
                            front.append(("act", ACT(pt[:, :], ps[sb_][:, :], AF.Exp, bias=negC), ["cv"], [PS(sb_), ptk]))
                            pv = [(0, 256, pt[:, 0:256], 2 * n), (0, 256, pt[:, 256:512], 2 * n + 1)]
                        else:
                            items = [(ps[sb_][:, 0:256], k0, qa, True, False),
                                     (ps[sb_][:, 0:128], identb, maskLE, False, True),
                                     (ps[sb_][:, 384:512], k1, qaug[hh][:, q0 + 128:q0 + 256], True, False),
                                     (ps[sb_][:, 384:512], identb, maskLE, False, True)]
                            front.append(("pe", MM(items), [QA[hh], KA[hh], "cb"], [PS(sb_)]))
                            front.append(("act", ACT(pt[:, 0:256], ps[sb_][:, 0:256], AF.Exp, bias=negC), ["cv"], [PS(sb_), ptk]))
                            front.append(("act", ACT(pt[:, 384:512], ps[sb_][:, 384:512], AF.Exp, bias=negC), ["cv"], [PS(sb_), ptk]))
                            pv = [(0, 256, pt[:, 0:256], 2 * n), (128, 256, pt[:, 384:512], 2 * n + 1)]
                        items = []
                        for pi, (c0_, c1_, rhs, kt) in enumerate(pv):
                            first = (n == 0 and pi == 0)
                            last = (n == tb and pi == 1)
                            items.append((ps[xb][:, c0_:c1_], vaug[:, kt, hh, :], rhs, first, last))
                        back.append(("pe", MM(items), [ptk, A("vaug")], [PS(xb)]))
                        if n == tb:
                            s_ = (tb * 2 + hh) % 4
                            nr = slice(64 * hh, 64 * hh + 64)
                            dr = slice(64 * (1 - hh), 64 * (1 - hh) + 64)
                            back.append(("dve", (lambda s_=s_, xb=xb, dr=dr: (lambda e: e.reciprocal(out=Rt[s_][dr, :], in_=ps[xb][dr, 0:256])))(),
                                         [], [PS(xb), A("Rt", s_)]))
                            back.append(("act", ACP(rs[s_][nr, :], ps[xb][nr, 0:256]), [], [PS(xb), A("rs", s_)]))
                            late1.append(("pe", MM([(ps[5][nr, 0:256], identf[dr, dr], Rt[s_][dr, :], True, True)]), [A("Rt", s_), "cst"], [("ps", 5, hh)]))
                            late2.append(("dve", TT(oo[s_][nr, :], ps[5][nr, 0:256], rs[s_][nr, :], ALU.mult), [A("rs", s_)], [("ps", 5, hh), A("oo", s_)]))
                            late2.append(("pool", TT(ost[nr, q0:q0 + 256], oo[s_][nr, :], sgT[nr, q0:q0 + 256], ALU.mult),
                                          [A("oo", s_), A("sgT")], [A("ost", tb, hh)]))
                        steps.append(dict(front=front, back=back, late1=late1, late2=late2))
            if hp + 1 < 8:
                moba_loads(hp + 1)
            emit_pipeline(steps, 2)
            S.add("sp", DMA(mixT1[hp, :, :], ost[:, :]), reads=[A("ost")], writes=[("mixT1", hp)], dma="ost")

    if stop_after != "l0":
        if on('moba'):
            phase_moba()
        if on('out1'):
            phase_outproj(mixT1, 8, o_wout, x1_d, out_d, None, 0, "out", [("mixT1",)], [("x1s",)])
        S.add("sp", None, reads=[("out",)])
    else:
        S.add("sp", None, reads=[("x1s",)])
    S.emit(st)
    K.n_ops = len(S.ops)
    K.n_sems = S.n_sems
    K.stack = st
    return nc, K


_CACHE = {}


def make_in_map(inputs, b, cst):
    g = lambda k: np.ascontiguousarray(np.asarray(inputs[k], dtype=np.float32)[0])
    m = {
        "x": np.ascontiguousarray(np.asarray(inputs["x"], dtype=np.float32)[b]),
        "even_norm_w": g("even_norm_w").reshape(1, DM),
        "even_w_in": g("even_w_in"),
        "even_conv_w": g("even_conv_w"),
        "even_conv_b": g("even_conv_b").reshape(2048, 1),
        "even_dt_bias": g("even_dt_bias").reshape(1, 16),
        "even_a_log": g("even_a_log").reshape(1, 16),
        "even_d_skip": g("even_d_skip").reshape(1, 16),
        "even_ssd_norm_w": g("even_ssd_norm_w").reshape(1, DM),
        "even_q_norm": g("even_q_norm").reshape(1, 64),
        "even_k_norm": g("even_k_norm").reshape(1, 64),
        "even_w_out": g("even_w_out"),
        "odd_norm_w": g("odd_norm_w").reshape(1, DM),
        "odd_w_in": g("odd_w_in"),
        "odd_q_norm": g("odd_q_norm").reshape(1, 64),
        "odd_k_norm": g("odd_k_norm").reshape(1, 64),
        "odd_w_out": g("odd_w_out"),
        "cst": cst,
        "koh": make_koh(),
    }
    return m


def kernel(**inputs):
    if "nc" not in _CACHE:
        _CACHE["nc"] = build_program()[0]
    nc = _CACHE["nc"]
    cst = make_consts()
    in_maps = [make_in_map(inputs, c % 4, cst) for c in range(8)]
    res = run_bass_kernel_spmd(nc, in_maps, core_ids=list(range(8)))
    out = np.stack([np.asarray(res.results[c]["out"], dtype=np.float32) for c in range(4)], axis=0)
    return out
```

```python
import contextlib
import numpy as np
import concourse.bass as bass
import concourse.mybir as mybir
from concourse.bass_utils import run_bass_kernel_spmd

F32 = mybir.dt.float32
BF16 = mybir.dt.bfloat16
AF = mybir.ActivationFunctionType
ALU = mybir.AluOpType
AX = mybir.AxisListType

SEQ = 4096
NT = 32
DM = 1024
EVEN_IN = 8208
NEG = -30000.0
EPS = 1e-6
ENGINES = ("pe", "act", "dve", "pool", "sp")
EPOCH = 30000


class Sched:
    def __init__(self, nc):
        self.nc = nc
        self.ops = []
        self.dummy = None

    @staticmethod
    def _norm(toks):
        return tuple((t,) if not isinstance(t, tuple) else t for t in toks)

    def add(self, eng, fn, reads=(), writes=(), dma=None):
        assert eng in ENGINES, eng
        self.ops.append(dict(eng=eng, fn=fn, reads=self._norm(reads), writes=self._norm(writes), dma=dma,
                             deps=set(), needs_inc=False))

    def claim(self, tok, eng="dve"):
        d = self.dummy
        self.add(eng, lambda e: e.memset(d[:, 0:1], 0.0), writes=[tok])

    def _analyze(self):
        state = {}
        kids = {}

        def related(tok):
            out = []
            for n in range(1, len(tok) + 1):
                p = tok[:n]
                if p in state:
                    out.append(p)
            for c in kids.get(tok, ()):
                if c in state:
                    out.append(c)
            return out

        def register(tok):
            if tok not in state:
                state[tok] = [None, []]
                for n in range(1, len(tok)):
                    kids.setdefault(tok[:n], set()).add(tok)

        for i, op in enumerate(self.ops):
            deps = set()
            for k in op["reads"]:
                for r in related(k):
                    if state[r][0] is not None:
                        deps.add(state[r][0])
            for k in op["writes"]:
                for r in related(k):
                    if state[r][0] is not None:
                        deps.add(state[r][0])
                    deps.update(state[r][1])
            deps.discard(i)
            pruned = set()
            for d in deps:
                p = self.ops[d]
                if p["dma"] is None and op["dma"] is None and p["eng"] == op["eng"] == "pe":
                    continue
                pruned.add(d)
            op["deps"] = pruned
            if op["dma"] is not None:
                op["needs_inc"] = True
            for d in pruned:
                self.ops[d]["needs_inc"] = True
            for k in op["reads"]:
                register(k)
                state[k][1].append(i)
            for k in op["writes"]:
                register(k)
                for c in list(kids.get(k, ())):
                    if c in state:
                        del state[c]
                state[k] = [i, []]

    def emit(self, stack):
        nc = self.nc
        self._analyze()
        counters = {}
        sig = {}
        semkeys = []
        for i, op in enumerate(self.ops):
            if not op["needs_inc"]:
                continue
            if op["dma"] is not None:
                key = ("dma", op["dma"])
                step = 16
            else:
                key = ("eng", op["eng"])
                step = 1
            c = counters.get(key, 0) + 1
            counters[key] = c
            sk = (key, c // EPOCH)
            ec = counters.get(("ec", sk), 0) + step
            counters[("ec", sk)] = ec
            sig[i] = (sk, ec)
            if sk not in semkeys:
                semkeys.append(sk)
        sems = {}
        for n, sk in enumerate(semkeys):
            sems[sk] = stack.enter_context(nc.semaphore("s%d" % n))
        self.n_sems = len(semkeys)
        per_eng = {e: [] for e in ENGINES}
        for i, op in enumerate(self.ops):
            per_eng[op["eng"]].append(i)
        ops = self.ops

        def run(e_name, eobj):
            waited = {}
            for i in per_eng[e_name]:
                op = ops[i]
                need = {}
                for d in op["deps"]:
                    sk, v = sig[d]
                    if waited.get(sk, 0) >= v:
                        continue
                    if need.get(sk, 0) < v:
                        need[sk] = v
                for sk, v in need.items():
                    eobj.wait_ge(sems[sk], v)
                    waited[sk] = v
                if op["fn"] is None:
                    continue
                ins = op["fn"](eobj)
                if op["needs_inc"]:
                    assert ins is not None
                    ins.then_inc(sems[sig[i][0]], 16 if op["dma"] is not None else 1)

        block = stack.enter_context(nc.Block())

        @block.tensor
        def _(e):
            run("pe", e)

        @block.scalar
        def _(e):
            run("act", e)

        @block.vector
        def _(e):
            run("dve", e)

        @block.gpsimd
        def _(e):
            run("pool", e)

        @block.sync
        def _(e):
            run("sp", e)


def ACT(out, in_, func, **kw):
    return lambda e: e.activation(out=out, in_=in_, func=func, **kw)


def TT(out, in0, in1, op):
    return lambda e: e.tensor_tensor(out=out, in0=in0, in1=in1, op=op)


def TS(out, in0, s1, s2, op0, op1=None):
    if op1 is None:
        return lambda e: e.tensor_scalar(out=out, in0=in0, scalar1=s1, scalar2=0.0, op0=op0, op1=ALU.add)
    return lambda e: e.tensor_scalar(out=out, in0=in0, scalar1=s1, scalar2=s2, op0=op0, op1=op1)


def STT(out, in0, scalar, in1, op0, op1):
    return lambda e: e.scalar_tensor_tensor(out=out, in0=in0, scalar=scalar, in1=in1, op0=op0, op1=op1)


def CP(out, in_):
    return lambda e: e.tensor_copy(out=out, in_=in_)


def ACP(out, in_):
    return lambda e: e.copy(out=out, in_=in_)


def DMA(out, in_):
    return lambda e: e.dma_start(out=out, in_=in_)


def MM(items):
    def fn(e):
        ins = None
        for (o, l, r, s0, s1) in items:
            ins = e.matmul(o, lhsT=l, rhs=r, start=s0, stop=s1)
        return ins
    return fn


def TRS(items):
    def fn(e):
        ins = None
        for (o, i_, idn) in items:
            ins = e.transpose(out=o, in_=i_, identity=idn)
        return ins
    return fn


CST_W = 2304


def make_consts():
    c = np.zeros((128, CST_W), np.float32)
    p = np.arange(128)[:, None]
    f = np.arange(128)[None, :]
    c[:, 0:128] = (p == f)
    c[:, 128:256] = (p <= f)
    c[:, 256:384] = 1.0
    c[127, 384:512] = 1.0
    c[:, 512:640] = np.where(p <= f, 0.0, NEG)
    c[:, 640:768] = np.where(p >= f, 0.0, NEG)
    inv = 500000.0 ** (-np.arange(8, dtype=np.float32) * 2.0 / 16.0)
    pos = (np.arange(NT)[None, :, None] * 128 + np.arange(128)[:, None, None]).astype(np.float32)
    ang = pos * inv[None, None, :].astype(np.float32)
    c[:, 768:1024] = np.cos(ang).reshape(128, 256)
    c[:, 1024:1280] = np.sin(ang).reshape(128, 256)
    t = np.arange(NT)[:, None]
    n = np.arange(16)[None, :]
    vb = np.where(n < (t // 2), 0.0, -1e30).astype(np.float32).reshape(1, 512)
    c[:, 1280:1792] = vb
    c[:, 1792:2304] = np.where(n == (t // 2), 0.0, -1.0).astype(np.float32).reshape(1, 512)
    return c


def make_koh():
    k = np.arange(SEQ)[None, :] // 256
    n = np.arange(16)[:, None]
    return np.where(k == n, 30000.0, 0.0).astype(np.float32)


class Prog:
    pass


def build_program(debug=False, stop_after=None, only=None):
    on = lambda p: (only is None) or (p in only)
    nc = bass.Bass("TRN2", target_bir_lowering=False)
    K = Prog()
    din = lambda name, shape: nc.dram_tensor(name, shape, F32, kind="ExternalInput").ap()
    x_d = din("x", [SEQ, DM])
    e_normw = din("even_norm_w", [1, DM])
    e_win = din("even_w_in", [DM, EVEN_IN])
    e_convw = din("even_conv_w", [2048, 4])
    e_convb = din("even_conv_b", [2048, 1])
    e_dtb = din("even_dt_bias", [1, 16])
    e_alog = din("even_a_log", [1, 16])
    e_dskip = din("even_d_skip", [1, 16])
    e_ssdnw = din("even_ssd_norm_w", [1, DM])
    e_qn = din("even_q_norm", [1, 64])
    e_kn = din("even_k_norm", [1, 64])
    e_wout = din("even_w_out", [1536, DM])
    o_normw = din("odd_norm_w", [1, DM])
    o_win = din("odd_w_in", [DM, 4096])
    o_qn = din("odd_q_norm", [1, 64])
    o_kn = din("odd_k_norm", [1, 64])
    o_wout = din("odd_w_out", [DM, DM])
    cst_d = din("cst", [128, CST_W])
    koh_d = din("koh", [16, SEQ])
    out_d = nc.dram_tensor("out", [SEQ, DM], F32, kind="ExternalOutput").ap()
    skind = "ExternalOutput" if debug else "Internal"
    mixT0 = nc.dram_tensor("mixT0", [12, 128, SEQ], BF16, kind=skind).ap()
    x1_d = nc.dram_tensor("x1s", [SEQ, DM], F32, kind=skind).ap()
    mixT1 = nc.dram_tensor("mixT1", [8, 128, SEQ], BF16, kind=skind).ap()
    acs_scr = nc.dram_tensor("acs_scr", [64, 16, 256], F32, kind="Internal").ap()

    st = contextlib.ExitStack()
    sb = lambda name, shape, dt: st.enter_context(nc.sbuf_tensor("k_" + name, shape, dt))
    cst = sb("cst_sb", [128, CST_W], F32)
    cb = sb("cstb", [128, 640], BF16)
    cv = sb("cv", [128, 8], F32)
    hT = sb("hT", [128, 8, SEQ], BF16)
    stats = sb("stats", [128, 640], F32)
    dummy = sb("dummy", [128, 4], F32)
    ARN = 65536
    AR = sb("arena", [128, ARN], BF16)
    ps = [st.enter_context(nc.psum_tensor("ps%d" % i, [128, 512], F32)) for i in range(8)]
    psb = [p[:].bitcast(BF16) for p in ps]

    S = Sched(nc)
    S.dummy = dummy
    identf = cst[:, 0:128]
    triU = cst[:, 128:256]
    onesf = cst[:, 256:384]
    e127 = cst[:, 384:512]
    cosT = cst[:, 768:1024].rearrange("p (t i) -> p t i", i=8)
    sinT = cst[:, 1024:1280].rearrange("p (t i) -> p t i", i=8)
    validb = cst[:, 1280:1792]
    ownm1 = cst[:, 1792:2304]
    identb = cb[:, 0:128]
    maskLE = cb[:, 128:256]
    maskGE = cb[:, 256:384]
    onesb = cb[:, 384:512]
    id30k = cb[:, 512:640]
    maskPO = cb[:, 128:384]
    PS = lambda b: ("ps", b)

    def arv(off_bytes, shape, dt):
        n = int(np.prod(shape[1:]))
        esz = 2 if dt == BF16 else 4
        a = AR[:, off_bytes // 2: off_bytes // 2 + n * esz // 2]
        if dt != BF16:
            a = a.bitcast(dt)
        if len(shape) == 3:
            a = a.rearrange("p (a b) -> p a b", b=shape[2])
        elif len(shape) == 4:
            a = a.rearrange("p (a b c) -> p a b c", b=shape[2], c=shape[3])
        return a

    S.add("sp", DMA(cst[:], cst_d[:, :]), writes=["cst"], dma="cst")
    S.add("dve", CP(cb[:, 0:128], cst[:, 0:128]), reads=["cst"], writes=["cb"])
    S.add("dve", CP(cb[:, 128:256], cst[:, 512:640]), reads=["cst"], writes=["cb"])
    S.add("dve", CP(cb[:, 256:384], cst[:, 640:768]), reads=["cst"], writes=["cb"])
    S.add("dve", CP(cb[:, 384:512], cst[:, 256:384]), reads=["cst"], writes=["cb"])
    S.add("dve", TS(cb[:, 512:640], cst[:, 0:128], 30000.0, None, ALU.mult), reads=["cst"], writes=["cb"])
    S.add("dve", lambda e: e.memset(cv[:, 0:1], EPS), writes=["cv"])
    S.add("dve", lambda e: e.memset(stats[:], 0.0), writes=["stats"])

    def qk_shift(qn_d, kn_d, col, nwqk, scr, scr_tok):
        S.add("sp", DMA(nwqk[:, 0, :], qn_d[0:1, :].partition_broadcast(128)), writes=[("AR", "nwqk")], dma="nwq0")
        S.add("sp", DMA(nwqk[:, 2, :], kn_d[0:1, :].partition_broadcast(128)), writes=[("AR", "nwqk")], dma="nwq1")
        tmp = stats[:, 480:484]
        S.add("act", ACT(scr[:, 0:64], nwqk[:, 0, :], AF.Abs), reads=[("AR", "nwqk")], writes=[scr_tok])
        S.add("act", ACT(scr[:, 64:128], nwqk[:, 2, :], AF.Abs), reads=[("AR", "nwqk")], writes=[scr_tok])
        S.add("dve", lambda e: e.tensor_reduce(out=tmp[:, 0:1], in_=scr[:, 0:64], axis=AX.X, op=ALU.max),
              reads=[scr_tok], writes=[("stats", "c")])
        S.add("dve", lambda e: e.tensor_reduce(out=tmp[:, 1:2], in_=scr[:, 64:128], axis=AX.X, op=ALU.max),
              reads=[scr_tok], writes=[("stats", "c")])
        S.add("dve", TT(tmp[:, 2:3], tmp[:, 0:1], tmp[:, 1:2], ALU.mult), reads=[("stats", "c")], writes=[("stats", "c")])
        S.add("dve", TS(cv[:, col:col + 1], tmp[:, 2:3], -8.0, None, ALU.mult), reads=[("stats", "c")], writes=["cv"])
        S.add("dve", TS(nwqk[:, 0, :], nwqk[:, 0, :], 0.125, None, ALU.mult), reads=[("AR", "nwqk")], writes=[("AR", "nwqk")])
        S.add("dve", CP(nwqk[:, 1, :], nwqk[:, 0, :]), reads=[("AR", "nwqk")], writes=[("AR", "nwqk")])
        S.add("dve", CP(nwqk[:, 3, :], nwqk[:, 2, :]), reads=[("AR", "nwqk")], writes=[("AR", "nwqk")])

    def norm_transpose_tile(xt, xt_tok, t, nw, nw_tok, hb, hb_tok, junk, scol, trbank, part="AB"):
        if "B" in part and "A" not in part:
            pb = psb[trbank]
            S.add("pe", TRS([(pb[:, kc * 128:(kc + 1) * 128], hb[:, kc * 128:(kc + 1) * 128], identb) for kc in range(8)]),
                  reads=[hb_tok, "cb"], writes=[PS(trbank)])
            S.add("act", ACP(hT[:, :, t * 128:(t + 1) * 128], pb[:, :].rearrange("p (k c) -> p k c", c=128)),
                  writes=[PS(trbank), ("hT", t)])
            return
        ss = stats[:, scol + t: scol + t + 1]
        ln = stats[:, scol + 32 + t: scol + 33 + t]
        rs = stats[:, scol + 64 + t: scol + 65 + t]
        stok = ("stats", scol + t)
        S.add("act", ACT(junk, xt, AF.Square, accum_out=ss), reads=[xt_tok], writes=[("AR", "junk"), stok])
        S.add("act", ACT(ln, ss, AF.Ln, scale=1.0 / DM, bias=cv[:, 0:1]), reads=[stok, "cv"], writes=[stok])
        S.add("act", ACT(rs, ln, AF.Exp, scale=-0.5), reads=[stok], writes=[stok])
        S.add("dve", STT(hb, xt, rs, nw, ALU.mult, ALU.mult), reads=[xt_tok, stok, nw_tok], writes=[hb_tok])
        if part == "A":
            return
        pb = psb[trbank]
        S.add("pe", TRS([(pb[:, kc * 128:(kc + 1) * 128], hb[:, kc * 128:(kc + 1) * 128], identb) for kc in range(8)]),
              reads=[hb_tok, "cb"], writes=[PS(trbank)])
        S.add("act", ACP(hT[:, :, t * 128:(t + 1) * 128], pb[:, :].rearrange("p (k c) -> p k c", c=128)),
              writes=[PS(trbank), ("hT", t)])

    def phase_norm(src_d, normw_d, scol):
        S.claim(("AR",))
        xin = [arv(i * 4096, [128, DM], F32) for i in range(3)]
        hb = [arv(12288 + i * 2048, [128, DM], BF16) for i in range(3)]
        nw = arv(18432, [128, DM], F32)
        junk = arv(22528, [128, DM], BF16)
        S.add("sp", DMA(nw, normw_d[0:1, :].partition_broadcast(128)), writes=[("AR", "nw")], dma="nw")
        for t in range(NT + 1):
            if t < NT:
                s = t % 3
                S.add("sp", DMA(xin[s], src_d[t * 128:(t + 1) * 128, :]), writes=[("AR", "xin", s)], dma=("xin", s))
                norm_transpose_tile(xin[s], ("AR", "xin", s), t, nw, ("AR", "nw"), hb[s], ("AR", "hb", s), junk, 0 + scol, 2 + t % 2, part="A")
            if t >= 1:
                p = t - 1
                norm_transpose_tile(None, None, p, None, None, hb[p % 3], ("AR", "hb", p % 3), None, 0 + scol, 2 + p % 2, part="B")

    def load_w(dst, wv, c0, n, tok, chan):
        S.add("pool", DMA(dst, wv[:, :, c0:c0 + n]), writes=[tok], dma=chan)

    def proj_tm(bank, cols, t, w, n, reads, toks=None):
        items = [(ps[bank][:, cols:cols + n], hT[:, kc, t * 128:(t + 1) * 128], w[:, kc, 0:n], kc == 0, kc == 7) for kc in range(8)]
        S.add("pe", MM(items), reads=[("hT",)] + list(reads), writes=[PS(bank)])

    e_winv = e_win.rearrange("(kc p) f -> p kc f", p=128)
    if on('norm'):
        phase_norm(x_d, e_normw, 0)

    def phase_ssd():
        S.claim(("AR",))
        A_ = lambda *k: ("AR",) + k
        o = 0

        def alloc(nbytes):
            nonlocal o
            r = o
            o += (nbytes + 63) // 64 * 64
            assert o <= ARN * 2, o
            return r
        wconv = arv(alloc(8 * 128 * 2), [128, 8, 128], BF16)
        wz_off = alloc(8 * 256 * 2)
        wz = arv(wz_off, [128, 8, 256], BF16)
        wdt = arv(alloc(8 * 16 * 2), [128, 8, 16], BF16)
        cin_off = alloc((SEQ + 4) * 4)
        cin = arv(cin_off, [128, SEQ + 4], F32)
        zs_all = arv(cin_off, [128, NT, 256], BF16)
        cacc_off = alloc(1024 * 4)
        cacc = arv(cacc_off, [128, 1024], F32)
        cacc2 = [arv(cacc_off + j * 2048, [128, 512], F32) for j in range(2)]
        cwb = arv(alloc(8 * 4), [128, 8], F32)
        ch_off = alloc(4 * SEQ * 2)
        chT = [arv(ch_off + i * SEQ * 2, [128, SEQ], BF16) for i in range(4)]
        mixst = arv(ch_off, [128, 2, SEQ], BF16)
        xs_tok = arv(alloc(NT * 256 * 2), [128, NT, 256], BF16)
        B_tok = arv(alloc(NT * 128 * 2), [128, NT, 128], BF16)
        xdt_c = [arv(alloc(512 * 2), [128, 2, 256], BF16) for _ in range(2)]
        xdtw_c = [arv(alloc(512 * 2), [128, 2, 256], BF16) for _ in range(2)]
        dtt = arv(alloc(512 * 4), [128, NT, 16], F32)
        acs = arv(alloc(512 * 4), [128, NT, 16], F32)
        wst = arv(alloc(512 * 4), [128, NT, 16], F32)
        tmp5 = arv(alloc(512 * 4), [128, NT, 16], F32)
        tmp6 = arv(alloc(512 * 4), [128, NT, 16], F32)
        wend = tmp6
        dec = arv(alloc(256 * 4), [128, 16, 16], F32)
        aend = arv(alloc(256 * 4), [128, 16, 16], F32)
        dtg = arv(alloc(128 * 4), [128, 128], F32)
        dtwg = arv(alloc(128 * 4), [128, 128], F32)
        wsg = arv(alloc(128 * 4), [128, 128], F32)
        bc16 = arv(alloc(64 * 4), [128, 4, 16], F32)
        acsT = [arv(alloc(256 * 4), [128, 256], F32) for _ in range(2)]
        Sst = arv(alloc(256 * 4), [128, 256], F32)
        Sbf = [arv(alloc(256 * 2), [128, 256], BF16) for _ in range(2)]
        Dt = [arv(alloc(384 * 2), [128, 384], BF16) for _ in range(2)]
        Ag = [arv(alloc(384 * 4), [128, 384], F32) for _ in range(2)]
        CBm = [arv(alloc(384 * 4), [128, 384], F32)] * 2
        m384 = arv(alloc(384 * 2), [128, 384], BF16)
        Ag1 = arv(alloc(384 * 4), [128, 384], F32)
        bcb = [[arv(off + h * 1024, [128, 256], F32) for h in range(4)] for off in (cacc_off, wz_off)]
        bcb_tok = [[A_("cacc", h) for h in range(4)], [A_("wz", h) for h in range(4)]]
        Mt = [arv(alloc(384 * 2), [128, 384], BF16) for _ in range(4)]
        E1 = arv(alloc(512 * 4), [128, 2, 256], F32)
        E2 = [arv(alloc(512 * 4), [128, 2, 256], F32) for _ in range(2)]
        E3 = [arv(alloc(512 * 4), [128, 2, 256], F32)] * 2
        yb = [arv(alloc(512 * 2), [128, 2, 256], BF16) for _ in range(2)]
        junk = arv(alloc(512 * 2), [128, 512], BF16)
        nwg = arv(alloc(256 * 4), [128, 256], F32)

        A = lambda *k: ("AR",) + k
        S.add("dve", CP(m384[:, 0:128], triU), reads=["cst"], writes=[A("m384")])
        S.add("dve", CP(m384[:, 128:256], onesf), reads=["cst"], writes=[A("m384")])
        S.add("dve", CP(m384[:, 256:384], triU), reads=["cst"], writes=[A("m384")])
        S.add("sp", DMA(bc16[:, 0, :], e_dtb[0:1, :].partition_broadcast(128)), writes=[A("bc16")], dma="bc0")
        S.add("sp", DMA(bc16[:, 1, :], e_alog[0:1, :].partition_broadcast(128)), writes=[A("bc16")], dma="bc1")
        S.add("sp", DMA(bc16[:, 2, :], e_dskip[0:1, :].partition_broadcast(128)), writes=[A("bc16")], dma="bc2")
        S.add("act", ACT(bc16[:, 1, :], bc16[:, 1, :], AF.Exp), reads=[A("bc16")], writes=[A("bc16")])
        S.add("dve", TS(bc16[:, 1, :], bc16[:, 1, :], -1.0, None, ALU.mult), reads=[A("bc16")], writes=[A("bc16")])
        load_w(wdt, e_winv, 3072, 16, A("wdt"), "wdt")
        for t in range(NT):
            proj_tm(0, t * 16, t, wdt, 16, [A("wdt")])
        dflat = lambda a: a.rearrange("p t h -> p (t h)")
        S.add("dve", TT(dtt, ps[0][:, :].rearrange("p (t h) -> p t h", h=16), bc16[:, 0:1, :].to_broadcast([128, NT, 16]), ALU.add),
              reads=[A("bc16")], writes=[PS(0), A("dtt")])
        S.add("act", ACT(dflat(tmp5), dflat(dtt), AF.Abs), reads=[A("dtt")], writes=[A("tmp5")])
        S.add("act", ACT(dflat(tmp5), dflat(tmp5), AF.Exp, scale=-1.0), reads=[A("tmp5")], writes=[A("tmp5")])
        S.add("act", ACT(dflat(tmp5), dflat(tmp5), AF.Ln, bias=1.0), reads=[A("tmp5")], writes=[A("tmp5")])
        S.add("dve", TS(dflat(tmp6), dflat(dtt), 0.0, None, ALU.max), reads=[A("dtt")], writes=[A("tmp6")])
        S.add("dve", TT(dflat(dtt), dflat(tmp6), dflat(tmp5), ALU.add), reads=[A("tmp5"), A("tmp6")], writes=[A("dtt")])
        S.add("dve", TT(tmp6, dtt, bc16[:, 1:2, :].to_broadcast([128, NT, 16]), ALU.mult), reads=[A("dtt"), A("bc16")], writes=[A("tmp6")])
        a4 = tmp6.rearrange("p (c j) h -> p c j h", j=2)
        acs4 = acs.rearrange("p (c j) h -> p c j h", j=2)
        S.add("pe", MM([(ps[1][:, 0:256], triU, a4[:, :, 0, :], True, True),
                        (ps[1][:, 256:512], onesf, a4[:, :, 0, :], True, False),
                        (ps[1][:, 256:512], triU, a4[:, :, 1, :], False, True)]), reads=[A("tmp6"), "cst"], writes=[PS(1)])
        S.add("dve", CP(acs4[:, :, 0, :], ps[1][:, 0:256].rearrange("p (c h) -> p c h", h=16)), writes=[PS(1), A("acs")])
        S.add("dve", CP(acs4[:, :, 1, :], ps[1][:, 256:512].rearrange("p (c h) -> p c h", h=16)), writes=[PS(1), A("acs")])
        S.add("pe", MM([(ps[0][:, 0:256], e127, acs4[:, :, 1, :], True, True)]), reads=[A("acs"), "cst"], writes=[PS(0)])
        S.add("act", ACT(dec.rearrange("p c h -> p (c h)"), ps[0][:, 0:256], AF.Exp), writes=[PS(0), A("dec")])
        S.add("dve", CP(aend.rearrange("p c h -> p (c h)"), ps[0][:, 0:256]), writes=[PS(0), A("aend")])
        for j in range(2):
            S.add("dve", TT(wend.rearrange("p (c j) h -> p c j h", j=2)[:, :, j, :], acs4[:, :, j, :], aend, ALU.subtract),
                  reads=[A("acs"), A("aend")], writes=[A("tmp6")])
        S.add("act", ACT(dflat(wend), dflat(wend), AF.Exp, scale=-1.0), reads=[A("tmp6")], writes=[A("tmp6")])
        S.add("act", ACT(dflat(wst), dflat(acs), AF.Exp), reads=[A("acs")], writes=[A("wst")])
        S.add("dve", TT(dflat(tmp5), dflat(dtt), dflat(wend), ALU.mult), reads=[A("dtt"), A("tmp6")], writes=[A("tmp5")])
        S.add("dve", TS(dflat(acs), dflat(acs), -1.0, None, ALU.mult), reads=[A("acs")], writes=[A("acs")])

        for g in range(4):
            chans = [1024 + g * 256, 1024 + g * 256 + 128, 2048 + g * 128, 2560 + g * 128]
            S.claim(A("cacc"))
            S.add("dve", lambda e: e.memset(cin[:, 0:4], 0.0), writes=[A("cin")])
            for ci, c0 in enumerate(chans):
                load_w(wconv, e_winv, c0, 128, A("wconv"), "wconv")
                S.add("sp", DMA(cwb[:, 0:4], e_convw[c0 - 1024:c0 - 1024 + 128, :]), writes=[A("cwb")], dma="cw")
                S.add("sp", DMA(cwb[:, 4:5], e_convb[c0 - 1024:c0 - 1024 + 128, :]), writes=[A("cwb")], dma="cb")
                def silu_step(i):
                    j = i % 2
                    S.add("act", ACT(chT[ci][:, i * 512:(i + 1) * 512], cacc2[j], AF.Silu), reads=[A("cacc", "h", j)], writes=[A("chT", ci, i // 2)])
                for i in range(8):
                    bk = i % 2
                    j = i % 2
                    items = [(ps[bk][:, 0:512], wconv[:, kc, :], hT[:, kc, i * 512:(i + 1) * 512], kc == 0, kc == 7) for kc in range(8)]
                    S.add("pe", MM(items), reads=[("hT",), A("wconv")], writes=[PS(bk)])
                    S.add("act", ACP(cin[:, 4 + i * 512: 4 + (i + 1) * 512], ps[bk][:, 0:512]), writes=[PS(bk), A("cin", i)])
                    q0 = i * 512
                    rd = [A("cin", i), A("cwb")] + ([A("cin", i - 1)] if i else [A("cin")])
                    S.add("act", ACT(cacc2[j], cin[:, 4 + q0: 4 + q0 + 512], AF.Identity, scale=cwb[:, 3:4], bias=cwb[:, 4:5]),
                          reads=rd, writes=[A("cacc", "h", j)])
                    for kk in (2, 1, 0):
                        sh = 3 - kk
                        S.add("dve", STT(cacc2[j], cin[:, 4 + q0 - sh: 4 + q0 - sh + 512], cwb[:, kk:kk + 1], cacc2[j], ALU.mult, ALU.add),
                              reads=rd + [A("cacc", "h", j)], writes=[A("cacc", "h", j)])
                    if i > 0:
                        silu_step(i - 1)
                silu_step(7)
            S.claim(A("cacc"))
            for t in range(NT):
                bk = 2 + t % 2
                S.add("pe", TRS([(psb[bk][:, j * 128:(j + 1) * 128], chT[j][:, t * 128:(t + 1) * 128], identb) for j in range(3)]),
                      reads=[A("chT", 0), A("chT", 1), A("chT", 2), "cb"], writes=[PS(bk)])
                S.add("act", ACP(xs_tok[:, t, :], psb[bk][:, 0:256]), writes=[PS(bk), A("xs_tok", t)])
                S.add("dve", CP(B_tok[:, t, :], psb[bk][:, 256:384]), writes=[PS(bk), A("B_tok", t)])
            S.add("dve", CP(dtg.rearrange("p (t h) -> p t h", h=4), dtt[:, :, 4 * g:4 * g + 4]), reads=[A("dtt")], writes=[A("dtg")])
            S.add("dve", CP(dtwg.rearrange("p (t h) -> p t h", h=4), tmp5[:, :, 4 * g:4 * g + 4]), reads=[A("tmp5")], writes=[A("dtwg")])
            S.add("dve", CP(wsg.rearrange("p (t h) -> p t h", h=4), wst[:, :, 4 * g:4 * g + 4]), reads=[A("wst")], writes=[A("wsg")])
            load_w(wz, e_winv, g * 256, 256, A("wz"), "wz")
            S.claim(A("cin"))
            for t in range(NT):
                proj_tm(t % 2, 0, t, wz, 256, [A("wz")])
                S.add("act", ACT(zs_all[:, t, :], ps[t % 2][:, 0:256], AF.Silu), writes=[PS(t % 2), A("cin", "zs", t)])
            S.add("sp", DMA(nwg, e_ssdnw[0:1, g * 256:(g + 1) * 256].partition_broadcast(128)), writes=[A("nwg")], dma="nwg")
            S.add("dve", lambda e: e.memset(Sst[:], 0.0), writes=[A("Sst")])
            BT, CT = chT[2], chT[3]
            DB = (5, 0)

            def prep(c):
                t0 = 2 * c
                ops = []
                xs3 = xs_tok[:, t0:t0 + 2, :].rearrange("p t (h e) -> p (t h) e", e=64)
                ops.append(("dve", TT(xdt_c[c % 2].rearrange("p t (h e) -> p (t h) e", e=64), xs3,
                                      dtg[:, t0 * 4:t0 * 4 + 8].unsqueeze(2).to_broadcast([128, 8, 64]), ALU.mult),
                            [A("xs_tok", t0), A("xs_tok", t0 + 1), A("dtg")], [A("xdt", c % 2)]))
                ops.append(("pool", TT(xdtw_c[c % 2].rearrange("p t (h e) -> p (t h) e", e=64), xs3,
                                       dtwg[:, t0 * 4:t0 * 4 + 8].unsqueeze(2).to_broadcast([128, 8, 64]), ALU.mult),
                            [A("xs_tok", t0), A("xs_tok", t0 + 1), A("dtwg")], [A("xdtw", c % 2)]))
                return ops

            def prep_acs(c):
                t0 = 2 * c
                aT = acsT[c % 2]
                ops = [("pe", TRS([(ps[3][0:16, 0:128], acs[:, t0, :], identf), (ps[3][0:16, 128:256], acs[:, t0 + 1, :], identf)]),
                        [A("acs"), "cst"], [PS(3)]),
                       ("dve", CP(aT[0:16, :], ps[3][0:16, 0:256]), [], [PS(3), A("acsT", c % 2)]),
                       ("sp", DMA(acs_scr[g * 16 + c, :, :], aT[0:16, :]), [A("acsT", c % 2)], [("acsscr", g * 16 + c)], ("acss", c % 2))]
                for h in range(4):
                    hh = 4 * g + h
                    ops.append(("sp", DMA(bcb[c % 2][h], acs_scr[g * 16 + c, hh:hh + 1, :].partition_broadcast(128)),
                                [("acsscr", g * 16 + c)], [bcb_tok[c % 2][h]], ("acsb", c % 2, h)))
                return ops

            def emit(ops):
                for op in ops:
                    S.add(*op)

            def deferred(c):
                t0 = 2 * c
                e2 = E2[c % 2]
                f2 = lambda a: a.rearrange("p t c -> p (t c)")
                sc0 = 128 + (g * NT + t0)
                st1 = [("pool", TT(E3[0].rearrange("p t (h e) -> p t h e", e=64), xs_tok[:, t0:t0 + 2, :].rearrange("p t (h e) -> p t h e", e=64),
                                  bc16[:, 2, 4 * g:4 * g + 4].unsqueeze(1).unsqueeze(3).to_broadcast([128, 2, 4, 64]), ALU.mult),
                        [A("xs_tok", t0), A("xs_tok", t0 + 1), A("bc16")], [A("E3")]),
                       ("dve", TT(f2(e2), f2(e2), f2(E3[0]), ALU.add), [A("E3"), A("E2", c % 2)], [A("E2", c % 2)]),
                       ("dve", TT(f2(e2), f2(e2), f2(zs_all[:, t0:t0 + 2, :]), ALU.mult), [A("cin", "zs", t0), A("cin", "zs", t0 + 1), A("E2", c % 2)], [A("E2", c % 2)])]
                for j in range(2):
                    st1.append(("act", ACT(junk[:, j * 256:(j + 1) * 256], e2[:, j, :], AF.Square, accum_out=stats[:, sc0 + j:sc0 + j + 1]),
                                [A("E2", c % 2)], [A("junk", j), ("stats", sc0 + j)]))
                ssv = stats[:, sc0:sc0 + 2]
                rsv = stats[:, sc0 + 128:sc0 + 130]
                st2 = [("act", ACT(rsv, ssv, AF.Ln, scale=1.0 / 256, bias=cv[:, 0:1]), [("stats", sc0), ("stats", sc0 + 1), "cv"], [("stats", "r", sc0)]),
                       ("act", ACT(rsv, rsv, AF.Exp, scale=-0.5), [("stats", "r", sc0)], [("stats", "r", sc0)])]
                st3 = [("dve", STT(yb[c % 2][:, j, :], e2[:, j, :], stats[:, sc0 + 128 + j:sc0 + 129 + j], nwg, ALU.mult, ALU.mult),
                        [A("E2", c % 2), ("stats", "r", sc0), A("nwg")], [A("yb", c % 2, j)]) for j in range(2)]
                st4 = [("pe", TRS([(psb[2][:, (j * 2 + i) * 128:(j * 2 + i + 1) * 128], yb[c % 2][:, j, i * 128:(i + 1) * 128], identb)
                                   for j in range(2) for i in range(2)]), [A("yb", c % 2), "cb"], [PS(2)]),
                       ("act", ACP(mixst[:, :, t0 * 128:(t0 + 2) * 128].rearrange("p i (j c) -> p i j c", c=128),
                                   psb[2][:, 0:512].rearrange("p (j i c) -> p i j c", i=2, c=128)),
                        [], [PS(2), A("chT", 0, t0 // 8), A("chT", 1, t0 // 8)])]
                return [st1, st2, st3, st4]

            def head(c):
                t0 = 2 * c
                s0 = c * 256
                S.add("pe", MM([(ps[4][:, 0:256], BT[:, s0:s0 + 128], CT[:, s0:s0 + 256], True, True),
                                (ps[4][:, 256:384], BT[:, s0 + 128:s0 + 256], CT[:, s0 + 128:s0 + 256], True, True)]),
                      reads=[A("chT", 2), A("chT", 3)], writes=[PS(4)])
                S.add("dve", TT(CBm[0][:, :], ps[4][:, 0:384], m384[:, :], ALU.mult), reads=[A("m384")], writes=[PS(4), A("CBm")])
                S.add("pe", MM([(ps[1][:, 0:256], B_tok[:, t0, :], xdtw_c[c % 2][:, 0, :], True, False),
                                (ps[1][:, 0:256], B_tok[:, t0 + 1, :], xdtw_c[c % 2][:, 1, :], False, True)]),
                      reads=[A("B_tok"), A("xdtw", c % 2)], writes=[PS(1)])
                fa(c, 0)
                fa(c, 1)

            def fa(c, h):
                t0 = 2 * c
                hh = 4 * g + h
                d_ = Dt[h % 2]
                bc = bcb[c % 2][h]
                S.add("act", ACT(Ag1[:, 0:256], bc[:, 0:256], AF.Abs, scale=-1.0, bias=acs[:, t0, hh:hh + 1]),
                      reads=[bcb_tok[c % 2][h], A("acs")], writes=[A("Ag1")])
                S.add("act", ACT(Ag1[:, 256:384], bc[:, 128:256], AF.Abs, scale=-1.0, bias=acs[:, t0 + 1, hh:hh + 1]),
                      reads=[bcb_tok[c % 2][h], A("acs")], writes=[A("Ag1")])
                S.add("act", ACT(d_[:, :], Ag1[:, :], AF.Exp, scale=-1.0), reads=[A("Ag1")], writes=[A("Dt", h % 2)])

            def fb(c, h):
                S.add("dve", STT(Mt[h][:, :], Dt[h % 2][:, :], 1.0, CBm[0][:, :], ALU.min, ALU.mult), reads=[A("Dt", h % 2), A("CBm")], writes=[A("Mt", h)])

            def back(c, h):
                m_ = Mt[h]
                xdt = xdt_c[c % 2]
                hc = slice(h * 64, (h + 1) * 64)
                S.add("pe", MM([(ps[6][:, h * 64:(h + 1) * 64], m_[:, 0:128], xdt[:, 0, hc], True, True),
                                (ps[6][:, 256 + h * 64:256 + (h + 1) * 64], m_[:, 128:256], xdt[:, 0, hc], True, False),
                                (ps[6][:, 256 + h * 64:256 + (h + 1) * 64], m_[:, 256:384], xdt[:, 1, hc], False, True)]),
                      reads=[A("Mt", h), A("xdt", c % 2)], writes=[PS(6)])

            def state_update(c):
                S.add("dve", TT(Sst.rearrange("p (h e) -> p h e", e=64), Sst.rearrange("p (h e) -> p h e", e=64),
                                dec[:, c, 4 * g:4 * g + 4].unsqueeze(2).to_broadcast([128, 4, 64]), ALU.mult),
                      reads=[A("Sst"), A("dec")], writes=[A("Sst")])
                S.add("dve", TT(Sst, Sst, ps[1][:, 0:256], ALU.add), reads=[A("Sst")], writes=[PS(1), A("Sst")])
                S.add("dve", CP(Sbf[(c + 1) % 2], Sst), reads=[A("Sst")], writes=[A("Sbf", (c + 1) % 2)])

            emit(prep(0))
            emit(prep_acs(0))
            head(0)
            pend = None
            for c in range(16):
                t0, t1 = 2 * c, 2 * c + 1
                s0 = c * 256
                fb(c, 0)
                if c + 1 < 16:
                    emit(prep_acs(c + 1))
                if pend:
                    emit(pend[0])
                fa(c, 2)
                fb(c, 1)
                state_update(c)
                back(c, 0)
                fa(c, 3)
                fb(c, 2)
                if pend:
                    emit(pend[1])
                if c + 1 < 16:
                    emit(prep(c + 1))
                back(c, 1)
                fb(c, 3)
                if pend:
                    emit(pend[2])
                back(c, 2)
                if pend:
                    emit(pend[3])
                back(c, 3)
                e2 = E2[c % 2]
                f2 = lambda a: a.rearrange("p t c -> p (t c)")
                if c > 0:
                    S.add("pe", MM([(ps[7][:, 0:256], CT[:, s0:s0 + 128], Sbf[c % 2], True, True),
                                    (ps[7][:, 256:512], CT[:, s0 + 128:s0 + 256], Sbf[c % 2], True, True)]),
                          reads=[A("chT", 3), A("Sbf", c % 2)], writes=[PS(7)])
                if c + 1 < 16:
                    head(c + 1)
                if c > 0:
                    S.add("dve", TT(E1.rearrange("p t (h e) -> p (t h) e", e=64), ps[7][:, :].rearrange("p (t h e) -> p (t h) e", h=4, e=64),
                                    wsg[:, t0 * 4:t0 * 4 + 8].unsqueeze(2).to_broadcast([128, 8, 64]), ALU.mult),
                          reads=[A("wsg")], writes=[PS(7), A("E1")])
                    S.add("dve", TT(f2(e2), ps[6][:, :], f2(E1), ALU.add), reads=[A("E1")], writes=[PS(6), A("E2", c % 2)])
                else:
                    S.add("dve", CP(f2(e2), ps[6][:, :]), writes=[PS(6), A("E2", c % 2)])
                pend = deferred(c)
            for stg in pend:
                emit(stg)
            for i in range(2):
                S.add("sp", DMA(mixT0[2 * g + i, :, :], mixst[:, i, :]), reads=[A("chT", i)], writes=[("mixT0", 2 * g + i)], dma=("mx", i))

    if on('ssd'):
        phase_ssd()

    def emit_pipeline(steps, depth=3, lag1=2, lag2=4):
        n = len(steps)
        for i in range(n + depth + lag2):
            if i < n:
                for op in steps[i]["front"]:
                    S.add(*op)
            for key, off in (("back", depth), ("late1", depth + lag1), ("late2", depth + lag2)):
                j = i - off
                if 0 <= j < n:
                    for op in steps[j].get(key, ()):
                        S.add(*op)

    SBANKS = (0, 1, 4, 5)
    DEPTH = 3

    def qk_group(gidx, t0, wqk, nwqk, bufs, qkT, perhead=None, qpad=None):
        qr, t1, t2, pr, qkb = bufs
        s = gidx % 2
        A = lambda *k: ("AR",) + k
        for half in range(2):
            for i in range(2):
                proj_tm(half, i * 256, t0 + 2 * half + i, wqk, 256, [A("wqk")])
            S.add("act", ACP(qr[s][:, 2 * half:2 * half + 2, :], ps[half][:, :].rearrange("p (i c) -> p i c", c=256)),
                  writes=[PS(half), A("qr", s, half)])
        fl = lambda a: a.rearrange("p i c -> p (i c)")
        v3 = lambda a: a.rearrange("p i (h e) -> p (i h) e", e=64)
        v4 = lambda a: a.rearrange("p i (h e) -> p i h e", e=64)
        S.add("act", ACT(fl(t1[s]), fl(qr[s]), AF.Square), reads=[A("qr", s)], writes=[A("t1", s)])
        ssq = stats[:, 512 + 16 * s: 528 + 16 * s]
        stok = ("stats", "qk", s)
        S.add("dve", lambda e: e.tensor_reduce(out=ssq, in_=v3(t1[s]), axis=AX.X, op=ALU.add), reads=[A("t1", s)], writes=[stok])
        S.add("act", ACT(ssq, ssq, AF.Ln, scale=1.0 / 64, bias=cv[:, 0:1]), reads=[stok, "cv"], writes=[stok])
        S.add("act", ACT(ssq, ssq, AF.Exp, scale=-0.5), reads=[stok], writes=[stok])
        S.add("dve", TT(v3(t1[s]), v3(qr[s]), ssq.unsqueeze(2).to_broadcast([128, 16, 64]), ALU.mult), reads=[stok, A("qr", s)], writes=[A("t1", s)])
        S.add("dve", TT(t2[s], t1[s], nwqk.rearrange("p h e -> p (h e)").unsqueeze(1).to_broadcast([128, 4, 256]), ALU.mult),
              reads=[A("t1", s), A("nwqk")], writes=[A("t1", s)])
        x1 = v4(t2[s])[:, :, :, 0:8]
        x2 = v4(t2[s])[:, :, :, 8:16]
        cs = cosT[:, t0:t0 + 4, :].unsqueeze(2).to_broadcast([128, 4, 4, 8])
        sn = sinT[:, t0:t0 + 4, :].unsqueeze(2).to_broadcast([128, 4, 4, 8])
        p4 = pr[s]
        pj = lambda j: p4[:, j].rearrange("p (i h) e -> p i h e", h=4)
        S.add("dve", TT(pj(0), x1, cs, ALU.mult), reads=[A("t1", s), "cst"], writes=[A("pr", s, 0)])
        S.add("dve", TT(pj(1), x2, sn, ALU.mult), reads=[A("t1", s), "cst"], writes=[A("pr", s, 1)])
        S.add("dve", TT(pj(2), x2, cs, ALU.mult), reads=[A("t1", s), "cst"], writes=[A("pr", s, 2)])
        S.add("dve", TT(pj(3), x1, sn, ALU.mult), reads=[A("t1", s), "cst"], writes=[A("pr", s, 3)])
        qb = v4(qkb[s])
        S.add("dve", TT(qb[:, :, :, 0:8], pj(0), pj(1), ALU.subtract), reads=[A("pr", s, 0), A("pr", s, 1)], writes=[A("qkb", s, 0)])
        S.add("dve", TT(qb[:, :, :, 8:16], pj(2), pj(3), ALU.add), reads=[A("pr", s, 2), A("pr", s, 3)], writes=[A("qkb", s, 1)])
        S.add("act", ACP(qb[:, :, :, 16:64], v4(t2[s])[:, :, :, 16:64]), reads=[A("t1", s)], writes=[A("qkb", s, 2)])
        if perhead is not None:
            for half in range(2):
                bk = 2 + half
                S.add("pe", TRS([(psb[bk][0:64, (jj * 4 + i) * 128:(jj * 4 + i + 1) * 128], qkb[s][:, i, (2 * half + jj) * 64:(2 * half + jj + 1) * 64], identb)
                                 for jj in range(2) for i in range(4)]), reads=[A("qkb", s), "cb"], writes=[PS(bk)])
                for jj in range(2):
                    dst, dtok = perhead[2 * half + jj]
                    eng, fn = ("act", ACP) if jj == 0 else ("dve", CP)
                    S.add(eng, fn(dst[0:64, t0 * 128:(t0 + 4) * 128], psb[bk][0:64, jj * 512:(jj + 1) * 512]),
                          writes=[PS(bk), dtok + ("qk", t0 // 4)])
            return
        bk = 2 + s
        S.add("pe", TRS([(psb[bk][:, (i * 2 + j) * 128:(i * 2 + j + 1) * 128], qkb[s][:, i, j * 128:(j + 1) * 128], identb)
                         for i in range(4) for j in range(2)]), reads=[A("qkb", s), "cb"], writes=[PS(bk)])
        if qpad is not None:
            qp0, qp1, kTt = qpad
            src = psb[bk][:, :].rearrange("p (i j c) -> p j i c", j=2, c=128)
            tv = lambda a: a[:, t0 * 128:(t0 + 4) * 128].rearrange("p (i c) -> p i c", c=128)
            S.add("act", ACP(tv(kTt), src[:, 1]), writes=[PS(bk), A("kTt", t0 // 4)])
            S.add("dve", CP(tv(qp0)[0:64], src[0:64, 0]), writes=[PS(bk), A("qp", 0, t0 // 4)])
            S.add("act", ACP(tv(qp1)[64:128], src[64:128, 0]), writes=[PS(bk), A("qp", 1, t0 // 4)])
            return
        S.add("act", ACP(qkT[:, :, t0 * 128:(t0 + 4) * 128].rearrange("p j (i c) -> p j i c", c=128),
                         psb[bk][:, :].rearrange("p (i j c) -> p j i c", j=2, c=128)),
              writes=[PS(bk)] + [A("qkT", t0 + i) for i in range(4)])

    def qk_groups_interleaved(calls):
        bounds = (0, 6, 8, 10, 12, 16, 19, None)
        recs = []
        for fn in calls:
            n0 = len(S.ops)
            fn()
            ops = S.ops[n0:]
            del S.ops[n0:]
            recs.append([ops[bounds[i]:bounds[i + 1]] for i in range(7)])
        n = len(recs)
        for k in range(n + 1):
            early = recs[k][0:4] if k < n else [[], [], [], []]
            late = recs[k - 1][4:7] if k >= 1 else [[], [], []]
            for stage in (early[0], late[0], early[1], late[1], early[2], late[2], early[3]):
                S.ops.extend(stage)

    def qk_bufs(alloc):
        qr = [arv(alloc(1024 * 4), [128, 4, 256], F32) for _ in range(2)]
        t1 = [arv(alloc(1024 * 4), [128, 4, 256], F32) for _ in range(2)]
        t2 = t1
        pr = [arv(alloc(512 * 4), [128, 4, 16, 8], F32) for _ in range(2)]
        qkb = [arv(alloc(1024 * 2), [128, 4, 256], BF16) for _ in range(2)]
        return (qr, t1, t2, pr, qkb)

    def silu_gate_T(wg, sgT, A):
        for i in range(8):
            bk = i % 2
            items = [(ps[bk][:, 0:512], wg[:, kc, :], hT[:, kc, i * 512:(i + 1) * 512], kc == 0, kc == 7) for kc in range(8)]
            S.add("pe", MM(items), reads=[("hT",), A("wg")], writes=[PS(bk)])
            S.add("act", ACT(sgT[:, i * 512:(i + 1) * 512], ps[bk][:, 0:512], AF.Silu), writes=[PS(bk), A("sgT", i)])

    def phase_dilated():
        S.claim(("AR",))
        o = 0

        def alloc(nbytes):
            nonlocal o
            r = o
            o += (nbytes + 63) // 64 * 64
            assert o <= ARN * 2, o
            return r
        A = lambda *k: ("AR",) + k
        wqk = arv(alloc(8 * 256 * 2), [128, 8, 256], BF16)
        wv = arv(alloc(8 * 128 * 2), [128, 8, 128], BF16)
        wg = arv(alloc(8 * 128 * 2), [128, 8, 128], BF16)
        qp = [arv(alloc(SEQ * 2), [128, SEQ], BF16) for _ in range(2)]
        kTt = arv(alloc(SEQ * 2), [128, SEQ], BF16)
        vblk = arv(alloc(NT * 256 * 2), [128, NT, 2, 128], BF16)
        sgT = arv(alloc(SEQ * 2), [128, SEQ], BF16)
        accN = arv(alloc(SEQ * 4), [128, SEQ], F32)
        accD = arv(alloc(SEQ * 4), [128, SEQ], F32)
        ost = arv(alloc(SEQ * 2), [128, SEQ], BF16)
        nwqk = arv(alloc(256 * 4), [128, 4, 64], F32)
        bufs = qk_bufs(alloc)
        Pt = [arv(alloc(512 * 2), [128, 512], BF16) for _ in range(4)]
        Rn2 = [bufs[1][j][:, 0:2, :].rearrange("p i c -> p (i c)") for j in range(2)]
        m01 = arv(alloc(512 * 2), [128, 512], BF16)
        for hh in range(2):
            S.add("dve", TS(m01[:, hh * 256:hh * 256 + 128], cst[:, 640:768], -1.0, 0.0, ALU.is_ge, ALU.add), reads=["cst"], writes=[A("m01")])
            S.add("dve", TS(m01[:, hh * 256 + 128:hh * 256 + 256], cst[:, 512:640], -1.0, 0.0, ALU.is_ge, ALU.add), reads=["cst"], writes=[A("m01")])
        S.add("pool", lambda e: e.memset(vblk[:, :, 0, 64:128], 1.0), writes=[A("vblk", "ones0")])
        S.add("pool", lambda e: e.memset(vblk[:, :, 1, 0:64], 1.0), writes=[A("vblk", "ones1")])
        S.add("pool", lambda e: e.memset(qp[0][64:128, :], 0.0), writes=[A("qp", 0)])
        S.add("pool", lambda e: e.memset(qp[1][0:64, :], 0.0), writes=[A("qp", 1)])
        qk_shift(e_qn, e_kn, 1, nwqk, bufs[1][0][:, 0, :], A("t1", 0))
        negC = cv[:, 1:2]
        sidx = 0
        gidx = 0
        def dil_loads(jp, gi):
            hcol = (8 * gi + 2 * jp) * 64
            load_w(wqk[:, :, 0:128], e_winv, 3088 + hcol, 128, A("wqk"), "wq")
            load_w(wqk[:, :, 128:256], e_winv, 4624 + hcol, 128, A("wqk"), "wk")
            load_w(wv, e_winv, 6160 + hcol, 128, A("wv"), "wv")

        load_w(wg, e_winv, 7696, 128, A("wg"), "wg")
        dil_loads(0, 0)
        for jp in range(4):
            silu_gate_T(wg, sgT, A)
            if jp + 1 < 4:
                load_w(wg, e_winv, 7696 + (jp + 1) * 128, 128, A("wg"), "wg")
            for gi, r in enumerate((1, 4, 16)):
                calls = []
                for t4 in range(NT // 4):
                    calls.append((lambda gidx=gidx, t4=t4: qk_group(gidx, t4 * 4, wqk, nwqk, bufs, None, qpad=(qp[0], qp[1], kTt))))
                    gidx += 1
                qk_groups_interleaved(calls)
                nb = NT // r
                for b4 in range(NT // 4):
                    bk = b4 % 2
                    for i in range(4):
                        blk = b4 * 4 + i
                        c, n = blk // nb, blk % nb
                        items = []
                        for kc in range(8):
                            hv = hT[:, kc, :].rearrange("p (u r) -> p u r", r=r)[:, 128 * n:128 * n + 128, c]
                            items.append((ps[bk][:, i * 128:(i + 1) * 128], hv, wv[:, kc, :], kc == 0, kc == 7))
                        S.add("pe", MM(items), reads=[("hT",), A("wv")], writes=[PS(bk)])
                    pv3 = ps[bk][:, :].rearrange("p (i c) -> p i c", c=128)
                    S.add("act", ACP(vblk[:, b4 * 4:b4 * 4 + 4, 0, 0:64], pv3[:, :, 0:64]), writes=[PS(bk), A("vblk", "v0", b4)])
                    S.add("dve", CP(vblk[:, b4 * 4:b4 * 4 + 4, 1, 64:128], pv3[:, :, 64:128]), writes=[PS(bk), A("vblk", "v1", b4)])
                qvh = [q_.rearrange("p (u r) -> p u r", r=r) for q_ in qp]
                kv = kTt.rearrange("p (u r) -> p u r", r=r)
                steps = []
                for b4 in range(NT // 4):
                    nbk, dbk = ((6, 7), (2, 3))[b4 % 2]
                    for i in range(4):
                        blk = b4 * 4 + i
                        c, n = blk // nb, blk % nb
                        sb_ = SBANKS[sidx % 4]
                        pt = Pt[sidx % 4]
                        ptk = A("Pt", sidx % 4)
                        sidx += 1
                        front, back = [], []
                        items = []
                        for hh in range(2):
                            hs = slice(64 * hh, 64 * hh + 64)
                            qa = qvh[hh][:, 128 * n:128 * n + 128, c]
                            ko = kv[:, 128 * n:128 * n + 128, c]
                            b0 = hh * 256
                            if n > 0:
                                kp = kv[:, 128 * (n - 1):128 * n, c]
                                items += [(ps[sb_][:, b0:b0 + 128], kp, qa, True, True),
                                          (ps[sb_][:, b0 + 128:b0 + 256], ko, qa, True, True)]
                            else:
                                items += [(ps[sb_][:, b0 + 128:b0 + 256], ko, qa, True, True)]
                        front.append(("pe", MM(items), [A("qp"), A("kTt")], [PS(sb_)]))
                        if n > 0:
                            front.append(("act", ACT(pt[:, :], ps[sb_][:, :], AF.Exp, bias=negC), ["cv"], [PS(sb_), ptk]))
                            front.append(("dve", TT(pt[:, :], pt[:, :], m01[:, :], ALU.mult), [A("m01")], [ptk]))
                        else:
                            for hh in range(2):
                                oc_ = slice(hh * 256 + 128, hh * 256 + 256)
                                front.append(("act", ACT(pt[:, oc_], ps[sb_][:, oc_], AF.Exp, bias=negC), ["cv"], [PS(sb_), ptk]))
                                front.append(("dve", TT(pt[:, oc_], pt[:, oc_], m01[:, oc_], ALU.mult), [A("m01")], [ptk]))
                        items = []
                        for hh in range(2):
                            hs = slice(64 * hh, 64 * hh + 64)
                            oc = slice(i * 128, (i + 1) * 128)
                            b0 = hh * 256
                            dst = ps[(nbk, dbk)[hh]]
                            if n > 0:
                                items += [(dst[:, oc], vblk[:, blk - 1, hh, :], pt[:, b0:b0 + 128], True, False),
                                          (dst[:, oc], vblk[:, blk, hh, :], pt[:, b0 + 128:b0 + 256], False, True)]
                            else:
                                items += [(dst[:, oc], vblk[:, blk, hh, :], pt[:, b0 + 128:b0 + 256], True, True)]
                        back.append(("pe", MM(items), [ptk, A("vblk")], [PS(nbk), PS(dbk)]))
                        if i == 3:
                            blk0 = b4 * 4
                            if r == 16:
                                c0 = blk0 // nb
                                dn = accN.rearrange("p (u r) -> p r u", r=16)[:, c0:c0 + 2, :]
                                dd = accD.rearrange("p (u r) -> p r u", r=16)[:, c0:c0 + 2, :]
                                sn_ = ps[nbk][:, :].rearrange("p (a b) -> p a b", b=256)
                                sd_ = ps[dbk][:, :].rearrange("p (a b) -> p a b", b=256)
                            else:
                                c0, n0 = blk0 // nb, blk0 % nb
                                dn = accN.rearrange("p (u r) -> p u r", r=r)[:, 128 * n0:128 * n0 + 512, c0]
                                dd = accD.rearrange("p (u r) -> p u r", r=r)[:, 128 * n0:128 * n0 + 512, c0]
                                sn_ = ps[nbk][:, :]
                                sd_ = ps[dbk][:, :]
                            if gi == 0:
                                back.append(("act", ACP(dn, sn_), [], [PS(nbk), A("accN")]))
                                back.append(("dve", CP(dd, sd_), [], [PS(dbk), A("accD")]))
                            else:
                                back.append(("dve", TT(dn, sn_, dn, ALU.add), [], [PS(nbk), A("accN")]))
                                back.append(("dve", TT(dd, sd_, dd, ALU.add), [], [PS(dbk), A("accD")]))
                        steps.append(dict(front=front, back=back))
                nxt = jp * 3 + gi + 1
                if nxt < 12:
                    dil_loads(nxt // 3, nxt % 3)
                emit_pipeline(steps, DEPTH)
            for hh, acc, atok in ((0, accN, A("accN")), (1, accD, A("accD"))):
                nr = slice(64 * hh, 64 * hh + 64)
                dr = slice(64 * (1 - hh), 64 * (1 - hh) + 64)
                for hf in range(2):
                    hc_ = slice(hf * 2048, (hf + 1) * 2048)
                    S.add("act", ACT(acc[dr, hc_], acc[dr, hc_], AF.Ln), writes=[atok + (hf,)])
                    S.add("act", ACT(acc[dr, hc_], acc[dr, hc_], AF.Exp, scale=-1.0), writes=[atok + (hf,)])
                for i8 in range(8):
                    cs_ = slice(i8 * 512, (i8 + 1) * 512)
                    bk = i8 % 2
                    rn = Rn2[i8 % 2]
                    rtok = A("t1", i8 % 2)
                    S.add("pe", MM([(ps[bk][nr, 0:512], identf[dr, dr], acc[dr, cs_], True, True)]), reads=[atok + (i8 // 4,), "cst"], writes=[PS(bk)])
                    S.add("dve", TT(rn[nr, :], ps[bk][nr, 0:512], acc[nr, cs_], ALU.mult), reads=[atok], writes=[PS(bk), rtok])
                    S.add("pool", TT(ost[nr, cs_], rn[nr, :], sgT[nr, cs_], ALU.mult), reads=[rtok, A("sgT")], writes=[A("ost", hh, i8)])
            S.add("sp", DMA(mixT0[8 + jp, :, :], ost[:, :]), reads=[A("ost")], writes=[("mixT0", 8 + jp)], dma="ost")

    if on('dil'):
        phase_dilated()

    def phase_outproj(mixT_d, nch, wout_d, res_d, dst_d, next_normw_d, scol, dst_tok, mix_tok, res_tok):
        S.claim(("AR",))
        A = lambda *k: ("AR",) + k
        o = 0

        def alloc(nbytes):
            nonlocal o
            r = o
            o += (nbytes + 63) // 64 * 64
            assert o <= ARN * 2, o
            return r
        wo = arv(alloc(nch * DM * 2), [128, nch, DM], BF16)
        NBUF = 3
        mt = [arv(alloc(nch * 128 * 2), [128, nch, 128], BF16) for _ in range(NBUF)]
        xr = [arv(alloc(DM * 4), [128, DM], F32) for _ in range(NBUF)]
        x1t = [arv(alloc(DM * 4), [128, DM], F32) for _ in range(NBUF)]
        hb = [arv(alloc(DM * 2), [128, DM], BF16) for _ in range(NBUF)]
        nw = arv(alloc(DM * 4), [128, DM], F32)
        junk = arv(alloc(DM * 2), [128, DM], BF16)
        wov = wout_d.rearrange("(c p) f -> p c f", p=128)
        for c in range(0, nch, 4):
            S.add("pool", DMA(wo[:, c:c + 4, :], wov[:, c:c + 4, :]), writes=[A("wo", c)], dma=("wo", c))
        if next_normw_d is not None:
            S.add("sp", DMA(nw, next_normw_d[0:1, :].partition_broadcast(128)), writes=[A("nw")], dma="nw")
        mv = mixT_d.rearrange("c p t -> p c t")
        for t in range(NT):
            s = t % NBUF
            S.add("sp", DMA(mt[s], mv[:, :, t * 128:(t + 1) * 128]), reads=mix_tok, writes=[A("mt", s)], dma=("mt", s))
            S.add("sp", DMA(xr[s], res_d[t * 128:(t + 1) * 128, :]), reads=res_tok, writes=[A("xr", s)], dma=("xr", s))
            for hf in range(2):
                items = [(ps[hf][:, 0:512], mt[s][:, c, :], wo[:, c, hf * 512:(hf + 1) * 512], c == 0, c == nch - 1) for c in range(nch)]
                S.add("pe", MM(items), reads=[A("mt", s), A("wo")], writes=[PS(hf)])
                S.add("dve", TT(x1t[s][:, hf * 512:(hf + 1) * 512], ps[hf][:, 0:512], xr[s][:, hf * 512:(hf + 1) * 512], ALU.add),
                      reads=[A("xr", s)], writes=[PS(hf), A("x1t", s, hf)])
            S.add("pool", DMA(dst_d[t * 128:(t + 1) * 128, :], x1t[s]), reads=[A("x1t", s)], writes=[(dst_tok, t)], dma=("x1o", s))
            if next_normw_d is not None:
                if t >= 1:
                    p = t - 1
                    norm_transpose_tile(None, None, p, None, None, hb[p % NBUF], A("hb", p % NBUF), None, scol, 2 + p % 2, part="B")
                norm_transpose_tile(x1t[s], A("x1t", s), t, nw, A("nw"), hb[s], A("hb", s), junk, scol, 2 + t % 2, part="A")
        if next_normw_d is not None:
            p = NT - 1
            norm_transpose_tile(None, None, p, None, None, hb[p % NBUF], A("hb", p % NBUF), None, scol, 2 + p % 2, part="B")

    if on('out0'):
        phase_outproj(mixT0, 12, e_wout, x_d, x1_d, o_normw, 384, "x1s", [("mixT0",)], [])

    o_winv = o_win.rearrange("(kc p) f -> p kc f", p=128)

    def phase_moba():
        S.claim(("AR",))
        o = 0

        def alloc(nbytes):
            nonlocal o
            r = o
            o += (nbytes + 63) // 64 * 64
            assert o <= ARN * 2, o
            return r
        A = lambda *k: ("AR",) + k
        wqk = arv(alloc(8 * 256 * 2), [128, 8, 256], BF16)
        wv = arv(alloc(8 * 128 * 2), [128, 8, 128], BF16)
        wg = arv(alloc(8 * 128 * 2), [128, 8, 128], BF16)
        qaug = [arv(alloc(SEQ * 2), [128, SEQ], BF16) for _ in range(2)]
        kaug = [arv(alloc(SEQ * 2), [128, SEQ], BF16) for _ in range(2)]
        vaug = arv(alloc(NT * 256 * 2), [128, NT, 2, 128], BF16)
        sgT = arv(alloc(SEQ * 2), [128, SEQ], BF16)
        ost = arv(alloc(SEQ * 2), [128, SEQ], BF16)
        nwqk = arv(alloc(256 * 4), [128, 4, 64], F32)
        bufs = qk_bufs(alloc)
        Pt = [arv(alloc(512 * 2), [128, 512], BF16) for _ in range(3)]
        kbf2 = [arv(alloc(16 * 4), [128, 16], F32) for _ in range(2)]
        kbb2 = [arv(alloc(16 * 2), [128, 16], BF16) for _ in range(2)]
        gm2 = [arv(alloc(512 * 4), [128, NT, 16], F32) for _ in range(2)]
        mx2 = [arv(alloc(NT * 8 * 4), [128, NT, 8], F32) for _ in range(2)]
        thr2 = [arv(alloc(NT * 4), [128, NT], F32) for _ in range(2)]
        selb2 = [arv(alloc(512 * 2), [128, NT, 16], BF16) for _ in range(2)]
        Rt = [arv(alloc(256 * 4), [128, 256], F32) for _ in range(4)]
        rs = [arv(alloc(256 * 4), [128, 256], F32) for _ in range(4)]
        oo = [arv(alloc(256 * 4), [128, 256], F32) for _ in range(4)]
        qk_shift(o_qn, o_kn, 2, nwqk, bufs[1][0][:, 0, :], A("t1", 0))
        negC = cv[:, 2:3]
        QA = [A("qaug", h) for h in range(2)]
        KA = [A("kaug", h) for h in range(2)]
        for h in range(2):
            S.add("pool", (lambda h=h: (lambda e: e.memset(qaug[h][64:128, :], 0.0)))(), writes=[QA[h]])
            S.add("pool", (lambda h=h: (lambda e: e.memset(kaug[h][64:128, :], 0.0)))(), writes=[KA[h]])
            S.add("pool", DMA(kaug[h][64:80, :], koh_d[:, :]), writes=[KA[h] + ("oh",)], dma=("koh", h))
        S.add("pool", lambda e: e.memset(vaug[:, :, 0, 64:128], 1.0), writes=[A("vaug", "ones0")])
        S.add("pool", lambda e: e.memset(vaug[:, :, 1, 0:64], 1.0), writes=[A("vaug", "ones1")])
        MBANKS = (0, 1, 4)
        sidx = 0
        gidx = 0
        def moba_loads(hp):
            load_w(wg, o_winv, 3072 + hp * 128, 128, A("wg"), "wg")
            load_w(wqk[:, :, 0:128], o_winv, hp * 128, 128, A("wqk"), "wq")
            load_w(wqk[:, :, 128:256], o_winv, 1024 + hp * 128, 128, A("wqk"), "wk")
            load_w(wv, o_winv, 2048 + hp * 128, 128, A("wv"), "wv")

        moba_loads(0)
        for hp in range(8):
            silu_gate_T(wg, sgT, A)
            calls = []
            for t4 in range(NT // 4):
                calls.append((lambda gidx=gidx, t4=t4: qk_group(gidx, t4 * 4, wqk, nwqk, bufs, None,
                              perhead=[(qaug[0], QA[0]), (qaug[1], QA[1]), (kaug[0], KA[0]), (kaug[1], KA[1])])))
                gidx += 1
            qk_groups_interleaved(calls)
            for hh in range(2):
                S.add("dve", (lambda hh=hh: (lambda e: e.tensor_reduce(out=kbf2[hh][0:64, :], in_=kaug[hh][0:64, :].rearrange("p (n k) -> p n k", k=256),
                                                                    axis=AX.X, op=ALU.add)))(), reads=[KA[hh]], writes=[A("kbf", hh)])
                S.add("dve", TS(kbb2[hh][0:64, :], kbf2[hh][0:64, :], 1.0 / 256, None, ALU.mult), reads=[A("kbf", hh)], writes=[A("kbb", hh)])
            for hh in range(2):
                gb = 4 + hh
                items = [(ps[gb][:, t * 16:(t + 1) * 16], qaug[hh][0:64, t * 128:(t + 1) * 128], kbb2[hh][0:64, :], True, True) for t in range(NT)]
                S.add("pe", MM(items), reads=[QA[hh], A("kbb", hh)], writes=[PS(gb)])
            for b4 in range(NT // 4):
                bk = b4 % 2
                for i in range(4):
                    t = b4 * 4 + i
                    proj_tm(bk, i * 128, t, wv, 128, [A("wv")])
                pv3 = ps[bk][:, :].rearrange("p (i c) -> p i c", c=128)
                S.add("act", ACP(vaug[:, b4 * 4:b4 * 4 + 4, 0, 0:64], pv3[:, :, 0:64]), writes=[PS(bk), A("vaug", "v0", b4)])
                S.add("act", ACP(vaug[:, b4 * 4:b4 * 4 + 4, 1, 64:128], pv3[:, :, 64:128]), writes=[PS(bk), A("vaug", "v1", b4)])
            for hh in range(2):
                gb = 4 + hh
                gm_, mx_, thr_, selb_ = gm2[hh], mx2[hh], thr2[hh], selb2[hh]
                G = lambda *k: A("g", hh) + k
                S.add("dve", TT(gm_.rearrange("p t n -> p (t n)"), ps[gb][:, :], validb, ALU.add), reads=["cst"], writes=[PS(gb), G("gm")])
                for t in range(NT):
                    S.add("dve", (lambda t=t, mx_=mx_, gm_=gm_: (lambda e: e.max(out=mx_[:, t, :], in_=gm_[:, t, :])))(), reads=[G("gm")], writes=[G("mx", t)])
                S.add("dve", TS(thr_[:, :], mx_[:, :, 2], -1e29, None, ALU.max), reads=[G("mx")], writes=[G("thr")])
                S.add("dve", TT(gm_, gm_, thr_.unsqueeze(2).to_broadcast([128, NT, 16]), ALU.is_ge), reads=[G("thr"), G("gm")], writes=[G("gm")])
                S.add("dve", TT(selb_.rearrange("p t n -> p (t n)"), gm_.rearrange("p t n -> p (t n)"), ownm1, ALU.add),
                      reads=[G("gm"), "cst"], writes=[G("selb")])
            for hh in range(2):
                selb_ = selb2[hh]
                for t8 in range(4):
                    mb = 3 - t8 % 2
                    S.add("pe", TRS([(psb[mb][64:80, i * 128:(i + 1) * 128], selb_[:, t8 * 8 + i, :], identb) for i in range(8)]),
                          reads=[A("g", hh, "selb"), "cb"], writes=[PS(mb)])
                    S.add("act", ACP(qaug[hh][64:80, t8 * 1024:(t8 + 1) * 1024], psb[mb][64:80, :]), writes=[PS(mb), QA[hh] + ("m", t8)])
            steps = []
            for tb in range(16):
                q0 = tb * 256
                for hh in range(2):
                    xb = ((6, 7), (2, 3))[tb % 2][hh]
                    qa = qaug[hh][:, q0:q0 + 256]
                    for n in range(tb + 1):
                        sb_ = MBANKS[sidx % 3]
                        pt = Pt[sidx % 3]
                        ptk = A("Pt", sidx % 3)
                        sidx += 1
                        front, back, late1, late2 = [], [], [], []
                        k0 = kaug[hh][:, (2 * n) * 128:(2 * n + 1) * 128]
                        k1 = kaug[hh][:, (2 * n + 1) * 128:(2 * n + 2) * 128]
                        if n < tb:
                            items = [(ps[sb_][:, 0:256], k0, qa, True, True),
                                     (ps[sb_][:, 256:512], k1, qa, True, True)]
                            front.append(("pe", MM(items), [QA[hh], KA[hh]], [PS(sb_)]))
                            front.append(("act", ACT(pt[:, :], ps[sb_][:, :], AF.Exp, bias=negC), ["cv"], [PS(sb_), ptk]))
                            pv = [(0, 256, pt[:, 0:256], 2 * n), (0, 256, pt[:, 256:512], 2 * n + 1)]
                        else:
                            items = [(ps[sb_][:, 0:256], k0, qa, True, False),
                                     (ps[sb_][:, 0:128], identb, maskLE, False, True),
                                     (ps[sb_][:, 384:512], k1, qaug[hh][:, q0 + 128:q0 + 256], True, False),
                                     (ps[sb_][:, 384:512], identb, maskLE, False, True)]
                            front.append(("pe", MM(items), [QA[hh], KA[hh], "cb"], [PS(sb_)]))
                            front.append(("act", ACT(pt[:, 0:256], ps[sb_][:, 0:256], AF.Exp, bias=negC), ["cv"], [PS(sb_), ptk]))
                            front.append(("act", ACT(pt[:, 384:512], ps[sb_][:, 384:512], AF.Exp, bias=negC), ["cv"], [PS(sb_), ptk]))
                            pv = [(0, 256, pt[:, 0:256], 2 * n), (128, 256, pt[:, 384:512], 2 * n + 1)]
                        items = []
                        for pi, (c0_, c1_, rhs, kt) in enumerate(pv):
                            first = (n == 0 and pi == 0)
                            last = (n == tb and pi == 1)
                            items.append((ps[xb][:, c0_:c1_], vaug[:, kt, hh, :], rhs, first, last))
                        back.append(("pe", MM(items), [ptk, A("vaug")], [PS(xb)]))
                        if n == tb:
                            s_ = (tb * 2 + hh) % 4
                            nr = slice(64 * hh, 64 * hh + 64)
                            dr = slice(64 * (1 - hh), 64 * (1 - hh) + 64)
                            back.append(("dve", (lambda s_=s_, xb=xb, dr=dr: (lambda e: e.reciprocal(out=Rt[s_][dr, :], in_=ps[xb][dr, 0:256])))(),
                                         [], [PS(xb), A("Rt", s_)]))
                            back.append(("act", ACP(rs[s_][nr, :], ps[xb][nr, 0:256]), [], [PS(xb), A("rs", s_)]))
                            late1.append(("pe", MM([(ps[5][nr, 0:256], identf[dr, dr], Rt[s_][dr, :], True, True)]), [A("Rt", s_), "cst"], [("ps", 5, hh)]))
                            late2.append(("dve", TT(oo[s_][nr, :], ps[5][nr, 0:256], rs[s_][nr, :], ALU.mult), [A("rs", s_)], [("ps", 5, hh), A("oo", s_)]))
                            late2.append(("pool", TT(ost[nr, q0:q0 + 256], oo[s_][nr, :], sgT[nr, q0:q0 + 256], ALU.mult),
                                          [A("oo", s_), A("sgT")], [A("ost", tb, hh)]))
                        steps.append(dict(front=front, back=back, late1=late1, late2=late2))
            if hp + 1 < 8:
                moba_loads(hp + 1)
            emit_pipeline(steps, 2)
            S.add("sp", DMA(mixT1[hp, :, :], ost[:, :]), reads=[A("ost")], writes=[("mixT1", hp)], dma="ost")

    if stop_after != "l0":
        if on('moba'):
            phase_moba()
        if on('out1'):
            phase_outproj(mixT1, 8, o_wout, x1_d, out_d, None, 0, "out", [("mixT1",)], [("x1s",)])
        S.add("sp", None, reads=[("out",)])
    else:
        S.add("sp", None, reads=[("x1s",)])
    S.emit(st)
    K.n_ops = len(S.ops)
    K.n_sems = S.n_sems
    K.stack = st
    return nc, K


_CACHE = {}


def make_in_map(inputs, b, cst):
    g = lambda k: np.ascontiguousarray(np.asarray(inputs[k], dtype=np.float32)[0])
    m = {
        "x": np.ascontiguousarray(np.asarray(inputs["x"], dtype=np.float32)[b]),
        "even_norm_w": g("even_norm_w").reshape(1, DM),
        "even_w_in": g("even_w_in"),
        "even_conv_w": g("even_conv_w"),
        "even_conv_b": g("even_conv_b").reshape(2048, 1),
        "even_dt_bias": g("even_dt_bias").reshape(1, 16),
        "even_a_log": g("even_a_log").reshape(1, 16),
        "even_d_skip": g("even_d_skip").reshape(1, 16),
        "even_ssd_norm_w": g("even_ssd_norm_w").reshape(1, DM),
        "even_q_norm": g("even_q_norm").reshape(1, 64),
        "even_k_norm": g("even_k_norm").reshape(1, 64),
        "even_w_out": g("even_w_out"),
        "odd_norm_w": g("odd_norm_w").reshape(1, DM),
        "odd_w_in": g("odd_w_in"),
        "odd_q_norm": g("odd_q_norm").reshape(1, 64),
        "odd_k_norm": g("odd_k_norm").reshape(1, 64),
        "odd_w_out": g("odd_w_out"),
        "cst": cst,
        "koh": make_koh(),
    }
    return m


def kernel(**inputs):
    if "nc" not in _CACHE:
        _CACHE["nc"] = build_program()[0]
    nc = _CACHE["nc"]
    cst = make_consts()
    in_maps = [make_in_map(inputs, c % 4, cst) for c in range(8)]
    res = run_bass_kernel_spmd(nc, in_maps, core_ids=list(range(8)))
    out = np.stack([np.asarray(res.results[c]["out"], dtype=np.float32) for c in range(4)], axis=0)
    return out
```
